# Optimizing a Trainium2 kernel written in Bass

```python
import math
import jax, jax.numpy as jnp
from jax import lax
import numpy as np

D_MODEL = 4096
BATCH = 2
SEQ = 8192
DEPTH = 1

ATTN_HEAD_DIM = 128
ATTN_WIDTH = D_MODEL // 2
ATTN_HEADS = ATTN_WIDTH // ATTN_HEAD_DIM
Q_BLOCK = 128
SSM_GROUP = 16
SSM_WIDTH = D_MODEL // 4
SSM_GROUPS = SSM_WIDTH // SSM_GROUP
SSM_STATE = 64
DT_MIN = 1e-3
DT_MAX = 1e-1
N_BRANCH = 2
OFF_Q = 0
OFF_K = ATTN_WIDTH
OFF_V = 2 * ATTN_WIDTH
OFF_F = 3 * ATTN_WIDTH
OFF_U = OFF_F + ATTN_HEADS
OFF_G = OFF_U + SSM_WIDTH
IN_WIDTH = OFF_G + N_BRANCH * D_MODEL
MEM_LEN = 256
XATTN_HEADS = 4
XATTN_HEAD_DIM = 256
XATTN_WIDTH = XATTN_HEADS * XATTN_HEAD_DIM
D_FF = 4 * D_MODEL
EPS = 1e-6
NEG_INF = -1e30

kernel_name = "hybrid_fox_s5_gated_block"


def rmsnorm(x, g):
    xf = x.astype(jnp.float32)
    r = lax.rsqrt(jnp.mean(xf * xf, axis=-1, keepdims=True) + EPS)
    return (xf * r).astype(x.dtype) * g


def forgetting_attention(q, k, v, log_f):
    bsz, seq, heads, dh = q.shape
    c = jnp.cumsum(log_f, axis=1).transpose(0, 2, 1)
    qh = q.transpose(0, 2, 1, 3)
    kh = k.transpose(0, 2, 1, 3)
    vh = v.transpose(0, 2, 1, 3)
    k_pos = jnp.arange(seq)
    scale = dh ** -0.5

    def block(i):
        start = i * Q_BLOCK
        qb = lax.dynamic_slice_in_dim(qh, start, Q_BLOCK, axis=2)
        cb = lax.dynamic_slice_in_dim(c, start, Q_BLOCK, axis=2)
        s = jnp.einsum('bhqd,bhkd->bhqk', qb, kh).astype(jnp.float32) * scale
        s = s + cb[..., :, None] - c[..., None, :]
        q_pos = start + jnp.arange(Q_BLOCK)
        s = jnp.where(k_pos[None, :] <= q_pos[:, None], s, NEG_INF)
        p = jax.nn.softmax(s, axis=-1).astype(vh.dtype)
        return jnp.einsum('bhqk,bhkd->bhqd', p, vh)

    out = lax.map(block, jnp.arange(seq // Q_BLOCK))
    return out.transpose(1, 0, 3, 2, 4).reshape(bsz, seq, heads * dh)


def _complex_linear_combine(e1, e2):
    a1r, a1i, b1r, b1i = e1
    a2r, a2i, b2r, b2i = e2
    ar = a1r * a2r - a1i * a2i
    ai = a1r * a2i + a1i * a2r
    br = a2r * b1r - a2i * b1i + b2r
    bi = a2r * b1i + a2i * b1r + b2i
    return (ar, ai, br, bi)


def s5_grouped(u, A_re, A_im, log_dt, B_re, B_im, C_re, C_im, D_skip):
    bsz, seq, _ = u.shape
    ug = u.reshape(bsz, seq, SSM_GROUPS, SSM_GROUP).astype(jnp.float32)
    dt = jnp.exp(log_dt.astype(jnp.float32))[:, None]
    a_re = A_re.astype(jnp.float32)
    a_im = A_im.astype(jnp.float32)
    mag = jnp.exp(dt * a_re)
    ang = dt * a_im
    lb_re = mag * jnp.cos(ang)
    lb_im = mag * jnp.sin(ang)
    den = a_re * a_re + a_im * a_im
    nr = lb_re - 1.0
    ni = lb_im
    f_re = (nr * a_re + ni * a_im) / den
    f_im = (ni * a_re - nr * a_im) / den
    br = B_re.astype(jnp.float32)
    bi = B_im.astype(jnp.float32)
    bb_re = f_re[..., None] * br - f_im[..., None] * bi
    bb_im = f_re[..., None] * bi + f_im[..., None] * br
    bu_re = jnp.einsum('bsgh,gph->bsgp', ug, bb_re)
    bu_im = jnp.einsum('bsgh,gph->bsgp', ug, bb_im)
    al_re = jnp.broadcast_to(lb_re, bu_re.shape)
    al_im = jnp.broadcast_to(lb_im, bu_im.shape)
    _, _, xs_re, xs_im = lax.associative_scan(
        _complex_linear_combine, (al_re, al_im, bu_re, bu_im), axis=1)
    y = (jnp.einsum('bsgp,ghp->bsgh', xs_re, C_re.astype(jnp.float32))
         - jnp.einsum('bsgp,ghp->bsgh', xs_im, C_im.astype(jnp.float32))
         + D_skip.astype(jnp.float32) * ug)
    return y.reshape(bsz, seq, SSM_WIDTH).astype(u.dtype)


def memory_cross_attention(h, m, wq, wk, wv, wo):
    bsz, seq, _ = h.shape
    q = (h @ wq).reshape(bsz, seq, XATTN_HEADS, XATTN_HEAD_DIM)
    k = (m @ wk).reshape(bsz, m.shape[1], XATTN_HEADS, XATTN_HEAD_DIM)
    v = (m @ wv).reshape(bsz, m.shape[1], XATTN_HEADS, XATTN_HEAD_DIM)
    s = jnp.einsum('bqhd,bkhd->bhqk', q, k).astype(jnp.float32) * (XATTN_HEAD_DIM ** -0.5)
    p = jax.nn.softmax(s, axis=-1).astype(v.dtype)
    o = jnp.einsum('bhqk,bkhd->bqhd', p, v).reshape(bsz, seq, XATTN_WIDTH)
    return o @ wo


def setup_inputs(seed: int = 0) -> dict:
    key = jax.random.key(seed)
    ks = jax.random.split(key, 32)

    def nrm(k, shape, scale):
        return jax.random.normal(k, shape, jnp.float32) * scale

    def gain(k):
        return 1.0 + 0.02 * jax.random.normal(k, (DEPTH, D_MODEL), jnp.float32)

    L = DEPTH
    n_idx = jnp.arange(SSM_STATE, dtype=jnp.float32)
    return {
        "x": nrm(ks[0], (BATCH, SEQ, D_MODEL), 1.0),
        "mem": nrm(ks[1], (BATCH, MEM_LEN, D_MODEL), 1.0),
        "g_mix": gain(ks[2]),
        "w_in": nrm(ks[3], (L, D_MODEL, IN_WIDTH), D_MODEL ** -0.5),
        "b_f": jax.random.uniform(ks[4], (L, ATTN_HEADS), jnp.float32, 1.0, 6.0),
        "b_gate": nrm(ks[5], (L, N_BRANCH * D_MODEL), 0.02),
        "A_re": -0.5 + nrm(ks[6], (L, SSM_GROUPS, SSM_STATE), 0.01),
        "A_im": math.pi * n_idx + nrm(ks[7], (L, SSM_GROUPS, SSM_STATE), 0.01),
        "log_dt": jax.random.uniform(ks[8], (L, SSM_GROUPS), jnp.float32,
                                     math.log(DT_MIN), math.log(DT_MAX)),
        "B_re": nrm(ks[9], (L, SSM_GROUPS, SSM_STATE, SSM_GROUP), (2 * SSM_GROUP) ** -0.5),
        "B_im": nrm(ks[10], (L, SSM_GROUPS, SSM_STATE, SSM_GROUP), (2 * SSM_GROUP) ** -0.5),
        "C_re": nrm(ks[11], (L, SSM_GROUPS, SSM_GROUP, SSM_STATE), (2 * SSM_STATE) ** -0.5),
        "C_im": nrm(ks[12], (L, SSM_GROUPS, SSM_GROUP, SSM_STATE), (2 * SSM_STATE) ** -0.5),
        "D_skip": nrm(ks[13], (L, SSM_GROUPS, SSM_GROUP), 1.0),
        "w_glu": nrm(ks[14], (L, SSM_WIDTH, SSM_WIDTH), SSM_WIDTH ** -0.5),
        "b_glu": nrm(ks[15], (L, SSM_WIDTH), 0.02),
        "w_attn_up": nrm(ks[16], (L, ATTN_WIDTH, D_MODEL), ATTN_WIDTH ** -0.5),
        "w_ssm_up": nrm(ks[17], (L, SSM_WIDTH, D_MODEL), SSM_WIDTH ** -0.5),
        "w_out": nrm(ks[18], (L, D_MODEL, D_MODEL), D_MODEL ** -0.5),
        "g_xattn": gain(ks[19]),
        "g_mem": gain(ks[20]),
        "wq_x": nrm(ks[21], (L, D_MODEL, XATTN_WIDTH), D_MODEL ** -0.5),
        "wk_x": nrm(ks[22], (L, D_MODEL, XATTN_WIDTH), D_MODEL ** -0.5),
        "wv_x": nrm(ks[23], (L, D_MODEL, XATTN_WIDTH), D_MODEL ** -0.5),
        "wo_x": nrm(ks[24], (L, XATTN_WIDTH, D_MODEL), XATTN_WIDTH ** -0.5),
        "g_mlp": gain(ks[25]),
        "w_ff1": nrm(ks[26], (L, D_MODEL, D_FF), D_MODEL ** -0.5),
        "w_ff2": nrm(ks[27], (L, D_FF, D_MODEL), D_FF ** -0.5),
        "g_final": 1.0 + 0.02 * jax.random.normal(ks[28], (D_MODEL,), jnp.float32),
    }


def reference(x, mem, g_mix, w_in, b_f, b_gate, A_re, A_im, log_dt, B_re, B_im, C_re, C_im,
              D_skip, w_glu, b_glu, w_attn_up, w_ssm_up, w_out, g_xattn, g_mem, wq_x, wk_x,
              wv_x, wo_x, g_mlp, w_ff1, w_ff2, g_final):
    bsz, seq, _ = x.shape
    for l in range(DEPTH):
        h = rmsnorm(x, g_mix[l])
        proj = h @ w_in[l]
        q = proj[..., OFF_Q:OFF_K].reshape(bsz, seq, ATTN_HEADS, ATTN_HEAD_DIM)
        k = proj[..., OFF_K:OFF_V].reshape(bsz, seq, ATTN_HEADS, ATTN_HEAD_DIM)
        v = proj[..., OFF_V:OFF_F].reshape(bsz, seq, ATTN_HEADS, ATTN_HEAD_DIM)
        log_f = jax.nn.log_sigmoid((proj[..., OFF_F:OFF_U] + b_f[l]).astype(jnp.float32))
        u = proj[..., OFF_U:OFF_G]
        gates = jax.nn.sigmoid((proj[..., OFF_G:] + b_gate[l]).astype(jnp.float32)).astype(x.dtype)
        g_attn = gates[..., :D_MODEL]
        g_ssm = gates[..., D_MODEL:]

        attn = forgetting_attention(q, k, v, log_f) @ w_attn_up[l]

        y = s5_grouped(u, A_re[l], A_im[l], log_dt[l], B_re[l], B_im[l],
                       C_re[l], C_im[l], D_skip[l])
        y = jax.nn.gelu(y)
        y = y * jax.nn.sigmoid(y @ w_glu[l] + b_glu[l])
        ssm = y @ w_ssm_up[l]

        x = x + (g_attn * attn + g_ssm * ssm) @ w_out[l]

        x = x + memory_cross_attention(rmsnorm(x, g_xattn[l]), rmsnorm(mem, g_mem[l]),
                                       wq_x[l], wk_x[l], wv_x[l], wo_x[l])

        hm = rmsnorm(x, g_mlp[l])
        x = x + jnp.square(jax.nn.relu(hm @ w_ff1[l])) @ w_ff2[l]
    return rmsnorm(x, g_final)
```

```python
import math
from contextlib import ExitStack

import numpy as np
import concourse.bass as bass
import concourse.mybir as mybir
from concourse.bass_utils import run_bass_kernel_spmd

F32 = mybir.dt.float32
BF16 = mybir.dt.bfloat16
AF = mybir.ActivationFunctionType
ALU = mybir.AluOpType
EPS = 1e-6
PI = math.pi


class Sem:
    def __init__(self, nc, name):
        self.h = nc.alloc_semaphore(name=name)
        self.c = 0


class Buf:
    __slots__ = ("name", "w", "r")

    def __init__(self, name):
        self.name = name
        self.w = None
        self.r = {}


class Sched:
    def __init__(self, nc):
        self.nc = nc
        self.eng = {"pe": nc.tensor, "act": nc.scalar, "dve": nc.vector,
                    "pool": nc.gpsimd, "sp": nc.sync}
        self.sem = {k: Sem(nc, "e_" + k) for k in self.eng}
        self.seen = {k: {} for k in self.eng}
        self.dsems = []
        self.ninst = 0

    def dsem(self, name):
        s = Sem(self.nc, name)
        self.dsems.append(s)
        return s

    def _deps(self, e, reads, writes, gen=None):
        need = {}
        own = self.sem.get(e)
        for b in reads:
            if b.w is not None:
                sm, v = b.w
                if need.get(sm, 0) < v:
                    need[sm] = v
        for b in writes:
            if b.w is not None and b.w[0] is not own and b.w[0] is not gen:
                sm, v = b.w
                if need.get(sm, 0) < v:
                    need[sm] = v
            for sm, v in b.r.items():
                if sm is not own and need.get(sm, 0) < v:
                    need[sm] = v
        seen = self.seen[e]
        for sm, v in need.items():
            if seen.get(sm, 0) < v:
                self.eng[e].wait_ge(sm.h, v)
                seen[sm] = v

    def _record(self, ev, reads, writes):
        sm, v = ev
        for b in reads:
            if b.r.get(sm, 0) < v:
                b.r[sm] = v
        for b in writes:
            b.w = ev
            b.r = {}

    def op(self, e, fn, reads=(), writes=()):
        self._deps(e, reads, writes)
        inst = fn(self.eng[e])
        sm = self.sem[e]
        sm.c += 1
        inst.then_inc(sm.h, 1)
        self._record((sm, sm.c), reads, writes)
        self.ninst += 1

    def dma(self, q, out, in_, sem, reads=(), writes=()):
        self._deps(q, reads, writes, gen=sem)
        inst = self.eng[q].dma_start(out=out, in_=in_)
        sem.c += 16
        inst.then_inc(sem.h, 16)
        self._record((sem, sem.c), reads, writes)
        self.ninst += 1

    def barrier(self):
        allsems = list(self.sem.values()) + self.dsems
        for e in self.eng:
            seen = self.seen[e]
            for sm in allsems:
                if sm.c > 0 and seen.get(sm, 0) < sm.c:
                    self.eng[e].wait_ge(sm.h, sm.c)
                    seen[sm] = sm.c


class Ring:
    def __init__(self, bld, stack, name, n, shape, dt, dma=True, psum=False):
        self.t, self.b, self.s = [], [], []
        for i in range(n):
            if psum:
                t = stack.enter_context(bld.nc.psum_tensor(f"ps_{name}{i}", list(shape), dt))
            else:
                t = stack.enter_context(bld.nc.sbuf_tensor(f"sb_{name}{i}", list(shape), dt))
            self.t.append(t)
            self.b.append(Buf(f"{name}{i}"))
            self.s.append(bld.getsem(f"{name}{i}") if dma else None)
        self.i = 0
        self.n = n

    def next(self):
        i = self.i % self.n
        self.i += 1
        return self.t[i], self.b[i], self.s[i]


class Cfg:
    def __init__(self, D, L, OWN, NH, SW, XH, MEM, DFF, debug=False):
        self.D, self.L, self.OWN, self.NH, self.SW = D, L, OWN, NH, SW
        self.XH, self.MEM, self.DFF, self.debug = XH, MEM, DFF, debug
        self.AW = NH * 128
        self.KT = D // 128
        self.G = SW // 16
        self.NST = self.G // 2
        self.CT = SW // 128
        self.XW = XH * 256
        self.XT = self.XW // 128
        self.OWN0 = L - OWN
        self.NKT = L // 128
        self.OFF_K = self.AW
        self.OFF_V = 2 * self.AW
        self.OFF_F = 3 * self.AW
        self.OFF_U = self.OFF_F + NH
        self.OFF_G = self.OFF_U + SW
        self.INW = self.OFF_G + 2 * D
        self.FE = min(16, DFF // 128)
        self.NE = DFF // (128 * self.FE)


FULL = Cfg(D=4096, L=8192, OWN=2048, NH=16, SW=1024, XH=4, MEM=256, DFF=16384)


class Builder:
    def __init__(self, cfg):
        self.c = c = cfg
        self.nc = nc = bass.Bass("TRN2", target_bir_lowering=False)
        self.S = Sched(nc)
        self._sempool = {}
        D, L, OWN = c.D, c.L, c.OWN

        def din(name, shape, dt=F32):
            return nc.dram_tensor(name, list(shape), dt, kind="ExternalInput").ap()

        def dscr(name, shape, dt):
            if c.debug:
                return nc.dram_tensor(name, list(shape), dt, kind="ExternalOutput").ap()
            return nc.dram_tensor(name, list(shape), dt).ap()

        self.xl = din("xl", [L, D])
        self.meml = din("meml", [c.MEM, D])
        self.kbias_d = din("kbias", [128, c.NKT])
        self.w_in = din("w_in", [D, c.INW])
        self.w_glu = din("w_glu", [c.SW, c.SW])
        self.w_attn_up = din("w_attn_up", [c.AW, D])
        self.w_ssm_up = din("w_ssm_up", [c.SW, D])
        self.w_out = din("w_out", [D, D])
        self.wq_x = din("wq_x", [D, c.XW])
        self.wk_x = din("wk_x", [D, c.XW])
        self.wv_x = din("wv_x", [D, c.XW])
        self.wo_x = din("wo_x", [c.XW, D])
        self.w_ff1 = din("w_ff1", [D, c.DFF])
        self.w_ff2 = din("w_ff2", [c.DFF, D])
        self.gcols_d = din("gcols", [128, 4 * c.KT])
        self.g_final_d = din("g_final", [D])
        self.bf_d = din("b_f", [c.NH, 1])
        self.bgate_d = din("b_gate_t", [128, 2 * c.KT])
        self.bglu_d = din("b_glu_t", [128, c.CT])
        self.dskip_d = din("dskip_t", [128, c.CT])
        self.s5p_d = din("s5p", [128, 3 * c.NST])
        self.s5b_d = din("s5b", [128, 2, c.NST, 16])
        self.s5c_d = din("s5c", [128, 2, c.NST, 16])
        self.cmat_d = din("cmat", [128, 4 * 128])
        self.out_d = nc.dram_tensor("out", [OWN, D], F32, kind="ExternalOutput").ap()

        self.KT_d = dscr("KT_d", [c.NH, 128, L], BF16)
        self.V_d = dscr("V_d", [L, c.AW], BF16)
        self.uT_d = dscr("uT_d", [c.SW, L], BF16)
        self.f_d = dscr("f_d", [c.NH, L], F32)
        self.QT_d = dscr("QT_d", [c.NH, 128, OWN], BF16)
        self.gT_d = dscr("gT_d", [2 * D, OWN], BF16)
        self.y2T_d = dscr("y2T_d", [c.SW, OWN], BF16)
        self.aoT_d = dscr("aoT_d", [c.AW, OWN], BF16)

    def getsem(self, name):
        return self.S.dsem(name)

    def sb(self, stack, name, shape, dt):
        self._uid = getattr(self, "_uid", 0) + 1
        return stack.enter_context(self.nc.sbuf_tensor(f"sb_{name}_{self._uid}", list(shape), dt))

    def build(self):
        c, nc, S = self.c, self.nc, self.S
        with ExitStack() as top:
            self.top = top
            self.pf = Ring(self, top, "pf", 6, [128, 512], F32, dma=False, psum=True)
            self.pbt = Ring(self, top, "pbt", 2, [128, 1024], BF16, dma=False, psum=True)
            self.identf = self.sb(top, "identf", [128, 128], F32)
            self.identb = self.sb(top, "identb", [128, 128], BF16)
            self.trib = self.sb(top, "trib", [128, 128], BF16)
            self.gcols = self.sb(top, "gcols", [128, 4 * c.KT], F32)
            self.bgate = self.sb(top, "bgate", [128, 2 * c.KT], F32)
            self.bglu = self.sb(top, "bglu", [128, c.CT], F32)
            self.dskip = self.sb(top, "dskip", [128, c.CT], F32)
            self.nbf = self.sb(top, "nbf", [c.NH, 1], F32)
            self.kbias = self.sb(top, "kbias", [128, c.NKT], F32)
            self.cst = self.sb(top, "cst", [128, 4], F32)
            self.small = Ring(self, top, "small", 8, [128, 4], F32, dma=False)
            self.constb = Buf("const")
            cs = self.getsem("const")
            cm = self.cmat_d
            S.dma("sp", self.identf[:], cm[:, 0:128], cs, writes=[self.constb])
            S.dma("pool", self.identb[:], cm[:, 0:128], cs, writes=[self.constb])
            S.dma("pool", self.trib[:], cm[:, 128:256], cs, writes=[self.constb])
            S.dma("sp", self.gcols[:], self.gcols_d[:, :], cs, writes=[self.constb])
            S.dma("sp", self.bgate[:], self.bgate_d[:, :], cs, writes=[self.constb])
            S.dma("sp", self.bglu[:], self.bglu_d[:, :], cs, writes=[self.constb])
            S.dma("sp", self.dskip[:], self.dskip_d[:, :], cs, writes=[self.constb])
            S.dma("sp", self.nbf[:], self.bf_d[:, :], cs, writes=[self.constb])
            S.dma("sp", self.kbias[:], self.kbias_d[:, :], cs, writes=[self.constb])
            S.op("dve", lambda e: e.memset(self.cst[:, 0:1], -0.5), writes=[self.constb])
            S.op("dve", lambda e: e.memset(self.cst[:, 1:2], 1.0), writes=[self.constb])
            S.op("dve", lambda e: e.tensor_scalar(out=self.nbf[:], in0=self.nbf[:], scalar1=-1.0,
                                                  scalar2=None, op0=ALU.mult),
                 reads=[self.constb], writes=[self.constb])
            S.barrier()

            self.phase_A()
            S.barrier()
            with ExitStack() as mid:
                self.sel127 = self.sb(mid, "sel127", [128, 128], F32)
                self.sel63 = self.sb(mid, "sel63", [128, 128], F32)
                self.c_tm = self.sb(mid, "c_tm", [128, c.NKT, c.NH], F32)
                self.c_tmb = Buf("c_tm")
                S.dma("sp", self.sel127[:], cm[:, 256:384], cs, writes=[self.constb])
                S.dma("sp", self.sel63[:], cm[:, 384:512], cs, writes=[self.constb])
                S.barrier()
                self.phase_S5()
                S.barrier()
                self.phase_B()
                S.barrier()
            self.phase_blocks()
            S.barrier()
        return nc

    def slab_load(self, W, k0, nk, c0, ncols):
        S = self.S
        tile, buf, sem = self.slabs.next()
        src = W.rearrange("(kt p) n -> p kt n", p=128)
        nd = min(4, nk)
        step = -(-nk // nd)
        for a in range(0, nk, step):
            b = min(nk, a + step)
            S.dma("pool", tile[:, a:b, 0:ncols], src[:, k0 + a:k0 + b, c0:c0 + ncols], sem,
                  writes=[buf])
        return tile, buf

    def run_items(self, items):
        idx = [i for i, it in enumerate(items) if it[0] is not None]
        loaded = {}
        ptr = 0

        def prefetch(upto):
            nonlocal ptr
            while ptr < len(idx) and ptr < upto:
                loaded[idx[ptr]] = self.slab_load(*items[idx[ptr]][0])
                ptr += 1

        pos = 0
        prefetch(2)
        for i, (spec, fn) in enumerate(items):
            if spec is not None:
                pos += 1
                prefetch(pos + 2)
                tile, buf = loaded.pop(i)
                fn(tile, buf)
            else:
                fn(None, None)

    def items_fm(self, items, W, c0, ncols, nk, actT, actb, ntok, epi, k0=0):
        S = self.S
        for s0 in range(0, ncols, 256):
            n_ = min(256, ncols - s0)

            def fn(tile, buf, s0=s0, n_=n_):
                for ci in range(n_ // 128):
                    ps, psb, _ = self.pf.next()
                    for kt in range(nk):
                        S.op("pe", lambda e, kt=kt, ci=ci: e.matmul(
                            ps[:, 0:ntok], lhsT=tile[:, kt, ci * 128:(ci + 1) * 128],
                            rhs=actT[:, kt, 0:ntok], start=(kt == 0), stop=(kt == nk - 1)),
                            reads=[buf, actb], writes=[psb])
                    epi(s0 // 128 + ci, ps, psb)

            items.append(((W, k0, nk, c0 + s0, n_), fn))

    def items_tm(self, items, W, k0, nk, c0, ncols, actT, actb, ntt, epi):
        S = self.S
        for s0 in range(0, ncols, 256):
            n_ = min(256, ncols - s0)

            def fn(tile, buf, s0=s0, n_=n_):
                for tt in range(ntt):
                    ps, psb, _ = self.pf.next()
                    for kt in range(nk):
                        S.op("pe", lambda e, kt=kt, tt=tt: e.matmul(
                            ps[:, 0:n_], lhsT=actT[:, kt, tt * 128:(tt + 1) * 128],
                            rhs=tile[:, kt, 0:n_], start=(kt == 0), stop=(kt == nk - 1)),
                            reads=[buf, actb], writes=[psb])
                    epi(tt, s0, n_, ps, psb)

            items.append(((W, k0, nk, c0 + s0, n_), fn))

    def norm_front(self, get_src, tt, hb, hbb):
        c, S = self.c, self.S
        D = c.D
        xs, xsb = get_src(tt)
        ss, ssb, _ = self.small.next()
        S.op("act", lambda e: e.activation(out=hb[:, 0:D], in_=xs, func=AF.Square,
                                           accum_out=ss[:, 0:1]),
             reads=[xsb], writes=[hbb, ssb])
        S.op("dve", lambda e: e.tensor_scalar(out=ss[:, 1:2], in0=ss[:, 0:1], scalar1=1.0 / D,
                                              scalar2=EPS, op0=ALU.mult, op1=ALU.add),
             reads=[ssb], writes=[ssb])
        S.op("pool", lambda e: e.tensor_tensor(out=ss[:, 2:3], in0=ss[:, 1:2],
                                               in1=self.cst[:, 0:1], op=ALU.pow),
             reads=[ssb], writes=[ssb])
        S.op("act", lambda e: e.activation(out=hb[:, 0:D], in_=xs, func=AF.Copy, scale=ss[:, 2:3]),
             reads=[xsb, ssb], writes=[hbb])

    def norm_back(self, tt, gofs, hb, hbb, hT, hTb):
        c, S = self.c, self.S
        KT = c.KT
        grp = min(8, KT)
        for k0 in range(0, KT, grp):
            pb, pbb, _ = self.pbt.next()
            for k in range(grp):
                S.op("pe", lambda e, k=k: e.transpose(
                    out=pb[:, k * 128:(k + 1) * 128],
                    in_=hb[:, (k0 + k) * 128:(k0 + k + 1) * 128], identity=self.identb[:]),
                    reads=[hbb], writes=[pbb])
            S.op("dve", lambda e: e.tensor_tensor(
                out=hT[:, k0:k0 + grp, tt * 128:(tt + 1) * 128],
                in0=pb[:, 0:grp * 128].rearrange("p (k t) -> p k t", k=grp),
                in1=self.gcols[:, gofs + k0:gofs + k0 + grp].unsqueeze(2).to_broadcast(
                    [128, grp, 128]),
                op=ALU.mult), reads=[pbb], writes=[hTb])

    def norm_T(self, get_src, nt, gofs, hb, hbb, hT, hTb):
        for tt in range(nt):
            self.norm_front(get_src, tt, hb, hbb)
            self.norm_back(tt, gofs, hb, hbb, hT, hTb)

    def phase_A(self):
        c, S = self.c, self.S
        D, L, KT = c.D, c.L, c.KT
        with ExitStack() as st:
            self.slabs = Ring(self, st, "slabA", 3, [128, 32, 256], BF16)
            xst = Ring(self, st, "xst", 2, [128, D], F32)
            hbs = [(self.sb(st, f"hbA{i}", [128, D], BF16), Buf(f"hbA{i}")) for i in range(4)]
            hTs = [(self.sb(st, f"hTA{i}", [128, KT, 512], BF16), Buf(f"hTA{i}")) for i in range(2)]
            stb = Ring(self, st, "stb", 4, [128, 512], BF16)
            stf = Ring(self, st, "stf", 2, [128, 512], F32)
            items = []
            for bi in range(L // 512):
                t0 = bi * 512
                own = t0 >= c.OWN0
                to0 = t0 - c.OWN0

                hT, hTb = hTs[bi % 2]

                def nfront(tile, buf, bi=bi):
                    t0_ = bi * 512

                    def get_src(tt):
                        xs, xsb, sem = xst.next()
                        S.dma("sp", xs[:], self.xl[t0_ + tt * 128:t0_ + (tt + 1) * 128, :], sem,
                              writes=[xsb])
                        return xs[:], xsb
                    for tt in range(4):
                        self.norm_front(get_src, tt, hbs[tt][0], hbs[tt][1])

                def nback(tile, buf, bi=bi):
                    hT_, hTb_ = hTs[bi % 2]
                    for tt in range(4):
                        self.norm_back(tt, 0, hbs[tt][0], hbs[tt][1], hT_, hTb_)

                items.append((None, nfront))
                items.append((None, nback))
                nb_next = None

                def epi_K(ci, ps, psb, t0=t0):
                    sg, sgb, sem = stb.next()
                    S.op("act", lambda e: e.activation(out=sg[:], in_=ps[:, 0:512], func=AF.Copy),
                         reads=[psb], writes=[sgb])
                    S.dma("sp", self.KT_d[ci, :, t0:t0 + 512], sg[:], sem, reads=[sgb])

                def epi_V(tt, s0, n_, ps, psb, t0=t0):
                    sg, sgb, sem = stb.next()
                    S.op("act", lambda e: e.activation(out=sg[:, 0:n_], in_=ps[:, 0:n_],
                                                       func=AF.Copy),
                         reads=[psb], writes=[sgb])
                    S.dma("sp", self.V_d[t0 + tt * 128:t0 + (tt + 1) * 128, s0:s0 + n_],
                          sg[:, 0:n_], sem, reads=[sgb])

                def epi_F(ci, ps, psb, t0=t0):
                    sg, sgb, sem = stf.next()
                    S.op("act", lambda e: e.activation(out=sg[:], in_=ps[:, 0:512], func=AF.Copy),
                         reads=[psb], writes=[sgb])
                    S.dma("sp", self.f_d[0:c.NH, t0:t0 + 512], sg[0:c.NH, :], sem, reads=[sgb])

                def epi_U(ci, ps, psb, t0=t0):
                    sg, sgb, sem = stb.next()
                    S.op("act", lambda e: e.activation(out=sg[:], in_=ps[:, 0:512], func=AF.Copy),
                         reads=[psb], writes=[sgb])
                    S.dma("sp", self.uT_d[ci * 128:(ci + 1) * 128, t0:t0 + 512], sg[:], sem,
                          reads=[sgb])

                def epi_Q(ci, ps, psb, to0=to0):
                    sg, sgb, sem = stb.next()
                    S.op("act", lambda e: e.activation(out=sg[:], in_=ps[:, 0:512], func=AF.Copy),
                         reads=[psb], writes=[sgb])
                    S.dma("sp", self.QT_d[ci, :, to0:to0 + 512], sg[:], sem, reads=[sgb])

                def epi_G(ci, ps, psb, to0=to0):
                    sg, sgb, sem = stb.next()
                    S.op("act", lambda e: e.activation(out=sg[:], in_=ps[:, 0:512],
                                                       func=AF.Sigmoid,
                                                       bias=self.bgate[:, ci:ci + 1]),
                         reads=[psb], writes=[sgb])
                    S.dma("sp", self.gT_d[ci * 128:(ci + 1) * 128, to0:to0 + 512], sg[:], sem,
                          reads=[sgb])

                self.items_fm(items, self.w_in, c.OFF_K, c.AW, KT, hT, hTb, 512, epi_K)
                self.items_tm(items, self.w_in, 0, KT, c.OFF_V, c.AW, hT, hTb, 4, epi_V)
                self.items_fm(items, self.w_in, c.OFF_F, 128, KT, hT, hTb, 512, epi_F)
                self.items_fm(items, self.w_in, c.OFF_U, c.SW, KT, hT, hTb, 512, epi_U)
                if own:
                    self.items_fm(items, self.w_in, 0, c.AW, KT, hT, hTb, 512, epi_Q)
                    self.items_fm(items, self.w_in, c.OFF_G, 2 * D, KT, hT, hTb, 512, epi_G)
                if nb_next is not None:
                    items.append((None, nb_next))
            self.run_items(items)
            S.barrier()

    def phase_S5(self):
        c, S = self.c, self.S
        L, NH, NST, CT, NKT = c.L, c.NH, c.NST, c.CT, c.NKT
        T = 256
        with ExitStack() as st:
            fl = self.sb(st, "fl", [NH, L], F32)
            fl2 = self.sb(st, "fl2", [NH, L], F32)
            flb = Buf("fl")
            fl2b = Buf("fl2")
            sem = self.getsem("fl")
            S.dma("sp", fl[:], self.f_d[:, :], sem, writes=[flb])
            S.op("act", lambda e: e.activation(out=fl[:], in_=fl[:], func=AF.Exp, scale=-1.0,
                                               bias=self.nbf[:, 0:1]),
                 reads=[flb], writes=[flb])
            S.op("act", lambda e: e.activation(out=fl[:], in_=fl[:], func=AF.Ln, bias=1.0),
                 reads=[flb], writes=[flb])
            CH = min(2048, L)
            for i in range(0, L, CH):
                init = 0.0 if i == 0 else fl2[:, i - 1:i]
                S.op("dve", lambda e, i=i, init=init: e.tensor_tensor_scan(
                    out=fl2[:, i:i + CH], data0=self.cst[0:NH, 1:2].to_broadcast([NH, CH]),
                    data1=fl[:, i:i + CH], initial=init, op0=ALU.mult, op1=ALU.subtract),
                    reads=[flb, fl2b], writes=[fl2b])
            per = 512 // NH
            for k0 in range(0, NKT, per):
                ps, psb, _ = self.pf.next()
                n = min(per, NKT - k0)
                for k in range(n):
                    S.op("pe", lambda e, k=k: e.transpose(
                        out=ps[:, k * NH:(k + 1) * NH], in_=fl2[:, (k0 + k) * 128:(k0 + k + 1) * 128],
                        identity=self.identf[0:NH, 0:NH]), reads=[fl2b], writes=[psb])
                S.op("act", lambda e, n=n: e.activation(
                    out=self.c_tm[:, k0:k0 + n, :],
                    in_=ps[:, 0:n * NH].rearrange("p (k h) -> p k h", h=NH), func=AF.Copy),
                    reads=[psb], writes=[self.c_tmb])
            S.barrier()

        with ExitStack() as st:
            P = self.sb(st, "s5P", [128, 34, NST], F32)
            Pb = Buf("s5P")
            BC = self.sb(st, "s5BC", [128, 2, NST, 16], F32)
            CC = self.sb(st, "s5CC", [128, 2, NST, 16], F32)
            BB = self.sb(st, "s5BB", [128, 2, NST, 16], F32)
            BCb = Buf("s5BC")
            Bm = self.sb(st, "Bm", [128, 2, NST, 128], BF16)
            Cm = self.sb(st, "Cm", [128, 2, NST, 128], BF16)
            Bmb, Cmb = Buf("Bm"), Buf("Cm")
            tabb = Buf("tab")
            sem = self.getsem("s5ld")
            S.dma("sp", P[:, 0:3, :], self.s5p_d.rearrange("p (a s) -> p a s", a=3), sem, writes=[Pb])
            S.dma("sp", BC[:], self.s5b_d[:, :, :, :], sem, writes=[BCb])
            S.dma("sp", CC[:], self.s5c_d[:, :, :, :], sem, writes=[BCb])
            Pb.w = (sem, sem.c)
            BCb.w = (sem, sem.c)

            def pv(i):
                return P[:, i, :]

            def dv(fn):
                S.op("dve", fn, reads=[Pb], writes=[Pb])

            def av(fn):
                S.op("act", fn, reads=[Pb], writes=[Pb])

            TT = lambda o, a, b, op: dv(lambda e: e.tensor_tensor(out=pv(o), in0=pv(a), in1=pv(b), op=op))
            TS = lambda o, a, s1, s2, op0, op1: dv(lambda e: e.tensor_scalar(
                out=pv(o), in0=pv(a), scalar1=s1, scalar2=s2, op0=op0, op1=op1))
            av(lambda e: e.activation(out=pv(3), in_=pv(2), func=AF.Exp))
            TT(4, 3, 0, ALU.mult)
            TT(5, 3, 1, ALU.mult)
            av(lambda e: e.activation(out=pv(6), in_=pv(4), func=AF.Exp))
            TS(7, 5, 0.0, None, ALU.add, ALU.bypass)
            TS(8, 5, PI / 2, None, ALU.add, ALU.bypass)
            for _ in range(4):
                for a in (7, 8):
                    TS(9, a, PI, 2 * PI, ALU.is_gt, ALU.mult)
                    TT(a, a, 9, ALU.subtract)
            av(lambda e: e.activation(out=pv(9), in_=pv(7), func=AF.Sin))
            av(lambda e: e.activation(out=pv(10), in_=pv(8), func=AF.Sin))
            TT(11, 6, 10, ALU.mult)
            TT(12, 6, 9, ALU.mult)
            TT(13, 0, 0, ALU.mult)
            TT(14, 1, 1, ALU.mult)
            TT(13, 13, 14, ALU.add)
            dv(lambda e: e.reciprocal(out=pv(14), in_=pv(13)))
            TS(15, 11, -1.0, None, ALU.add, ALU.bypass)
            TT(16, 15, 0, ALU.mult)
            TT(17, 12, 1, ALU.mult)
            TT(16, 16, 17, ALU.add)
            TT(16, 16, 14, ALU.mult)
            TT(17, 12, 0, ALU.mult)
            TT(18, 15, 1, ALU.mult)
            TT(17, 17, 18, ALU.subtract)
            TT(17, 17, 14, ALU.mult)
            fre = P[:, 16, :].unsqueeze(2).to_broadcast([128, NST, 16])
            fim = P[:, 17, :].unsqueeze(2).to_broadcast([128, NST, 16])

            def bop(o, a, b, op):
                S.op("dve", lambda e: e.tensor_tensor(out=o, in0=a, in1=b, op=op),
                     reads=[Pb, BCb], writes=[BCb])
            bop(BB[:, 0], BC[:, 0], fre, ALU.mult)
            bop(BB[:, 1], BC[:, 1], fim, ALU.mult)
            bop(BB[:, 0], BB[:, 0], BB[:, 1], ALU.subtract)
            bop(BB[:, 1], BC[:, 1], fre, ALU.mult)
            bop(BC[:, 1], BC[:, 0], fim, ALU.mult)
            bop(BB[:, 1], BB[:, 1], BC[:, 1], ALU.add)
            zb = self.sb(st, "zb", [128, 8, 128], BF16)
            zbb = Buf("zb")
            S.op("pool", lambda e: e.memset(zb[:], 0.0), writes=[zbb])
            S.op("pool", lambda e: e.memset(Cm[:], 0.0), writes=[Cmb])
            for s in range(NST):
                off = 32 * (s % 4)
                for ri in range(2):
                    z = zb[:, (s % 4) * 2 + ri, :]
                    S.op("act", lambda e: e.activation(out=z[0:64, off:off + 16],
                                                       in_=BB[0:64, ri, s, :], func=AF.Copy),
                         reads=[BCb], writes=[zbb])
                    S.op("act", lambda e: e.activation(out=z[64:128, off + 16:off + 32],
                                                       in_=BB[64:128, ri, s, :], func=AF.Copy),
                         reads=[BCb], writes=[zbb])
                    pb, pbb, _ = self.pbt.next()
                    S.op("pe", lambda e: e.transpose(out=pb[:, 0:128], in_=z, identity=self.identb[:]),
                         reads=[zbb], writes=[pbb])
                    S.op("dve", lambda e: e.tensor_copy(out=Bm[:, ri, s, :], in_=pb[:, 0:128]),
                         reads=[pbb], writes=[Bmb])
                    sc = 1.0 if ri == 0 else -1.0
                    S.op("act", lambda e: e.activation(out=Cm[0:64, ri, s, off:off + 16],
                                                       in_=CC[0:64, ri, s, :], func=AF.Copy, scale=sc),
                         reads=[BCb], writes=[Cmb])
                    S.op("act", lambda e: e.activation(out=Cm[64:128, ri, s, off + 16:off + 32],
                                                       in_=CC[64:128, ri, s, :], func=AF.Copy, scale=sc),
                         reads=[BCb], writes=[Cmb])
            def build_tables(p1_re, p1_im, reverse, writer):
                H = max(1, NST // 2)
                halves = [(0, H), (H, NST)] if NST > 1 else [(0, NST)]
                for (s0, s1) in halves:
                    n = s1 - s0
                    with ExitStack() as st2:
                        ER = self.sb(st2, "ER", [128, n, T], F32)
                        EI = self.sb(st2, "EI", [128, n, T], F32)
                        T1 = self.sb(st2, "T1", [128, n, T // 2], F32)
                        T2 = self.sb(st2, "T2", [128, n, T // 2], F32)
                        Eb = Buf("E")

                        def ev(fn):
                            S.op("dve", fn, reads=[Pb, Eb], writes=[Eb, Pb])
                        i0 = T - 1 if reverse else 0
                        ev(lambda e: e.memset(ER[:, :, i0:i0 + 1], 1.0))
                        ev(lambda e: e.memset(EI[:, :, i0:i0 + 1], 0.0))
                        TS(20, p1_re, 0.0, None, ALU.add, ALU.bypass)
                        TS(21, p1_im, 0.0, None, ALU.add, ALU.bypass)
                        k = 1
                        while k < T:
                            pr = P[:, 20, s0:s1].unsqueeze(2).to_broadcast([128, n, k])
                            pi_ = P[:, 21, s0:s1].unsqueeze(2).to_broadcast([128, n, k])
                            if reverse:
                                src, dst = slice(T - k, T), slice(T - 2 * k, T - k)
                            else:
                                src, dst = slice(0, k), slice(k, 2 * k)
                            a_r, a_i = ER[:, :, src], EI[:, :, src]
                            t1, t2 = T1[:, :, 0:k], T2[:, :, 0:k]
                            ev(lambda e: e.tensor_tensor(out=t1, in0=a_r, in1=pr, op=ALU.mult))
                            ev(lambda e: e.tensor_tensor(out=t2, in0=a_i, in1=pi_, op=ALU.mult))
                            ev(lambda e: e.tensor_tensor(out=ER[:, :, dst], in0=t1, in1=t2, op=ALU.subtract))
                            ev(lambda e: e.tensor_tensor(out=t1, in0=a_r, in1=pi_, op=ALU.mult))
                            ev(lambda e: e.tensor_tensor(out=t2, in0=a_i, in1=pr, op=ALU.mult))
                            ev(lambda e: e.tensor_tensor(out=EI[:, :, dst], in0=t1, in1=t2, op=ALU.add))
                            TT(22, 20, 20, ALU.mult)
                            TT(23, 21, 21, ALU.mult)
                            TT(24, 20, 21, ALU.mult)
                            TT(20, 22, 23, ALU.subtract)
                            TS(21, 24, 2.0, None, ALU.mult, ALU.bypass)
                            k *= 2
                        writer(ER, EI, s0, s1, Eb)
                        S.barrier()

            burn = Ring.__new__(Ring)
            burn.t, burn.b, burn.s, burn.i, burn.n = self.pf.t[0:4], self.pf.b[0:4], [None] * 4, 0, 4
            utsrc = self.uT_d.rearrange("(ct p) t -> p ct t", p=128)
            dv(lambda e: e.memset(P[:, 28:30, :], 0.0))
            NPRE = c.OWN0 // T
            with ExitStack() as sa:
                WA = self.sb(sa, "WA", [128, NST, 2 * T], BF16)
                WB = self.sb(sa, "WB", [128, NST, 2 * T], BF16)

                def wr_ab(ER, EI, s0, s1, Eb):
                    S.op("dve", lambda e: e.tensor_copy(out=WA[:, s0:s1, 0:T], in_=ER[:]), reads=[Eb], writes=[tabb])
                    S.op("act", lambda e: e.activation(out=WA[:, s0:s1, T:2 * T], in_=EI[:], func=AF.Copy,
                                                       scale=-1.0), reads=[Eb], writes=[tabb])
                    S.op("dve", lambda e: e.tensor_copy(out=WB[:, s0:s1, 0:T], in_=EI[:]), reads=[Eb], writes=[tabb])
                    S.op("act", lambda e: e.activation(out=WB[:, s0:s1, T:2 * T], in_=ER[:], func=AF.Copy),
                         reads=[Eb], writes=[tabb])
                if NPRE > 0:
                    build_tables(11, 12, True, wr_ab)
                    TS(30, 20, 0.0, None, ALU.add, ALU.bypass)
                    TS(31, 21, 0.0, None, ALU.add, ALU.bypass)
                    utr = Ring(self, sa, "utrA", 2, [128, CT, 1024], BF16)
                    junk = Ring(self, sa, "junk", 2, [128, 2 * T], BF16, dma=False)
                    locr = Ring(self, sa, "locr", 2, [128, 2, NST], F32, dma=False)
                    uts = {}
                    for chn in range(NPRE):
                        cg, cc = divmod(chn, 4)
                        if cg not in uts:
                            ut, utb, usem = utr.next()
                            S.dma("sp", ut[:], utsrc[:, :, cg * 1024:(cg + 1) * 1024], usem, writes=[utb])
                            uts[cg] = (ut, utb)
                        ut, utb = uts[cg]
                        ucols = slice(cc * T, (cc + 1) * T)
                        loc, locb, _ = locr.next()
                        for s in range(NST):
                            ct = s // 4
                            ps, psb, _ = burn.next()
                            S.op("pe", lambda e: e.matmul(ps[:, 0:T], lhsT=Bm[:, 0, s, :], rhs=ut[:, ct, ucols],
                                                          start=True, stop=True), reads=[Bmb, utb], writes=[psb])
                            S.op("pe", lambda e: e.matmul(ps[:, T:2 * T], lhsT=Bm[:, 1, s, :], rhs=ut[:, ct, ucols],
                                                          start=True, stop=True), reads=[Bmb, utb], writes=[psb])
                            for ri, W_ in enumerate((WA, WB)):
                                jk, jkb, _ = junk.next()
                                S.op("dve", lambda e, ri=ri, W_=W_, jk=jk: e.scalar_tensor_tensor(
                                    out=jk[:], in0=ps[:, 0:2 * T], scalar=1.0, in1=W_[:, s, :],
                                    op0=ALU.mult, op1=ALU.mult, accum_out=loc[:, ri, s:s + 1]),
                                    reads=[psb, tabb], writes=[jkb, locb])
                        TT(22, 30, 28, ALU.mult)
                        TT(23, 31, 29, ALU.mult)
                        TT(32, 30, 29, ALU.mult)
                        TT(33, 31, 28, ALU.mult)
                        TT(22, 22, 23, ALU.subtract)
                        TT(32, 32, 33, ALU.add)
                        S.op("dve", lambda e: e.tensor_tensor(out=pv(28), in0=pv(22), in1=loc[:, 0, :], op=ALU.add),
                             reads=[Pb, locb], writes=[Pb])
                        S.op("dve", lambda e: e.tensor_tensor(out=pv(29), in0=pv(32), in1=loc[:, 1, :], op=ALU.add),
                             reads=[Pb, locb], writes=[Pb])
                    S.barrier()
            cosT = self.sb(st, "cosT", [128, NST, T], BF16)
            sinT = self.sb(st, "sinT", [128, NST, T], BF16)

            def wr_cs(ER, EI, s0, s1, Eb):
                S.op("dve", lambda e: e.tensor_copy(out=cosT[:, s0:s1, :], in_=ER[:]), reads=[Eb], writes=[tabb])
                S.op("act", lambda e: e.activation(out=sinT[:, s0:s1, :], in_=EI[:], func=AF.Copy),
                     reads=[Eb], writes=[tabb])
            build_tables(10, 9, False, wr_cs)
            TT(22, 10, 28, ALU.mult)
            TT(23, 9, 29, ALU.mult)
            TT(25, 22, 23, ALU.subtract)
            TT(22, 10, 29, ALU.mult)
            TT(23, 9, 28, ALU.mult)
            TT(26, 22, 23, ALU.add)

            wg = self.sb(st, "wg", [128, CT, c.SW], BF16)
            wgb = Buf("wg")
            semw = self.getsem("wgld")
            for ct in range(CT):
                S.dma("pool", wg[:, ct, :], self.w_glu[ct * 128:(ct + 1) * 128, :], semw, writes=[wgb])
            wgb.w = (semw, semw.c)
            zl = self.sb(st, "zl", [128, 2, NST], F32)
            zlb = Buf("zl")
            zib = Buf("zi")
            utr = Ring(self, st, "utr", 2, [128, CT, 1024], BF16)
            wk1 = Ring(self, st, "wk1", 3, [128, 512], F32, dma=False)
            wk2 = Ring(self, st, "wk2", 3, [128, 512], F32, dma=False)
            wwr = Ring(self, st, "wwr", 2, [128, 512], F32, dma=False)
            zr = Ring(self, st, "zr", 3, [128, 512], F32, dma=False)
            q1r = Ring(self, st, "q1r", 2, [128, 512], BF16, dma=False)
            q2r = Ring(self, st, "q2r", 2, [128, 512], BF16, dma=False)
            zhr = Ring(self, st, "zhr", 3, [128, 512], BF16, dma=False)
            xr = Ring(self, st, "xr", 3, [128, 512], BF16, dma=False)
            ygf = Ring(self, st, "ygf", 2, [128, CT, T], F32, dma=False)
            ygb = Ring(self, st, "ygb", 2, [128, CT, T], BF16, dma=False)
            tmp = Ring(self, st, "s5tmp", 3, [128, T], F32, dma=False)
            y2s = Ring(self, st, "y2s", 3, [128, T], BF16)
            ybank = [(self.pf.t[4], self.pf.b[4]), (self.pf.t[5], self.pf.b[5])]
            tiles = [(chn // 4, chn % 4, s) for chn in range(NPRE, L // T) for s in range(NST)]
            uts, st0, st1, st15, st2, st3, ych = {}, {}, {}, {}, {}, {}, {}
            yidx = [0]

            def stage1(i):
                cg, cc, s = tiles[i]
                if cg not in uts:
                    ut, utb, usem = utr.next()
                    S.dma("sp", ut[:], utsrc[:, :, cg * 1024:(cg + 1) * 1024], usem, writes=[utb])
                    uts[cg] = (ut, utb)
                ut, utb = uts[cg]
                ct = s // 4
                ucols = slice(cc * T, (cc + 1) * T)
                ps, psb, _ = burn.next()
                S.op("pe", lambda e: e.matmul(ps[:, 0:T], lhsT=Bm[:, 0, s, :], rhs=ut[:, ct, ucols],
                                              start=True, stop=True), reads=[Bmb, utb], writes=[psb])
                S.op("pe", lambda e: e.matmul(ps[:, T:2 * T], lhsT=Bm[:, 1, s, :], rhs=ut[:, ct, ucols],
                                              start=True, stop=True), reads=[Bmb, utb], writes=[psb])
                st0[i] = (ps, psb)

            def stage1b(i):
                cg, cc, s = tiles[i]
                ps, psb = st0.pop(i)
                t13, t13b, _ = wk1.next()
                t42, t42b, _ = wk2.next()
                ps3 = ps[:, :].rearrange("p (two t) -> p two t", two=2)
                cosb = cosT[:, s:s + 1, :].to_broadcast([128, 2, T])
                sinb = sinT[:, s:s + 1, :].to_broadcast([128, 2, T])
                S.op("dve", lambda e: e.tensor_tensor(
                    out=t13[:, :].rearrange("p (two t) -> p two t", two=2), in0=ps3, in1=cosb,
                    op=ALU.mult), reads=[psb, tabb], writes=[t13b])
                S.op("dve", lambda e: e.tensor_tensor(
                    out=t42[:, :].rearrange("p (two t) -> p two t", two=2), in0=ps3, in1=sinb,
                    op=ALU.mult), reads=[psb, tabb], writes=[t42b])
                st1[i] = (t13, t13b, t42, t42b, cosb, sinb)

            def stage2(i):
                cg, cc, s = tiles[i]
                ut, utb = uts[cg]
                ct = s // 4
                ucols = slice(cc * T, (cc + 1) * T)
                tl0 = cg * 1024 + cc * T
                own = tl0 >= c.OWN0
                chn = cg * 4 + cc
                if own and s == 0:
                    yf, yfb, _ = ygf.next()
                    yb, ybb, _ = ygb.next()
                    ych[chn] = (yf, yfb, yb, ybb)
                t13, t13b, t42, t42b, cosb, sinb = st1.pop(i)
                ww, wwb, _ = wwr.next()
                z, zb_, _ = zr.next()
                S.op("dve", lambda e: e.scalar_tensor_tensor(out=ww[:, 0:T], in0=t13[:, 0:T], scalar=1.0,
                                                             in1=t42[:, T:2 * T], op0=ALU.mult, op1=ALU.add),
                     reads=[t13b, t42b], writes=[wwb])
                S.op("dve", lambda e: e.scalar_tensor_tensor(out=ww[:, T:2 * T], in0=t13[:, T:2 * T], scalar=1.0,
                                                             in1=t42[:, 0:T], op0=ALU.mult, op1=ALU.subtract),
                     reads=[t13b, t42b], writes=[wwb])
                st15[i] = (ww, wwb, z, zb_, cosb, sinb)

            def stage2b(i):
                cg, cc, s = tiles[i]
                ww, wwb, z, zb_, cosb, sinb = st15.pop(i)
                rb = P[:, 6, s:s + 1].to_broadcast([128, T])
                for ri in range(2):
                    S.op("dve", lambda e, ri=ri: e.tensor_tensor_scan(
                        out=z[:, ri * T:(ri + 1) * T], data0=rb, data1=ww[:, ri * T:(ri + 1) * T],
                        initial=P[:, 25 + ri, s:s + 1], op0=ALU.mult, op1=ALU.add),
                        reads=[wwb, Pb, zib], writes=[zb_])
                    S.op("act", lambda e, ri=ri: e.activation(
                        out=zl[:, ri, s:s + 1], in_=z[:, (ri + 1) * T - 1:(ri + 1) * T], func=AF.Copy),
                        reads=[zb_], writes=[zlb])
                zh, zhb, _ = zhr.next()
                S.op("act", lambda e: e.activation(out=zh[:], in_=z[:], func=AF.Copy),
                     reads=[zb_], writes=[zhb])
                st2[i] = (zh, zhb, cosb, sinb)
                if s == NST - 1:
                    def cv(fn):
                        S.op("dve", fn, reads=[Pb, zlb, zib], writes=[Pb])
                    cv(lambda e: e.tensor_tensor(out=pv(22), in0=pv(20), in1=zl[:, 0, :], op=ALU.mult))
                    cv(lambda e: e.tensor_tensor(out=pv(23), in0=pv(21), in1=zl[:, 1, :], op=ALU.mult))
                    cv(lambda e: e.tensor_tensor(out=pv(24), in0=pv(20), in1=zl[:, 1, :], op=ALU.mult))
                    cv(lambda e: e.tensor_tensor(out=pv(27), in0=pv(21), in1=zl[:, 0, :], op=ALU.mult))
                    S.op("dve", lambda e: e.tensor_tensor(out=pv(25), in0=pv(22), in1=pv(23), op=ALU.subtract),
                         reads=[Pb], writes=[Pb, zib])
                    S.op("dve", lambda e: e.tensor_tensor(out=pv(26), in0=pv(24), in1=pv(27), op=ALU.add),
                         reads=[Pb], writes=[Pb, zib])

            def stage3(i):
                cg, cc, s = tiles[i]
                ut, utb = uts[cg]
                ct = s // 4
                ucols = slice(cc * T, (cc + 1) * T)
                tl0 = cg * 1024 + cc * T
                own = True
                chn = cg * 4 + cc
                zh, zhb, cosb, sinb = st2.pop(i)
                zb_ = zhb
                if own:
                    yf, yfb, yb, ybb = ych[chn]
                    q1, q1b, _ = q1r.next()
                    q2, q2b, _ = q2r.next()
                    x, xb_, _ = xr.next()
                    z3 = zh[:, :].rearrange("p (two t) -> p two t", two=2)
                    S.op("dve", lambda e: e.tensor_tensor(
                        out=q1[:, :].rearrange("p (two t) -> p two t", two=2), in0=z3, in1=cosb,
                        op=ALU.mult), reads=[zb_, tabb], writes=[q1b])
                    S.op("dve", lambda e: e.tensor_tensor(
                        out=q2[:, :].rearrange("p (two t) -> p two t", two=2), in0=z3, in1=sinb,
                        op=ALU.mult), reads=[zb_, tabb], writes=[q2b])
                    S.op("pool", lambda e: e.tensor_tensor(out=x[:, 0:T], in0=q1[:, 0:T],
                                                           in1=q2[:, T:2 * T], op=ALU.subtract),
                         reads=[q1b, q2b], writes=[xb_])
                    S.op("pool", lambda e: e.tensor_tensor(out=x[:, T:2 * T], in0=q2[:, 0:T],
                                                           in1=q1[:, T:2 * T], op=ALU.add),
                         reads=[q1b, q2b], writes=[xb_])
                    st3[i] = (x, xb_)

            def stage4(i):
                cg, cc, s = tiles[i]
                ut, utb = uts[cg]
                ct = s // 4
                ucols = slice(cc * T, (cc + 1) * T)
                tl0 = cg * 1024 + cc * T
                own = True
                chn = cg * 4 + cc
                x, xb_ = st3.pop(i)
                if own:
                    yf, yfb, yb, ybb = ych[chn]
                    yps, ypsb = ybank[yidx[0] % 2]
                    S.op("pe", lambda e: e.matmul(yps[:, 0:T], lhsT=Cm[:, 0, s, :], rhs=x[:, 0:T],
                                                  start=(s % 4 == 0), stop=False),
                         reads=[Cmb, xb_], writes=[ypsb])
                    S.op("pe", lambda e: e.matmul(yps[:, 0:T], lhsT=Cm[:, 1, s, :], rhs=x[:, T:2 * T],
                                                  start=False, stop=(s % 4 == 3)),
                         reads=[Cmb, xb_], writes=[ypsb])
                    if s % 4 == 3:
                        yidx[0] += 1
                        yv = yf[:, ct, :]
                        t1, t1b, _ = tmp.next()
                        S.op("dve", lambda e: e.scalar_tensor_tensor(
                            out=yv, in0=ut[:, ct, ucols], scalar=self.dskip[:, ct:ct + 1],
                            in1=yps[:, 0:T], op0=ALU.mult, op1=ALU.add),
                            reads=[utb, ypsb], writes=[yfb])
                        S.op("pool", lambda e: e.tensor_tensor(out=t1[:], in0=yv, in1=yv, op=ALU.mult),
                             reads=[yfb], writes=[t1b])
                        S.op("act", lambda e: e.activation(out=t1[:], in_=t1[:], func=AF.Copy,
                                                           scale=0.044715, bias=1.0),
                             reads=[t1b], writes=[t1b])
                        S.op("pool", lambda e: e.tensor_tensor(out=t1[:], in0=t1[:], in1=yv, op=ALU.mult),
                             reads=[t1b, yfb], writes=[t1b])
                        S.op("act", lambda e: e.activation(out=t1[:], in_=t1[:], func=AF.Sigmoid,
                                                           scale=1.5957691216),
                             reads=[t1b], writes=[t1b])
                        S.op("pool", lambda e: e.tensor_tensor(out=yv, in0=yv, in1=t1[:], op=ALU.mult),
                             reads=[t1b, yfb], writes=[yfb])
                        S.op("act", lambda e: e.activation(out=yb[:, ct, :], in_=yv, func=AF.Copy),
                             reads=[yfb], writes=[ybb])
                if s == NST - 1:
                    if own:
                        yf, yfb, yb, ybb = ych.pop(chn)
                        to0 = tl0 - c.OWN0
                        for co in range(CT):
                            ps, psb = glu_ps, self.pbt.b[0]
                            for ct2 in range(CT):
                                S.op("pe", lambda e, ct2=ct2: e.matmul(
                                    ps[:, 0:T], lhsT=wg[:, ct2, co * 128:(co + 1) * 128], rhs=yb[:, ct2, :],
                                    start=(ct2 == 0), stop=(ct2 == CT - 1)),
                                    reads=[wgb, ybb], writes=[psb])
                            t1, t1b, _ = tmp.next()
                            S.op("act", lambda e: e.activation(out=t1[:], in_=ps[:, 0:T], func=AF.Sigmoid,
                                                               bias=self.bglu[:, co:co + 1]),
                                 reads=[psb], writes=[t1b])
                            sg, sgb, ssem = y2s.next()
                            S.op("pool", lambda e: e.tensor_tensor(out=sg[:], in0=yf[:, co, :], in1=t1[:],
                                                                   op=ALU.mult),
                                 reads=[yfb, t1b], writes=[sgb])
                            S.dma("sp", self.y2T_d[co * 128:(co + 1) * 128, to0:to0 + T], sg[:], ssem,
                                  reads=[sgb])

            glu_ps = self.pbt.t[0][:].bitcast(F32)
            NT_ = len(tiles)
            for i in range(-2, NT_ + 2):
                if 0 <= i + 2 < NT_:
                    stage1(i + 2)
                if 0 <= i < NT_:
                    stage2(i)
                if 0 <= i - 1 < NT_:
                    stage3(i - 1)
                if 0 <= i < NT_:
                    stage2b(i)
                if 0 <= i + 1 < NT_:
                    stage1b(i + 1)
                if 0 <= i - 2 < NT_:
                    stage4(i - 2)
            S.barrier()

    def phase_B(self):
        c, S = self.c, self.S
        L, NH, NKT, OWN = c.L, c.NH, c.NKT, c.OWN
        NQT = OWN // 128
        NSB = OWN // 512
        NB = NKT // 4
        K0 = c.OWN0 // 128
        scale = 128 ** -0.5
        with ExitStack() as st:
            ktr = Ring(self, st, "ktr", 2, [128, L], BF16)
            vr = Ring(self, st, "vr", 2, [128, NKT, 129], BF16)
            qr = Ring(self, st, "qr", 2, [128, OWN], BF16)
            ptr_ = Ring(self, st, "ptr", 12, [128, 512], BF16, dma=False)
            accs = []
            for i in range(2):
                t = self.sb(st, f"acc{i}", [128, 4, 129], F32)
                accs.append((t, [Buf(f"acc{i}_{q}") for q in range(4)]))
            tabs = []
            for i in range(2):
                tabs.append(dict(
                    rE=self.sb(st, f"rE{i}", [128, NKT], F32), rM=self.sb(st, f"rM{i}", [128, NKT], F32),
                    bO=self.sb(st, f"bO{i}", [128, NKT], F32), bD=self.sb(st, f"bD{i}", [128, NKT], F32),
                    fO=self.sb(st, f"fO{i}", [128, NQT, NB], F32), fD=self.sb(st, f"fD{i}", [128, NQT, 4], F32),
                    b=Buf(f"btab{i}")))
            obr = Ring(self, st, "obr", 4, [128, 128], BF16, dma=False)
            aos = Ring(self, st, "aos", 2, [128, 512], BF16)
            recr = Ring(self, st, "recr", 4, [128, 1], F32, dma=False)
            for i in range(2):
                S.op("pool", lambda e, i=i: e.memset(vr.t[i][:, :, 128:129], 1.0), writes=[vr.b[i]])
            stb = Ring.__new__(Ring)
            st4 = self.pbt.t[1][:].bitcast(F32)
            stb.t = [self.pf.t[0][:, :], self.pf.t[1][:, :], self.pf.t[2][:, :], st4]
            stb.b, stb.s, stb.i, stb.n = self.pf.b[0:3] + [self.pbt.b[1]], [None] * 4, 0, 4
            poslots = [(self.pf.t[3], 0, self.pf.b[3]), (self.pf.t[4], 0, self.pf.b[4]),
                       (self.pf.t[5], 0, self.pf.b[5])]
            poi = [0]

            def next_po():
                t, o, b = poslots[poi[0] % len(poslots)]
                poi[0] += 1
                return t[:, o:o + 129], b

            steps = []
            for h in range(NH):
                for sb_ in range(NSB):
                    nboff = (c.OWN0 + sb_ * 512) // 512
                    for B in range(nboff):
                        steps.append((h, sb_, B, False, B == 0, False))
                    steps.append((h, sb_, nboff, True, nboff == 0, True))
            heads, fr = {}, {}

            def load_head(h):
                kt_, ktb, ks = ktr.next()
                v_, vb, vs = vr.next()
                q_, qb, qs = qr.next()
                S.dma("sp", kt_[:], self.KT_d[h, :, :], ks, writes=[ktb])
                vsrc = self.V_d.rearrange("(kt p) n -> p kt n", p=128)
                step = max(1, NKT // 4)
                for a in range(0, NKT, step):
                    S.dma("sp", v_[:, a:a + step, 0:128], vsrc[:, a:a + step, h * 128:(h + 1) * 128], vs,
                          writes=[vb])
                S.dma("sp", q_[:], self.QT_d[h, :, :], qs, writes=[qb])
                heads[h] = (kt_, ktb, v_, vb, q_, qb)

            def prologue(h):
                T_ = tabs[h % 2]
                tb = T_["b"]
                rE, rM, bO, bD, fO, fD = T_["rE"], T_["rM"], T_["bO"], T_["bD"], T_["fO"], T_["fD"]
                ch = self.c_tm[:, :, h]
                ps, psb, _ = stb.next()
                S.op("pe", lambda e: e.matmul(ps[:, 0:NKT], lhsT=self.sel127[:], rhs=ch, start=True, stop=True),
                     reads=[self.c_tmb], writes=[psb])
                S.op("pe", lambda e: e.matmul(ps[:, NKT:2 * NKT], lhsT=self.sel63[:], rhs=ch, start=True, stop=True),
                     reads=[self.c_tmb], writes=[psb])
                S.op("act", lambda e: e.activation(out=rE[:], in_=ps[:, 0:NKT], func=AF.Copy),
                     reads=[psb], writes=[tb])
                S.op("act", lambda e: e.activation(out=rM[:], in_=ps[:, NKT:2 * NKT], func=AF.Copy),
                     reads=[psb], writes=[tb])
                rE4 = rE[:, :].rearrange("p (b f) -> p b f", f=4)
                S.op("dve", lambda e: e.tensor_tensor(
                    out=bO[:, :].rearrange("p (b f) -> p b f", f=4),
                    in0=rE4[:, :, 3:4].to_broadcast([128, NB, 4]),
                    in1=ch.rearrange("p (b f) -> p b f", f=4), op=ALU.subtract),
                    reads=[tb, self.c_tmb], writes=[tb])
                S.op("dve", lambda e: e.tensor_tensor(out=bO[:], in0=bO[:], in1=self.kbias[:], op=ALU.add),
                     reads=[tb], writes=[tb])
                S.op("dve", lambda e: e.tensor_tensor(out=bD[:], in0=rM[:], in1=ch, op=ALU.subtract),
                     reads=[tb, self.c_tmb], writes=[tb])
                S.op("dve", lambda e: e.tensor_tensor(out=bD[:], in0=bD[:], in1=self.kbias[:], op=ALU.add),
                     reads=[tb], writes=[tb])
                for qt in range(NQT):
                    cq = self.c_tm[:, K0 + qt, h:h + 1]
                    S.op("act", lambda e, qt=qt, cq=cq: e.activation(
                        out=fO[:, qt, :], in_=rE4[:, :, 3], func=AF.Exp, scale=-1.0, bias=cq),
                        reads=[tb, self.c_tmb], writes=[tb])
                    kd = K0 + 4 * (qt // 4)
                    S.op("act", lambda e, qt=qt, cq=cq, kd=kd: e.activation(
                        out=fD[:, qt, :], in_=rM[:, kd:kd + 4], func=AF.Exp, scale=-1.0, bias=cq),
                        reads=[tb, self.c_tmb], writes=[tb])

            def front(i):
                h, sb_, B, diag, first, last = steps[i]
                if sb_ == 0 and B == 0:
                    if h == 0:
                        load_head(0)
                    prologue(h)
                kt_, ktb, v_, vb, q_, qb = heads[h]
                T_ = tabs[h % 2]
                btab = T_["bD"] if diag else T_["bO"]
                qcols = slice(sb_ * 512, (sb_ + 1) * 512)
                pts = []
                for j in range(4):
                    kt = 4 * B + j
                    ps, psb, _ = stb.next()
                    S.op("pe", lambda e, kt=kt: e.matmul(ps[:, 0:512], lhsT=kt_[:, kt * 128:(kt + 1) * 128],
                                                         rhs=q_[:, qcols], start=True, stop=True),
                         reads=[ktb, qb], writes=[psb])
                    pt, ptb, _ = ptr_.next()
                    S.op("act", lambda e, kt=kt: e.activation(out=pt[:], in_=ps[:, 0:512], func=AF.Exp,
                                                              scale=scale, bias=btab[:, kt:kt + 1]),
                         reads=[psb, T_["b"]], writes=[ptb])
                    if diag:
                        S.op("pool", lambda e, j=j: e.tensor_tensor(
                            out=pt[:, j * 128:(j + 1) * 128], in0=pt[:, j * 128:(j + 1) * 128],
                            in1=self.trib[:], op=ALU.mult), reads=[ptb], writes=[ptb])
                    pts.append((pt, ptb))
                fr[i] = pts

            def back(i):
                h, sb_, B, diag, first, last = steps[i]
                if sb_ == 0 and B == 0 and h + 1 < NH:
                    load_head(h + 1)
                kt_, ktb, v_, vb, q_, qb = heads[h]
                T_ = tabs[h % 2]
                tb = T_["b"]
                acc, accb = accs[(h * NSB + sb_) % 2]
                if first:
                    S.op("pool", lambda e: e.memset(acc[:], 0.0), writes=accb)
                pts = fr.pop(i)
                if not diag:
                    for qt in range(4):
                        po, pob = next_po()
                        for j in range(4):
                            pt, ptb = pts[j]
                            S.op("pe", lambda e, pt=pt, j=j, qt=qt: e.matmul(
                                po, lhsT=pt[:, qt * 128:(qt + 1) * 128], rhs=v_[:, 4 * B + j, :],
                                start=(j == 0), stop=(j == 3)), reads=[ptb, vb], writes=[pob])
                        qg = sb_ * 4 + qt
                        S.op("dve", lambda e, qt=qt, qg=qg: e.scalar_tensor_tensor(
                            out=acc[:, qt, :], in0=po, scalar=T_["fO"][:, qg, B:B + 1],
                            in1=acc[:, qt, :], op0=ALU.mult, op1=ALU.add),
                            reads=[pob, tb, accb[qt]], writes=[accb[qt]])
                else:
                    for j in range(4):
                        pt, ptb = pts[j]
                        kt = 4 * B + j
                        for qt in range(j, 4):
                            po, pob = next_po()
                            S.op("pe", lambda e, qt=qt, pt=pt, kt=kt: e.matmul(
                                po, lhsT=pt[:, qt * 128:(qt + 1) * 128], rhs=v_[:, kt, :],
                                start=True, stop=True), reads=[ptb, vb], writes=[pob])
                            qg = sb_ * 4 + qt
                            S.op("dve", lambda e, qt=qt, qg=qg, j=j: e.scalar_tensor_tensor(
                                out=acc[:, qt, :], in0=po, scalar=T_["fD"][:, qg, j:j + 1],
                                in1=acc[:, qt, :], op0=ALU.mult, op1=ALU.add),
                                reads=[pob, tb, accb[qt]], writes=[accb[qt]])
                if last:
                    pb, pbb = self.pbt.t[0], self.pbt.b[0]
                    for qt in range(4):
                        rc, rcb, _ = recr.next()
                        ob, obb, _ = obr.next()
                        S.op("dve", lambda e, qt=qt: e.reciprocal(out=rc[:], in_=acc[:, qt, 128:129]),
                             reads=[accb[qt]], writes=[rcb])
                        S.op("dve", lambda e, qt=qt: e.tensor_scalar(
                            out=ob[:], in0=acc[:, qt, 0:128], scalar1=rc[:, 0:1], scalar2=None, op0=ALU.mult),
                            reads=[accb[qt], rcb], writes=[obb])
                        S.op("pe", lambda e, qt=qt: e.transpose(out=pb[:, qt * 128:(qt + 1) * 128], in_=ob[:],
                                                                identity=self.identb[:]),
                             reads=[obb], writes=[pbb])
                    ao, aob, asem = aos.next()
                    S.op("act", lambda e: e.activation(out=ao[:], in_=pb[:, 0:512], func=AF.Copy),
                         reads=[pbb], writes=[aob])
                    S.dma("sp", self.aoT_d[h * 128:(h + 1) * 128, sb_ * 512:(sb_ + 1) * 512], ao[:], asem,
                          reads=[aob])

            front(0)
            for i in range(len(steps)):
                if i + 1 < len(steps):
                    front(i + 1)
                back(i)
            S.barrier()

    def phase_blocks(self):
        c, S = self.c, self.S
        D, KT, CT, XT, XW = c.D, c.KT, c.CT, c.XT, c.XW
        MT = c.MEM // 128
        with ExitStack() as st:
            self.slabs = Ring(self, st, "slabB", 3, [128, 32, 256], BF16)
            xcur = self.sb(st, "xcur", [128, 4, D], F32)
            xb = [Buf(f"xcur{i}") for i in range(4)]
            xsems = [self.getsem(f"xcur{i}") for i in range(4)]
            actA = self.sb(st, "actA", [128, KT, 512], BF16)
            actAb = Buf("actA")
            actB = self.sb(st, "actB", [128, 16, 512], BF16)
            actBb = Buf("actB")
            HBN = max(D, CT * 512, 4 * XW)
            hb = self.sb(st, "hbB", [128, HBN], BF16)
            hbb = Buf("hbB")
            hsem = self.getsem("hbB")
            asem = self.getsem("actBld")
            gts = Ring(self, st, "gts", 3, [128, 2, 512], BF16)
            tmp = Ring(self, st, "btmp", 4, [128, 512], F32, dma=False)
            ptx = Ring(self, st, "ptx", 4, [128, 512], BF16, dma=False)
            recr = Ring(self, st, "recx", 4, [128, 1], F32, dma=False)
            kxT = self.sb(st, "kxT", [128, XT, c.MEM], BF16)
            kxb = Buf("kxT")
            vx = self.sb(st, "vx", [128, MT, c.XH, 257], BF16)
            vxb = Buf("vx")
            gsem = self.getsem("gfin")
            actAf = actA[:].rearrange("p a b -> p (a b)").bitcast(F32)

            items = []
            xst_done = []

            def mem_norm(tile, buf):
                def get_src(tt):
                    S.dma("sp", xcur[:, tt, :], self.meml[tt * 128:(tt + 1) * 128, :], xsems[tt], writes=[xb[tt]])
                    return xcur[:, tt, :], xb[tt]
                self.norm_T(get_src, MT, 2 * KT, hb, hbb, actA, actAb)
                S.op("pool", lambda e: e.memset(vx[:, :, :, 256:257], 1.0), writes=[vxb])
            items.append((None, mem_norm))

            def epi_kx(ci, ps, psb):
                S.op("act", lambda e: e.activation(out=kxT[:, ci, :], in_=ps[:, 0:c.MEM], func=AF.Copy),
                     reads=[psb], writes=[kxb])

            def epi_vx(tt, s0, n_, ps, psb):
                hx = s0 // 256
                S.op("act", lambda e: e.activation(out=vx[:, tt, hx, 0:256], in_=ps[:, 0:n_], func=AF.Copy),
                     reads=[psb], writes=[vxb])
            self.items_fm(items, self.wk_x, 0, XW, KT, actA, actAb, c.MEM, epi_kx)
            self.items_tm(items, self.wv_x, 0, KT, 0, XW, actA, actAb, MT, epi_vx)

            for ob in range(c.OWN // 512):
                to0 = ob * 512
                tl0 = c.OWN0 + to0
                def c_load(tile, buf, to0=to0, tl0=tl0):
                    S.dma("sp", actB[:, 0:c.AW // 128, :],
                          self.aoT_d.rearrange("(kt p) t -> p kt t", p=128)[:, :, to0:to0 + 512], asem,
                          writes=[actBb])
                    S.dma("sp", hb[:, 0:CT * 512].rearrange("p (k t) -> p k t", k=CT),
                          self.y2T_d.rearrange("(kt p) t -> p kt t", p=128)[:, :, to0:to0 + 512], hsem,
                          writes=[hbb])
                    for tt in range(4):
                        S.dma("sp", xcur[:, tt, :], self.xl[tl0 + tt * 128:tl0 + (tt + 1) * 128, :], xsems[tt],
                              writes=[xb[tt]])
                items.append((None, c_load))
                y2v = hb[:, 0:CT * 512].rearrange("p (k t) -> p k t", k=CT)
                NA = c.AW // 128
                for s0 in range(0, D, 256):
                    st_ = {}

                    def fnA(tile, buf, st_=st_, s0=s0, to0=to0):
                        for ci in range(2):
                            j = s0 // 128 + ci
                            g, gb, gs = gts.next()
                            S.dma("sp", g[:, 0, :], self.gT_d[j * 128:(j + 1) * 128, to0:to0 + 512], gs, writes=[gb])
                            S.dma("sp", g[:, 1, :], self.gT_d[D + j * 128:D + (j + 1) * 128, to0:to0 + 512], gs,
                                  writes=[gb])
                            ps, psb, _ = self.pf.next()
                            for kt in range(NA):
                                S.op("pe", lambda e, kt=kt, ci=ci: e.matmul(
                                    ps[:, 0:512], lhsT=tile[:, kt, ci * 128:(ci + 1) * 128], rhs=actB[:, kt, :],
                                    start=(kt == 0), stop=(kt == NA - 1)), reads=[buf, actBb], writes=[psb])
                            t1, t1b, _ = tmp.next()
                            S.op("dve", lambda e: e.tensor_tensor(out=t1[:], in0=ps[:, 0:512], in1=g[:, 0, :],
                                                                  op=ALU.mult), reads=[psb, gb], writes=[t1b])
                            st_[ci] = (t1, t1b, g, gb)

                    def fnS(tile, buf, st_=st_, s0=s0):
                        for ci in range(2):
                            j = s0 // 128 + ci
                            t1, t1b, g, gb = st_[ci]
                            ps, psb, _ = self.pf.next()
                            for kt in range(CT):
                                S.op("pe", lambda e, kt=kt, ci=ci: e.matmul(
                                    ps[:, 0:512], lhsT=tile[:, kt, ci * 128:(ci + 1) * 128], rhs=y2v[:, kt, :],
                                    start=(kt == 0), stop=(kt == CT - 1)), reads=[buf, hbb], writes=[psb])
                            t2, t2b, _ = tmp.next()
                            S.op("dve", lambda e: e.tensor_tensor(out=t2[:], in0=ps[:, 0:512], in1=g[:, 1, :],
                                                                  op=ALU.mult), reads=[psb, gb], writes=[t2b])
                            S.op("pool", lambda e, j=j: e.tensor_tensor(out=actA[:, j, :], in0=t1[:], in1=t2[:],
                                                                        op=ALU.add),
                                 reads=[t1b, t2b], writes=[actAb])

                    items.append(((self.w_attn_up, 0, NA, s0, 256), fnA))
                    items.append(((self.w_ssm_up, 0, CT, s0, 256), fnS))

                def epi_res(tt, s0, n_, ps, psb):
                    S.op("dve", lambda e: e.tensor_tensor(out=xcur[:, tt, s0:s0 + n_], in0=ps[:, 0:n_],
                                                          in1=xcur[:, tt, s0:s0 + n_], op=ALU.add),
                         reads=[psb, xb[tt]], writes=[xb[tt]])
                self.items_tm(items, self.w_out, 0, KT, 0, D, actA, actAb, 4, epi_res)

                def d_norm(tile, buf):
                    self.norm_T(lambda tt: (xcur[:, tt, :], xb[tt]), 4, KT, hb, hbb, actA, actAb)
                items.append((None, d_norm))

                def epi_qx(ci, ps, psb):
                    S.op("act", lambda e: e.activation(out=actB[:, ci, :], in_=ps[:, 0:512], func=AF.Copy),
                         reads=[psb], writes=[actBb])
                self.items_fm(items, self.wq_x, 0, XW, KT, actA, actAb, 512, epi_qx)
                oxv = hb[:, 0:4 * XW].rearrange("p (t w) -> p t w", t=4)

                def d_attn(tile, buf):
                    for hx in range(c.XH):
                        pts = []
                        for mt in range(MT):
                            ps, psb, _ = self.pf.next()
                            for dt_ in range(2):
                                S.op("pe", lambda e, dt_=dt_, mt=mt: e.matmul(
                                    ps[:, 0:512], lhsT=kxT[:, hx * 2 + dt_, mt * 128:(mt + 1) * 128],
                                    rhs=actB[:, hx * 2 + dt_, :], start=(dt_ == 0), stop=(dt_ == 1)),
                                    reads=[kxb, actBb], writes=[psb])
                            pt, ptb, _ = ptx.next()
                            S.op("act", lambda e: e.activation(out=pt[:], in_=ps[:, 0:512], func=AF.Exp,
                                                               scale=1.0 / 16.0), reads=[psb], writes=[ptb])
                            pts.append((pt, ptb))
                        for qt in range(4):
                            po, pob, _ = self.pf.next()
                            for mt in range(MT):
                                pt, ptb = pts[mt]
                                S.op("pe", lambda e, mt=mt, pt=pt, qt=qt: e.matmul(
                                    po[:, 0:257], lhsT=pt[:, qt * 128:(qt + 1) * 128], rhs=vx[:, mt, hx, :],
                                    start=(mt == 0), stop=(mt == MT - 1)), reads=[ptb, vxb], writes=[pob])
                            rc, rcb, _ = recr.next()
                            S.op("dve", lambda e: e.reciprocal(out=rc[:], in_=po[:, 256:257]),
                                 reads=[pob], writes=[rcb])
                            S.op("dve", lambda e, qt=qt: e.tensor_scalar(
                                out=oxv[:, qt, hx * 256:(hx + 1) * 256], in0=po[:, 0:256], scalar1=rc[:, 0:1],
                                scalar2=None, op0=ALU.mult), reads=[pob, rcb], writes=[hbb])
                    for qt in range(4):
                        pb, pbb, _ = self.pbt.next()
                        for j in range(XT):
                            S.op("pe", lambda e, j=j, qt=qt: e.transpose(
                                out=pb[:, j * 128:(j + 1) * 128], in_=oxv[:, qt, j * 128:(j + 1) * 128],
                                identity=self.identb[:]), reads=[hbb], writes=[pbb])
                        S.op("act", lambda e, qt=qt: e.activation(
                            out=actB[:, 8:8 + XT, qt * 128:(qt + 1) * 128],
                            in_=pb[:, 0:XT * 128].rearrange("p (k t) -> p k t", k=XT), func=AF.Copy),
                            reads=[pbb], writes=[actBb])
                items.append((None, d_attn))
                self.items_tm(items, self.wo_x, 0, XT, 0, D, actB[:, 8:8 + XT, :], actBb, 4, epi_res)

                def e_norm(tile, buf):
                    self.norm_T(lambda tt: (xcur[:, tt, :], xb[tt]), 4, 3 * KT, hb, hbb, actA, actAb)
                items.append((None, e_norm))
                FE = c.FE
                for q in range(c.NE):
                    def epi_h(ci, ps, psb):
                        t1, t1b, _ = tmp.next()
                        S.op("act", lambda e: e.activation(out=t1[:], in_=ps[:, 0:512], func=AF.Relu),
                             reads=[psb], writes=[t1b])
                        S.op("pool", lambda e: e.tensor_tensor(out=actB[:, ci, :], in0=t1[:], in1=t1[:],
                                                               op=ALU.mult), reads=[t1b], writes=[actBb])
                    self.items_fm(items, self.w_ff1, q * FE * 128, FE * 128, KT, actA, actAb, 512, epi_h)
                    self.items_tm(items, self.w_ff2, q * FE, FE, 0, D, actB, actBb, 4, epi_res)

                def fin(tile, buf, to0=to0):
                    gv = actAf[:, 0:D]
                    S.dma("sp", gv, self.g_final_d.partition_broadcast(128), gsem, reads=[], writes=[actAb])
                    for tt in range(4):
                        ss, ssb, _ = self.small.next()
                        S.op("act", lambda e, tt=tt: e.activation(out=hb[:, 0:D], in_=xcur[:, tt, :], func=AF.Square,
                                                                   accum_out=ss[:, 0:1]),
                             reads=[xb[tt]], writes=[hbb, ssb])
                        S.op("dve", lambda e: e.tensor_scalar(out=ss[:, 1:2], in0=ss[:, 0:1], scalar1=1.0 / D,
                                                              scalar2=EPS, op0=ALU.mult, op1=ALU.add),
                             reads=[ssb], writes=[ssb])
                        S.op("pool", lambda e: e.tensor_tensor(out=ss[:, 2:3], in0=ss[:, 1:2],
                                                               in1=self.cst[:, 0:1], op=ALU.pow),
                             reads=[ssb], writes=[ssb])
                        S.op("dve", lambda e, tt=tt: e.scalar_tensor_tensor(
                            out=xcur[:, tt, :], in0=xcur[:, tt, :], scalar=ss[:, 2:3], in1=gv,
                            op0=ALU.mult, op1=ALU.mult), reads=[xb[tt], ssb, actAb], writes=[xb[tt]])
                        S.dma("sp", self.out_d[to0 + tt * 128:to0 + (tt + 1) * 128, :], xcur[:, tt, :], xsems[tt],
                              reads=[xb[tt]])
                items.append((None, fin))
            self.run_items(items)
            S.barrier()


def _const_mats():
    ident = np.eye(128, dtype=np.float32)
    k = np.arange(128)
    tri = (k[:, None] <= k[None, :]).astype(np.float32)
    sel127 = np.zeros((128, 128), np.float32)
    sel127[127, :] = 1.0
    sel63 = np.zeros((128, 128), np.float32)
    sel63[63, :] = 1.0
    return np.concatenate([ident, tri, sel127, sel63], axis=1)


def _col(v, kt):
    return np.ascontiguousarray(np.asarray(v, np.float32).reshape(kt, 128).T)


def make_core_inputs(cfg, xl, kvalid0, meml, p):
    c = cfg
    kb = np.zeros(c.L, np.float32)
    kb[:kvalid0] = -30000.0
    f32 = lambda a: np.ascontiguousarray(np.asarray(a, np.float32))
    sm = lambda a: np.ascontiguousarray(np.asarray(a, np.float32).reshape(c.NST, 128).T)
    ldt = np.repeat(np.asarray(p["log_dt"], np.float32)[:, None], 64, axis=1)
    s5p = np.concatenate([sm(p["A_re"]), sm(p["A_im"]), sm(ldt)], axis=1)

    def bl(a):
        a = np.asarray(a, np.float32).reshape(c.NST, 2, 64, 16)
        return a.transpose(1, 2, 0, 3).reshape(128, c.NST, 16)

    def cl(a):
        a = np.asarray(a, np.float32).transpose(0, 2, 1).reshape(c.NST, 2, 64, 16)
        return a.transpose(1, 2, 0, 3).reshape(128, c.NST, 16)

    return {
        "xl": f32(xl), "meml": f32(meml), "kbias": _col(kb, c.NKT),
        "w_in": f32(p["w_in"]), "w_glu": f32(p["w_glu"]), "w_attn_up": f32(p["w_attn_up"]),
        "w_ssm_up": f32(p["w_ssm_up"]), "w_out": f32(p["w_out"]), "wq_x": f32(p["wq_x"]),
        "wk_x": f32(p["wk_x"]), "wv_x": f32(p["wv_x"]), "wo_x": f32(p["wo_x"]),
        "w_ff1": f32(p["w_ff1"]), "w_ff2": f32(p["w_ff2"]),
        "gcols": np.concatenate([_col(p["g_mix"], c.KT), _col(p["g_xattn"], c.KT),
                                 _col(p["g_mem"], c.KT), _col(p["g_mlp"], c.KT)], axis=1),
        "g_final": f32(p["g_final"]),
        "b_f": f32(np.asarray(p["b_f"]).reshape(c.NH, 1)),
        "b_gate_t": _col(p["b_gate"], 2 * c.KT), "b_glu_t": _col(p["b_glu"], c.CT),
        "dskip_t": _col(np.asarray(p["D_skip"]).reshape(-1), c.CT),
        "s5p": np.ascontiguousarray(s5p),
        "s5b": np.ascontiguousarray(np.stack([bl(p["B_re"]), bl(p["B_im"])], axis=1)),
        "s5c": np.ascontiguousarray(np.stack([cl(p["C_re"]), cl(p["C_im"])], axis=1)),
        "cmat": _const_mats(),
    }


_NC_CACHE = {}


def kernel(x, mem, g_mix, w_in, b_f, b_gate, A_re, A_im, log_dt, B_re, B_im, C_re, C_im,
           D_skip, w_glu, b_glu, w_attn_up, w_ssm_up, w_out, g_xattn, g_mem, wq_x, wk_x,
           wv_x, wo_x, g_mlp, w_ff1, w_ff2, g_final):
    cfg = FULL
    x = np.asarray(x, np.float32)
    mem = np.asarray(mem, np.float32)
    p = dict(g_mix=g_mix[0], w_in=w_in[0], b_f=b_f[0], b_gate=b_gate[0], A_re=A_re[0], A_im=A_im[0],
             log_dt=log_dt[0], B_re=B_re[0], B_im=B_im[0], C_re=C_re[0], C_im=C_im[0], D_skip=D_skip[0],
             w_glu=w_glu[0], b_glu=b_glu[0], w_attn_up=w_attn_up[0], w_ssm_up=w_ssm_up[0], w_out=w_out[0],
             g_xattn=g_xattn[0], g_mem=g_mem[0], wq_x=wq_x[0], wk_x=wk_x[0], wv_x=wv_x[0], wo_x=wo_x[0],
             g_mlp=g_mlp[0], w_ff1=w_ff1[0], w_ff2=w_ff2[0], g_final=g_final)
    p = {k: np.asarray(v, np.float32) for k, v in p.items()}
    B, SEQ, D = x.shape
    nq = SEQ // cfg.OWN
    in_maps = []
    shared = None
    for core in range(8):
        b, j = core // nq, core % nq
        xl = np.zeros((cfg.L, D), np.float32)
        n = (j + 1) * cfg.OWN
        xl[cfg.L - n:] = x[b, :n]
        m = make_core_inputs(cfg, xl, cfg.L - n, mem[b], p) if shared is None else None
        if shared is None:
            shared = m
        else:
            m = dict(shared)
            kb = np.zeros(cfg.L, np.float32)
            kb[:cfg.L - n] = -30000.0
            m["xl"] = xl
            m["meml"] = np.ascontiguousarray(mem[b])
            m["kbias"] = _col(kb, cfg.NKT)
        in_maps.append(m)
    if "full" not in _NC_CACHE:
        _NC_CACHE["full"] = Builder(cfg).build()
    nc = _NC_CACHE["full"]
    res = run_bass_kernel_spmd(nc, in_maps, core_ids=list(range(8)))
    out = np.zeros((B, SEQ, D), np.float32)
    for core in range(8):
        b, j = core // nq, core % nq
        out[b, j * cfg.OWN:(j + 1) * cfg.OWN] = res.results[core]["out"]
    return out
```

```python
import math
from contextlib import ExitStack

import numpy as np
import concourse.bass as bass
import concourse.mybir as mybir
from concourse.bass_utils import run_bass_kernel_spmd

F32 = mybir.dt.float32
BF16 = mybir.dt.bfloat16
AF = mybir.ActivationFunctionType
ALU = mybir.AluOpType
EPS = 1e-6
PI = math.pi


class Sem:
    def __init__(self, nc, name):
        self.h = nc.alloc_semaphore(name=name)
        self.c = 0


class Buf:
    __slots__ = ("name", "w", "r")

    def __init__(self, name):
        self.name = name
        self.w = None
        self.r = {}


class Sched:
    def __init__(self, nc):
        self.nc = nc
        self.eng = {"pe": nc.tensor, "act": nc.scalar, "dve": nc.vector,
                    "pool": nc.gpsimd, "sp": nc.sync}
        self.sem = {k: Sem(nc, "e_" + k) for k in self.eng}
        self.seen = {k: {} for k in self.eng}
        self.dsems = []
        self.ninst = 0

    def dsem(self, name):
        s = Sem(self.nc, name)
        self.dsems.append(s)
        return s

    def _deps(self, e, reads, writes, gen=None):
        need = {}
        own = self.sem.get(e)
        for b in reads:
            if b.w is not None:
                sm, v = b.w
                if need.get(sm, 0) < v:
                    need[sm] = v
        for b in writes:
            if b.w is not None and b.w[0] is not own and b.w[0] is not gen:
                sm, v = b.w
                if need.get(sm, 0) < v:
                    need[sm] = v
            for sm, v in b.r.items():
                if sm is not own and need.get(sm, 0) < v:
                    need[sm] = v
        seen = self.seen[e]
        for sm, v in need.items():
            if seen.get(sm, 0) < v:
                self.eng[e].wait_ge(sm.h, v)
                seen[sm] = v

    def _record(self, ev, reads, writes):
        sm, v = ev
        for b in reads:
            if b.r.get(sm, 0) < v:
                b.r[sm] = v
        for b in writes:
            b.w = ev
            b.r = {}

    def op(self, e, fn, reads=(), writes=()):
        self._deps(e, reads, writes)
        inst = fn(self.eng[e])
        sm = self.sem[e]
        sm.c += 1
        inst.then_inc(sm.h, 1)
        self._record((sm, sm.c), reads, writes)
        self.ninst += 1

    def dma(self, q, out, in_, sem, reads=(), writes=()):
        self._deps(q, reads, writes, gen=sem)
        inst = self.eng[q].dma_start(out=out, in_=in_)
        sem.c += 16
        inst.then_inc(sem.h, 16)
        self._record((sem, sem.c), reads, writes)
        self.ninst += 1

    def barrier(self):
        allsems = list(self.sem.values()) + self.dsems
        for e in self.eng:
            seen = self.seen[e]
            for sm in allsems:
                if sm.c > 0 and seen.get(sm, 0) < sm.c:
                    self.eng[e].wait_ge(sm.h, sm.c)
                    seen[sm] = sm.c


class Ring:
    def __init__(self, bld, stack, name, n, shape, dt, dma=True, psum=False):
        self.t, self.b, self.s = [], [], []
        for i in range(n):
            if psum:
                t = stack.enter_context(bld.nc.psum_tensor(f"ps_{name}{i}", list(shape), dt))
            else:
                t = stack.enter_context(bld.nc.sbuf_tensor(f"sb_{name}{i}", list(shape), dt))
            self.t.append(t)
            self.b.append(Buf(f"{name}{i}"))
            self.s.append(bld.getsem(f"{name}{i}") if dma else None)
        self.i = 0
        self.n = n

    def next(self):
        i = self.i % self.n
        self.i += 1
        return self.t[i], self.b[i], self.s[i]


class Cfg:
    def __init__(self, D, L, OWN, NH, SW, XH, MEM, DFF, debug=False):
        self.D, self.L, self.OWN, self.NH, self.SW = D, L, OWN, NH, SW
        self.XH, self.MEM, self.DFF, self.debug = XH, MEM, DFF, debug
        self.AW = NH * 128
        self.KT = D // 128
        self.G = SW // 16
        self.NST = self.G // 2
        self.CT = SW // 128
        self.XW = XH * 256
        self.XT = self.XW // 128
        self.OWN0 = L - OWN
        self.NKT = L // 128
        self.OFF_K = self.AW
        self.OFF_V = 2 * self.AW
        self.OFF_F = 3 * self.AW
        self.OFF_U = self.OFF_F + NH
        self.OFF_G = self.OFF_U + SW
        self.INW = self.OFF_G + 2 * D
        self.FE = min(16, DFF // 128)
        self.NE = DFF // (128 * self.FE)


FULL = Cfg(D=4096, L=8192, OWN=2048, NH=16, SW=1024, XH=4, MEM=256, DFF=16384)


class Builder:
    def __init__(self, cfg):
        self.c = c = cfg
        self.nc = nc = bass.Bass("TRN2", target_bir_lowering=False)
        self.S = Sched(nc)
        self._sempool = {}
        D, L, OWN = c.D, c.L, c.OWN

        def din(name, shape, dt=F32):
            return nc.dram_tensor(name, list(shape), dt, kind="ExternalInput").ap()

        def dscr(name, shape, dt):
            if c.debug:
                return nc.dram_tensor(name, list(shape), dt, kind="ExternalOutput").ap()
            return nc.dram_tensor(name, list(shape), dt).ap()

        self.xl = din("xl", [L, D])
        self.meml = din("meml", [c.MEM, D])
        self.kbias_d = din("kbias", [128, c.NKT])
        self.w_in = din("w_in", [D, c.INW])
        self.w_glu = din("w_glu", [c.SW, c.SW])
        self.w_attn_up = din("w_attn_up", [c.AW, D])
        self.w_ssm_up = din("w_ssm_up", [c.SW, D])
        self.w_out = din("w_out", [D, D])
        self.wq_x = din("wq_x", [D, c.XW])
        self.wk_x = din("wk_x", [D, c.XW])
        self.wv_x = din("wv_x", [D, c.XW])
        self.wo_x = din("wo_x", [c.XW, D])
        self.w_ff1 = din("w_ff1", [D, c.DFF])
        self.w_ff2 = din("w_ff2", [c.DFF, D])
        self.gcols_d = din("gcols", [128, 4 * c.KT])
        self.g_final_d = din("g_final", [D])
        self.bf_d = din("b_f", [c.NH, 1])
        self.bgate_d = din("b_gate_t", [128, 2 * c.KT])
        self.bglu_d = din("b_glu_t", [128, c.CT])
        self.dskip_d = din("dskip_t", [128, c.CT])
        self.s5p_d = din("s5p", [128, 3 * c.NST])
        self.s5b_d = din("s5b", [128, 2, c.NST, 16])
        self.s5c_d = din("s5c", [128, 2, c.NST, 16])
        self.cmat_d = din("cmat", [128, 4 * 128])
        self.out_d = nc.dram_tensor("out", [OWN, D], F32, kind="ExternalOutput").ap()

        self.KT_d = dscr("KT_d", [c.NH, 128, L], BF16)
        self.V_d = dscr("V_d", [L, c.AW], BF16)
        self.uT_d = dscr("uT_d", [c.SW, L], BF16)
        self.f_d = dscr("f_d", [c.NH, L], F32)
        self.QT_d = dscr("QT_d", [c.NH, 128, OWN], BF16)
        self.gT_d = dscr("gT_d", [2 * D, OWN], BF16)
        self.y2T_d = dscr("y2T_d", [c.SW, OWN], BF16)
        self.aoT_d = dscr("aoT_d", [c.AW, OWN], BF16)

    def getsem(self, name):
        return self.S.dsem(name)

    def sb(self, stack, name, shape, dt):
        self._uid = getattr(self, "_uid", 0) + 1
        return stack.enter_context(self.nc.sbuf_tensor(f"sb_{name}_{self._uid}", list(shape), dt))

    def build(self):
        c, nc, S = self.c, self.nc, self.S
        with ExitStack() as top:
            self.top = top
            self.pf = Ring(self, top, "pf", 6, [128, 512], F32, dma=False, psum=True)
            self.pbt = Ring(self, top, "pbt", 2, [128, 1024], BF16, dma=False, psum=True)
            self.identf = self.sb(top, "identf", [128, 128], F32)
            self.identb = self.sb(top, "identb", [128, 128], BF16)
            self.trib = self.sb(top, "trib", [128, 128], BF16)
            self.gcols = self.sb(top, "gcols", [128, 4 * c.KT], F32)
            self.bgate = self.sb(top, "bgate", [128, 2 * c.KT], F32)
            self.bglu = self.sb(top, "bglu", [128, c.CT], F32)
            self.dskip = self.sb(top, "dskip", [128, c.CT], F32)
            self.nbf = self.sb(top, "nbf", [c.NH, 1], F32)
            self.kbias = self.sb(top, "kbias", [128, c.NKT], F32)
            self.cst = self.sb(top, "cst", [128, 4], F32)
            self.small = Ring(self, top, "small", 8, [128, 4], F32, dma=False)
            self.constb = Buf("const")
            cs = self.getsem("const")
            cm = self.cmat_d
            S.dma("sp", self.identf[:], cm[:, 0:128], cs, writes=[self.constb])
            S.dma("pool", self.identb[:], cm[:, 0:128], cs, writes=[self.constb])
            S.dma("pool", self.trib[:], cm[:, 128:256], cs, writes=[self.constb])
            S.dma("sp", self.gcols[:], self.gcols_d[:, :], cs, writes=[self.constb])
            S.dma("sp", self.bgate[:], self.bgate_d[:, :], cs, writes=[self.constb])
            S.dma("sp", self.bglu[:], self.bglu_d[:, :], cs, writes=[self.constb])
            S.dma("sp", self.dskip[:], self.dskip_d[:, :], cs, writes=[self.constb])
            S.dma("sp", self.nbf[:], self.bf_d[:, :], cs, writes=[self.constb])
            S.dma("sp", self.kbias[:], self.kbias_d[:, :], cs, writes=[self.constb])
            S.op("dve", lambda e: e.memset(self.cst[:, 0:1], -0.5), writes=[self.constb])
            S.op("dve", lambda e: e.memset(self.cst[:, 1:2], 1.0), writes=[self.constb])
            S.op("dve", lambda e: e.tensor_scalar(out=self.nbf[:], in0=self.nbf[:], scalar1=-1.0,
                                                  scalar2=None, op0=ALU.mult),
                 reads=[self.constb], writes=[self.constb])
            S.barrier()

            self.phase_A()
            S.barrier()
            with ExitStack() as mid:
                self.sel127 = self.sb(mid, "sel127", [128, 128], F32)
                self.sel63 = self.sb(mid, "sel63", [128, 128], F32)
                self.c_tm = self.sb(mid, "c_tm", [128, c.NKT, c.NH], F32)
                self.c_tmb = Buf("c_tm")
                S.dma("sp", self.sel127[:], cm[:, 256:384], cs, writes=[self.constb])
                S.dma("sp", self.sel63[:], cm[:, 384:512], cs, writes=[self.constb])
                S.barrier()
                self.phase_S5()
                S.barrier()
                self.phase_B()
                S.barrier()
            self.phase_blocks()
            S.barrier()
        return nc

    def slab_load(self, W, k0, nk, c0, ncols):
        S = self.S
        tile, buf, sem = self.slabs.next()
        src = W.rearrange("(kt p) n -> p kt n", p=128)
        nd = min(4, nk)
        step = -(-nk // nd)
        for a in range(0, nk, step):
            b = min(nk, a + step)
            S.dma("pool", tile[:, a:b, 0:ncols], src[:, k0 + a:k0 + b, c0:c0 + ncols], sem,
                  writes=[buf])
        return tile, buf

    def run_items(self, items):
        idx = [i for i, it in enumerate(items) if it[0] is not None]
        loaded = {}
        ptr = 0

        def prefetch(upto):
            nonlocal ptr
            while ptr < len(idx) and ptr < upto:
                loaded[idx[ptr]] = self.slab_load(*items[idx[ptr]][0])
                ptr += 1

        pos = 0
        prefetch(2)
        for i, (spec, fn) in enumerate(items):
            if spec is not None:
                pos += 1
                prefetch(pos + 2)
                tile, buf = loaded.pop(i)
                fn(tile, buf)
            else:
                fn(None, None)

    def items_fm(self, items, W, c0, ncols, nk, actT, actb, ntok, epi, k0=0):
        S = self.S
        for s0 in range(0, ncols, 256):
            n_ = min(256, ncols - s0)

            def fn(tile, buf, s0=s0, n_=n_):
                for ci in range(n_ // 128):
                    ps, psb, _ = self.pf.next()
                    for kt in range(nk):
                        S.op("pe", lambda e, kt=kt, ci=ci: e.matmul(
                            ps[:, 0:ntok], lhsT=tile[:, kt, ci * 128:(ci + 1) * 128],
                            rhs=actT[:, kt, 0:ntok], start=(kt == 0), stop=(kt == nk - 1)),
                            reads=[buf, actb], writes=[psb])
                    epi(s0 // 128 + ci, ps, psb)

            items.append(((W, k0, nk, c0 + s0, n_), fn))

    def items_tm(self, items, W, k0, nk, c0, ncols, actT, actb, ntt, epi):
        S = self.S
        for s0 in range(0, ncols, 256):
            n_ = min(256, ncols - s0)

            def fn(tile, buf, s0=s0, n_=n_):
                for tt in range(ntt):
                    ps, psb, _ = self.pf.next()
                    for kt in range(nk):
                        S.op("pe", lambda e, kt=kt, tt=tt: e.matmul(
                            ps[:, 0:n_], lhsT=actT[:, kt, tt * 128:(tt + 1) * 128],
                            rhs=tile[:, kt, 0:n_], start=(kt == 0), stop=(kt == nk - 1)),
                            reads=[buf, actb], writes=[psb])
                    epi(tt, s0, n_, ps, psb)

            items.append(((W, k0, nk, c0 + s0, n_), fn))

    def norm_front(self, get_src, tt, hb, hbb):
        c, S = self.c, self.S
        D = c.D
        xs, xsb = get_src(tt)
        ss, ssb, _ = self.small.next()
        S.op("act", lambda e: e.activation(out=hb[:, 0:D], in_=xs, func=AF.Square,
                                           accum_out=ss[:, 0:1]),
             reads=[xsb], writes=[hbb, ssb])
        S.op("dve", lambda e: e.tensor_scalar(out=ss[:, 1:2], in0=ss[:, 0:1], scalar1=1.0 / D,
                                              scalar2=EPS, op0=ALU.mult, op1=ALU.add),
             reads=[ssb], writes=[ssb])
        S.op("pool", lambda e: e.tensor_tensor(out=ss[:, 2:3], in0=ss[:, 1:2],
                                               in1=self.cst[:, 0:1], op=ALU.pow),
             reads=[ssb], writes=[ssb])
        S.op("act", lambda e: e.activation(out=hb[:, 0:D], in_=xs, func=AF.Copy, scale=ss[:, 2:3]),
             reads=[xsb, ssb], writes=[hbb])

    def norm_back(self, tt, gofs, hb, hbb, hT, hTb):
        c, S = self.c, self.S
        KT = c.KT
        grp = min(8, KT)
        for k0 in range(0, KT, grp):
            pb, pbb, _ = self.pbt.next()
            for k in range(grp):
                S.op("pe", lambda e, k=k: e.transpose(
                    out=pb[:, k * 128:(k + 1) * 128],
                    in_=hb[:, (k0 + k) * 128:(k0 + k + 1) * 128], identity=self.identb[:]),
                    reads=[hbb], writes=[pbb])
            S.op("dve", lambda e: e.tensor_tensor(
                out=hT[:, k0:k0 + grp, tt * 128:(tt + 1) * 128],
                in0=pb[:, 0:grp * 128].rearrange("p (k t) -> p k t", k=grp),
                in1=self.gcols[:, gofs + k0:gofs + k0 + grp].unsqueeze(2).to_broadcast(
                    [128, grp, 128]),
                op=ALU.mult), reads=[pbb], writes=[hTb])

    def norm_T(self, get_src, nt, gofs, hb, hbb, hT, hTb):
        for tt in range(nt):
            self.norm_front(get_src, tt, hb, hbb)
            self.norm_back(tt, gofs, hb, hbb, hT, hTb)

    def phase_A(self):
        c, S = self.c, self.S
        D, L, KT = c.D, c.L, c.KT
        with ExitStack() as st:
            self.slabs = Ring(self, st, "slabA", 3, [128, 32, 256], BF16)
            xst = Ring(self, st, "xst", 2, [128, D], F32)
            hbs = [(self.sb(st, f"hbA{i}", [128, D], BF16), Buf(f"hbA{i}")) for i in range(4)]
            hTs = [(self.sb(st, f"hTA{i}", [128, KT, 512], BF16), Buf(f"hTA{i}")) for i in range(2)]
            stb = Ring(self, st, "stb", 4, [128, 512], BF16)
            stf = Ring(self, st, "stf", 2, [128, 512], F32)
            items = []
            for bi in range(L // 512):
                t0 = bi * 512
                own = t0 >= c.OWN0
                to0 = t0 - c.OWN0

                hT, hTb = hTs[bi % 2]

                def nfront(tile, buf, bi=bi):
                    t0_ = bi * 512

                    def get_src(tt):
                        xs, xsb, sem = xst.next()
                        S.dma("sp", xs[:], self.xl[t0_ + tt * 128:t0_ + (tt + 1) * 128, :], sem,
                              writes=[xsb])
                        return xs[:], xsb
                    for tt in range(4):
                        self.norm_front(get_src, tt, hbs[tt][0], hbs[tt][1])

                def nback(tile, buf, bi=bi):
                    hT_, hTb_ = hTs[bi % 2]
                    for tt in range(4):
                        self.norm_back(tt, 0, hbs[tt][0], hbs[tt][1], hT_, hTb_)

                if bi == 0:
                    items.append((None, nfront))
                    items.append((None, nback))
                if bi + 1 < L // 512:
                    nf_next = (lambda tile, buf, f=nfront, b=bi + 1: f(tile, buf, bi=b))
                    nb_next = (lambda tile, buf, f=nback, b=bi + 1: f(tile, buf, bi=b))
                    items.append((None, nf_next))
                else:
                    nb_next = None

                def epi_K(ci, ps, psb, t0=t0):
                    sg, sgb, sem = stb.next()
                    S.op("act", lambda e: e.activation(out=sg[:], in_=ps[:, 0:512], func=AF.Copy),
                         reads=[psb], writes=[sgb])
                    S.dma("sp", self.KT_d[ci, :, t0:t0 + 512], sg[:], sem, reads=[sgb])

                def epi_V(tt, s0, n_, ps, psb, t0=t0):
                    sg, sgb, sem = stb.next()
                    S.op("act", lambda e: e.activation(out=sg[:, 0:n_], in_=ps[:, 0:n_],
                                                       func=AF.Copy),
                         reads=[psb], writes=[sgb])
                    S.dma("sp", self.V_d[t0 + tt * 128:t0 + (tt + 1) * 128, s0:s0 + n_],
                          sg[:, 0:n_], sem, reads=[sgb])

                def epi_F(ci, ps, psb, t0=t0):
                    sg, sgb, sem = stf.next()
                    S.op("act", lambda e: e.activation(out=sg[:], in_=ps[:, 0:512], func=AF.Copy),
                         reads=[psb], writes=[sgb])
                    S.dma("sp", self.f_d[0:c.NH, t0:t0 + 512], sg[0:c.NH, :], sem, reads=[sgb])

                def epi_U(ci, ps, psb, t0=t0):
                    sg, sgb, sem = stb.next()
                    S.op("act", lambda e: e.activation(out=sg[:], in_=ps[:, 0:512], func=AF.Copy),
                         reads=[psb], writes=[sgb])
                    S.dma("sp", self.uT_d[ci * 128:(ci + 1) * 128, t0:t0 + 512], sg[:], sem,
                          reads=[sgb])

                def epi_Q(ci, ps, psb, to0=to0):
                    sg, sgb, sem = stb.next()
                    S.op("act", lambda e: e.activation(out=sg[:], in_=ps[:, 0:512], func=AF.Copy),
                         reads=[psb], writes=[sgb])
                    S.dma("sp", self.QT_d[ci, :, to0:to0 + 512], sg[:], sem, reads=[sgb])

                def epi_G(ci, ps, psb, to0=to0):
                    sg, sgb, sem = stb.next()
                    S.op("act", lambda e: e.activation(out=sg[:], in_=ps[:, 0:512],
                                                       func=AF.Sigmoid,
                                                       bias=self.bgate[:, ci:ci + 1]),
                         reads=[psb], writes=[sgb])
                    S.dma("sp", self.gT_d[ci * 128:(ci + 1) * 128, to0:to0 + 512], sg[:], sem,
                          reads=[sgb])

                self.items_fm(items, self.w_in, c.OFF_K, c.AW, KT, hT, hTb, 512, epi_K)
                self.items_tm(items, self.w_in, 0, KT, c.OFF_V, c.AW, hT, hTb, 4, epi_V)
                self.items_fm(items, self.w_in, c.OFF_F, 128, KT, hT, hTb, 512, epi_F)
                self.items_fm(items, self.w_in, c.OFF_U, c.SW, KT, hT, hTb, 512, epi_U)
                if own:
                    self.items_fm(items, self.w_in, 0, c.AW, KT, hT, hTb, 512, epi_Q)
                    self.items_fm(items, self.w_in, c.OFF_G, 2 * D, KT, hT, hTb, 512, epi_G)
                if nb_next is not None:
                    items.append((None, nb_next))
            self.run_items(items)
            S.barrier()

    def phase_S5(self):
        c, S = self.c, self.S
        L, NH, NST, CT, NKT = c.L, c.NH, c.NST, c.CT, c.NKT
        T = 256
        with ExitStack() as st:
            fl = self.sb(st, "fl", [NH, L], F32)
            fl2 = self.sb(st, "fl2", [NH, L], F32)
            flb = Buf("fl")
            fl2b = Buf("fl2")
            sem = self.getsem("fl")
            S.dma("sp", fl[:], self.f_d[:, :], sem, writes=[flb])
            S.op("act", lambda e: e.activation(out=fl[:], in_=fl[:], func=AF.Exp, scale=-1.0,
                                               bias=self.nbf[:, 0:1]),
                 reads=[flb], writes=[flb])
            S.op("act", lambda e: e.activation(out=fl[:], in_=fl[:], func=AF.Ln, bias=1.0),
                 reads=[flb], writes=[flb])
            CH = min(2048, L)
            for i in range(0, L, CH):
                init = 0.0 if i == 0 else fl2[:, i - 1:i]
                S.op("dve", lambda e, i=i, init=init: e.tensor_tensor_scan(
                    out=fl2[:, i:i + CH], data0=self.cst[0:NH, 1:2].to_broadcast([NH, CH]),
                    data1=fl[:, i:i + CH], initial=init, op0=ALU.mult, op1=ALU.subtract),
                    reads=[flb, fl2b], writes=[fl2b])
            per = 512 // NH
            for k0 in range(0, NKT, per):
                ps, psb, _ = self.pf.next()
                n = min(per, NKT - k0)
                for k in range(n):
                    S.op("pe", lambda e, k=k: e.transpose(
                        out=ps[:, k * NH:(k + 1) * NH], in_=fl2[:, (k0 + k) * 128:(k0 + k + 1) * 128],
                        identity=self.identf[0:NH, 0:NH]), reads=[fl2b], writes=[psb])
                S.op("act", lambda e, n=n: e.activation(
                    out=self.c_tm[:, k0:k0 + n, :],
                    in_=ps[:, 0:n * NH].rearrange("p (k h) -> p k h", h=NH), func=AF.Copy),
                    reads=[psb], writes=[self.c_tmb])
            S.barrier()

        with ExitStack() as st:
            P = self.sb(st, "s5P", [128, 34, NST], F32)
            Pb = Buf("s5P")
            BC = self.sb(st, "s5BC", [128, 2, NST, 16], F32)
            CC = self.sb(st, "s5CC", [128, 2, NST, 16], F32)
            BB = self.sb(st, "s5BB", [128, 2, NST, 16], F32)
            BCb = Buf("s5BC")
            Bm = self.sb(st, "Bm", [128, 2, NST, 128], BF16)
            Cm = self.sb(st, "Cm", [128, 2, NST, 128], BF16)
            Bmb, Cmb = Buf("Bm"), Buf("Cm")
            tabb = Buf("tab")
            sem = self.getsem("s5ld")
            S.dma("sp", P[:, 0:3, :], self.s5p_d.rearrange("p (a s) -> p a s", a=3), sem, writes=[Pb])
            S.dma("sp", BC[:], self.s5b_d[:, :, :, :], sem, writes=[BCb])
            S.dma("sp", CC[:], self.s5c_d[:, :, :, :], sem, writes=[BCb])
            Pb.w = (sem, sem.c)
            BCb.w = (sem, sem.c)

            def pv(i):
                return P[:, i, :]

            def dv(fn):
                S.op("dve", fn, reads=[Pb], writes=[Pb])

            def av(fn):
                S.op("act", fn, reads=[Pb], writes=[Pb])

            TT = lambda o, a, b, op: dv(lambda e: e.tensor_tensor(out=pv(o), in0=pv(a), in1=pv(b), op=op))
            TS = lambda o, a, s1, s2, op0, op1: dv(lambda e: e.tensor_scalar(
                out=pv(o), in0=pv(a), scalar1=s1, scalar2=s2, op0=op0, op1=op1))
            av(lambda e: e.activation(out=pv(3), in_=pv(2), func=AF.Exp))
            TT(4, 3, 0, ALU.mult)
            TT(5, 3, 1, ALU.mult)
            av(lambda e: e.activation(out=pv(6), in_=pv(4), func=AF.Exp))
            TS(7, 5, 0.0, None, ALU.add, ALU.bypass)
            TS(8, 5, PI / 2, None, ALU.add, ALU.bypass)
            for _ in range(4):
                for a in (7, 8):
                    TS(9, a, PI, 2 * PI, ALU.is_gt, ALU.mult)
                    TT(a, a, 9, ALU.subtract)
            av(lambda e: e.activation(out=pv(9), in_=pv(7), func=AF.Sin))
            av(lambda e: e.activation(out=pv(10), in_=pv(8), func=AF.Sin))
            TT(11, 6, 10, ALU.mult)
            TT(12, 6, 9, ALU.mult)
            TT(13, 0, 0, ALU.mult)
            TT(14, 1, 1, ALU.mult)
            TT(13, 13, 14, ALU.add)
            dv(lambda e: e.reciprocal(out=pv(14), in_=pv(13)))
            TS(15, 11, -1.0, None, ALU.add, ALU.bypass)
            TT(16, 15, 0, ALU.mult)
            TT(17, 12, 1, ALU.mult)
            TT(16, 16, 17, ALU.add)
            TT(16, 16, 14, ALU.mult)
            TT(17, 12, 0, ALU.mult)
            TT(18, 15, 1, ALU.mult)
            TT(17, 17, 18, ALU.subtract)
            TT(17, 17, 14, ALU.mult)
            fre = P[:, 16, :].unsqueeze(2).to_broadcast([128, NST, 16])
            fim = P[:, 17, :].unsqueeze(2).to_broadcast([128, NST, 16])

            def bop(o, a, b, op):
                S.op("dve", lambda e: e.tensor_tensor(out=o, in0=a, in1=b, op=op),
                     reads=[Pb, BCb], writes=[BCb])
            bop(BB[:, 0], BC[:, 0], fre, ALU.mult)
            bop(BB[:, 1], BC[:, 1], fim, ALU.mult)
            bop(BB[:, 0], BB[:, 0], BB[:, 1], ALU.subtract)
            bop(BB[:, 1], BC[:, 1], fre, ALU.mult)
            bop(BC[:, 1], BC[:, 0], fim, ALU.mult)
            bop(BB[:, 1], BB[:, 1], BC[:, 1], ALU.add)
            zb = self.sb(st, "zb", [128, 8, 128], BF16)
            zbb = Buf("zb")
            S.op("pool", lambda e: e.memset(zb[:], 0.0), writes=[zbb])
            S.op("pool", lambda e: e.memset(Cm[:], 0.0), writes=[Cmb])
            for s in range(NST):
                off = 32 * (s % 4)
                for ri in range(2):
                    z = zb[:, (s % 4) * 2 + ri, :]
                    S.op("act", lambda e: e.activation(out=z[0:64, off:off + 16],
                                                       in_=BB[0:64, ri, s, :], func=AF.Copy),
                         reads=[BCb], writes=[zbb])
                    S.op("act", lambda e: e.activation(out=z[64:128, off + 16:off + 32],
                                                       in_=BB[64:128, ri, s, :], func=AF.Copy),
                         reads=[BCb], writes=[zbb])
                    pb, pbb, _ = self.pbt.next()
                    S.op("pe", lambda e: e.transpose(out=pb[:, 0:128], in_=z, identity=self.identb[:]),
                         reads=[zbb], writes=[pbb])
                    S.op("dve", lambda e: e.tensor_copy(out=Bm[:, ri, s, :], in_=pb[:, 0:128]),
                         reads=[pbb], writes=[Bmb])
                    sc = 1.0 if ri == 0 else -1.0
                    S.op("act", lambda e: e.activation(out=Cm[0:64, ri, s, off:off + 16],
                                                       in_=CC[0:64, ri, s, :], func=AF.Copy, scale=sc),
                         reads=[BCb], writes=[Cmb])
                    S.op("act", lambda e: e.activation(out=Cm[64:128, ri, s, off + 16:off + 32],
                                                       in_=CC[64:128, ri, s, :], func=AF.Copy, scale=sc),
                         reads=[BCb], writes=[Cmb])
            def build_tables(p1_re, p1_im, reverse, writer):
                H = max(1, NST // 2)
                halves = [(0, H), (H, NST)] if NST > 1 else [(0, NST)]
                for (s0, s1) in halves:
                    n = s1 - s0
                    with ExitStack() as st2:
                        ER = self.sb(st2, "ER", [128, n, T], F32)
                        EI = self.sb(st2, "EI", [128, n, T], F32)
                        T1 = self.sb(st2, "T1", [128, n, T // 2], F32)
                        T2 = self.sb(st2, "T2", [128, n, T // 2], F32)
                        Eb = Buf("E")

                        def ev(fn):
                            S.op("dve", fn, reads=[Pb, Eb], writes=[Eb, Pb])
                        i0 = T - 1 if reverse else 0
                        ev(lambda e: e.memset(ER[:, :, i0:i0 + 1], 1.0))
                        ev(lambda e: e.memset(EI[:, :, i0:i0 + 1], 0.0))
                        TS(20, p1_re, 0.0, None, ALU.add, ALU.bypass)
                        TS(21, p1_im, 0.0, None, ALU.add, ALU.bypass)
                        k = 1
                        while k < T:
                            pr = P[:, 20, s0:s1].unsqueeze(2).to_broadcast([128, n, k])
                            pi_ = P[:, 21, s0:s1].unsqueeze(2).to_broadcast([128, n, k])
                            if reverse:
                                src, dst = slice(T - k, T), slice(T - 2 * k, T - k)
                            else:
                                src, dst = slice(0, k), slice(k, 2 * k)
                            a_r, a_i = ER[:, :, src], EI[:, :, src]
                            t1, t2 = T1[:, :, 0:k], T2[:, :, 0:k]
                            ev(lambda e: e.tensor_tensor(out=t1, in0=a_r, in1=pr, op=ALU.mult))
                            ev(lambda e: e.tensor_tensor(out=t2, in0=a_i, in1=pi_, op=ALU.mult))
                            ev(lambda e: e.tensor_tensor(out=ER[:, :, dst], in0=t1, in1=t2, op=ALU.subtract))
                            ev(lambda e: e.tensor_tensor(out=t1, in0=a_r, in1=pi_, op=ALU.mult))
                            ev(lambda e: e.tensor_tensor(out=t2, in0=a_i, in1=pr, op=ALU.mult))
                            ev(lambda e: e.tensor_tensor(out=EI[:, :, dst], in0=t1, in1=t2, op=ALU.add))
                            TT(22, 20, 20, ALU.mult)
                            TT(23, 21, 21, ALU.mult)
                            TT(24, 20, 21, ALU.mult)
                            TT(20, 22, 23, ALU.subtract)
                            TS(21, 24, 2.0, None, ALU.mult, ALU.bypass)
                            k *= 2
                        writer(ER, EI, s0, s1, Eb)
                        S.barrier()

            burn = Ring.__new__(Ring)
            burn.t, burn.b, burn.s, burn.i, burn.n = self.pf.t[0:4], self.pf.b[0:4], [None] * 4, 0, 4
            utsrc = self.uT_d.rearrange("(ct p) t -> p ct t", p=128)
            dv(lambda e: e.memset(P[:, 28:30, :], 0.0))
            NPRE = c.OWN0 // T
            with ExitStack() as sa:
                WA = self.sb(sa, "WA", [128, NST, 2 * T], BF16)
                WB = self.sb(sa, "WB", [128, NST, 2 * T], BF16)

                def wr_ab(ER, EI, s0, s1, Eb):
                    S.op("dve", lambda e: e.tensor_copy(out=WA[:, s0:s1, 0:T], in_=ER[:]), reads=[Eb], writes=[tabb])
                    S.op("act", lambda e: e.activation(out=WA[:, s0:s1, T:2 * T], in_=EI[:], func=AF.Copy,
                                                       scale=-1.0), reads=[Eb], writes=[tabb])
                    S.op("dve", lambda e: e.tensor_copy(out=WB[:, s0:s1, 0:T], in_=EI[:]), reads=[Eb], writes=[tabb])
                    S.op("act", lambda e: e.activation(out=WB[:, s0:s1, T:2 * T], in_=ER[:], func=AF.Copy),
                         reads=[Eb], writes=[tabb])
                if NPRE > 0:
                    build_tables(11, 12, True, wr_ab)
                    TS(30, 20, 0.0, None, ALU.add, ALU.bypass)
                    TS(31, 21, 0.0, None, ALU.add, ALU.bypass)
                    utr = Ring(self, sa, "utrA", 2, [128, CT, 1024], BF16)
                    junk = Ring(self, sa, "junk", 2, [128, 2 * T], BF16, dma=False)
                    locr = Ring(self, sa, "locr", 2, [128, 2, NST], F32, dma=False)
                    uts = {}
                    for chn in range(NPRE):
                        cg, cc = divmod(chn, 4)
                        if cg not in uts:
                            ut, utb, usem = utr.next()
                            S.dma("sp", ut[:], utsrc[:, :, cg * 1024:(cg + 1) * 1024], usem, writes=[utb])
                            uts[cg] = (ut, utb)
                        ut, utb = uts[cg]
                        ucols = slice(cc * T, (cc + 1) * T)
                        loc, locb, _ = locr.next()
                        for s in range(NST):
                            ct = s // 4
                            ps, psb, _ = burn.next()
                            S.op("pe", lambda e: e.matmul(ps[:, 0:T], lhsT=Bm[:, 0, s, :], rhs=ut[:, ct, ucols],
                                                          start=True, stop=True), reads=[Bmb, utb], writes=[psb])
                            S.op("pe", lambda e: e.matmul(ps[:, T:2 * T], lhsT=Bm[:, 1, s, :], rhs=ut[:, ct, ucols],
                                                          start=True, stop=True), reads=[Bmb, utb], writes=[psb])
                            for ri, W_ in enumerate((WA, WB)):
                                jk, jkb, _ = junk.next()
                                S.op("dve", lambda e, ri=ri, W_=W_, jk=jk: e.scalar_tensor_tensor(
                                    out=jk[:], in0=ps[:, 0:2 * T], scalar=1.0, in1=W_[:, s, :],
                                    op0=ALU.mult, op1=ALU.mult, accum_out=loc[:, ri, s:s + 1]),
                                    reads=[psb, tabb], writes=[jkb, locb])
                        TT(22, 30, 28, ALU.mult)
                        TT(23, 31, 29, ALU.mult)
                        TT(32, 30, 29, ALU.mult)
                        TT(33, 31, 28, ALU.mult)
                        TT(22, 22, 23, ALU.subtract)
                        TT(32, 32, 33, ALU.add)
                        S.op("dve", lambda e: e.tensor_tensor(out=pv(28), in0=pv(22), in1=loc[:, 0, :], op=ALU.add),
                             reads=[Pb, locb], writes=[Pb])
                        S.op("dve", lambda e: e.tensor_tensor(out=pv(29), in0=pv(32), in1=loc[:, 1, :], op=ALU.add),
                             reads=[Pb, locb], writes=[Pb])
                    S.barrier()
            cosT = self.sb(st, "cosT", [128, NST, T], BF16)
            sinT = self.sb(st, "sinT", [128, NST, T], BF16)

            def wr_cs(ER, EI, s0, s1, Eb):
                S.op("dve", lambda e: e.tensor_copy(out=cosT[:, s0:s1, :], in_=ER[:]), reads=[Eb], writes=[tabb])
                S.op("act", lambda e: e.activation(out=sinT[:, s0:s1, :], in_=EI[:], func=AF.Copy),
                     reads=[Eb], writes=[tabb])
            build_tables(10, 9, False, wr_cs)
            TT(22, 10, 28, ALU.mult)
            TT(23, 9, 29, ALU.mult)
            TT(25, 22, 23, ALU.subtract)
            TT(22, 10, 29, ALU.mult)
            TT(23, 9, 28, ALU.mult)
            TT(26, 22, 23, ALU.add)

            wg = self.sb(st, "wg", [128, CT, c.SW], BF16)
            wgb = Buf("wg")
            semw = self.getsem("wgld")
            for ct in range(CT):
                S.dma("pool", wg[:, ct, :], self.w_glu[ct * 128:(ct + 1) * 128, :], semw, writes=[wgb])
            wgb.w = (semw, semw.c)
            zl = self.sb(st, "zl", [128, 2, NST], F32)
            zlb = Buf("zl")
            zib = Buf("zi")
            utr = Ring(self, st, "utr", 2, [128, CT, 1024], BF16)
            wk1 = Ring(self, st, "wk1", 3, [128, 512], F32, dma=False)
            wk2 = Ring(self, st, "wk2", 3, [128, 512], F32, dma=False)
            wwr = Ring(self, st, "wwr", 2, [128, 512], F32, dma=False)
            zr = Ring(self, st, "zr", 3, [128, 512], F32, dma=False)
            q1r = Ring(self, st, "q1r", 2, [128, 512], BF16, dma=False)
            q2r = Ring(self, st, "q2r", 2, [128, 512], BF16, dma=False)
            zhr = Ring(self, st, "zhr", 3, [128, 512], BF16, dma=False)
            xr = Ring(self, st, "xr", 3, [128, 512], BF16, dma=False)
            ygf = Ring(self, st, "ygf", 2, [128, CT, T], F32, dma=False)
            ygb = Ring(self, st, "ygb", 2, [128, CT, T], BF16, dma=False)
            tmp = Ring(self, st, "s5tmp", 3, [128, T], F32, dma=False)
            y2s = Ring(self, st, "y2s", 3, [128, T], BF16)
            ybank = [(self.pf.t[4], self.pf.b[4]), (self.pf.t[5], self.pf.b[5])]
            tiles = [(chn // 4, chn % 4, s) for chn in range(NPRE, L // T) for s in range(NST)]
            uts, st0, st1, st2, st3, ych = {}, {}, {}, {}, {}, {}
            yidx = [0]

            def stage1(i):
                cg, cc, s = tiles[i]
                if cg not in uts:
                    ut, utb, usem = utr.next()
                    S.dma("sp", ut[:], utsrc[:, :, cg * 1024:(cg + 1) * 1024], usem, writes=[utb])
                    uts[cg] = (ut, utb)
                ut, utb = uts[cg]
                ct = s // 4
                ucols = slice(cc * T, (cc + 1) * T)
                ps, psb, _ = burn.next()
                S.op("pe", lambda e: e.matmul(ps[:, 0:T], lhsT=Bm[:, 0, s, :], rhs=ut[:, ct, ucols],
                                              start=True, stop=True), reads=[Bmb, utb], writes=[psb])
                S.op("pe", lambda e: e.matmul(ps[:, T:2 * T], lhsT=Bm[:, 1, s, :], rhs=ut[:, ct, ucols],
                                              start=True, stop=True), reads=[Bmb, utb], writes=[psb])
                st0[i] = (ps, psb)

            def stage1b(i):
                cg, cc, s = tiles[i]
                ps, psb = st0.pop(i)
                t13, t13b, _ = wk1.next()
                t42, t42b, _ = wk2.next()
                ps3 = ps[:, :].rearrange("p (two t) -> p two t", two=2)
                cosb = cosT[:, s:s + 1, :].to_broadcast([128, 2, T])
                sinb = sinT[:, s:s + 1, :].to_broadcast([128, 2, T])
                S.op("dve", lambda e: e.tensor_tensor(
                    out=t13[:, :].rearrange("p (two t) -> p two t", two=2), in0=ps3, in1=cosb,
                    op=ALU.mult), reads=[psb, tabb], writes=[t13b])
                S.op("dve", lambda e: e.tensor_tensor(
                    out=t42[:, :].rearrange("p (two t) -> p two t", two=2), in0=ps3, in1=sinb,
                    op=ALU.mult), reads=[psb, tabb], writes=[t42b])
                st1[i] = (t13, t13b, t42, t42b, cosb, sinb)

            def stage2(i):
                cg, cc, s = tiles[i]
                ut, utb = uts[cg]
                ct = s // 4
                ucols = slice(cc * T, (cc + 1) * T)
                tl0 = cg * 1024 + cc * T
                own = tl0 >= c.OWN0
                chn = cg * 4 + cc
                if own and s == 0:
                    yf, yfb, _ = ygf.next()
                    yb, ybb, _ = ygb.next()
                    ych[chn] = (yf, yfb, yb, ybb)
                t13, t13b, t42, t42b, cosb, sinb = st1.pop(i)
                ww, wwb, _ = wwr.next()
                z, zb_, _ = zr.next()
                S.op("dve", lambda e: e.scalar_tensor_tensor(out=ww[:, 0:T], in0=t13[:, 0:T], scalar=1.0,
                                                             in1=t42[:, T:2 * T], op0=ALU.mult, op1=ALU.add),
                     reads=[t13b, t42b], writes=[wwb])
                S.op("dve", lambda e: e.scalar_tensor_tensor(out=ww[:, T:2 * T], in0=t13[:, T:2 * T], scalar=1.0,
                                                             in1=t42[:, 0:T], op0=ALU.mult, op1=ALU.subtract),
                     reads=[t13b, t42b], writes=[wwb])
                rb = P[:, 6, s:s + 1].to_broadcast([128, T])
                for ri in range(2):
                    S.op("dve", lambda e, ri=ri: e.tensor_tensor_scan(
                        out=z[:, ri * T:(ri + 1) * T], data0=rb, data1=ww[:, ri * T:(ri + 1) * T],
                        initial=P[:, 25 + ri, s:s + 1], op0=ALU.mult, op1=ALU.add),
                        reads=[wwb, Pb, zib], writes=[zb_])
                    S.op("act", lambda e, ri=ri: e.activation(
                        out=zl[:, ri, s:s + 1], in_=z[:, (ri + 1) * T - 1:(ri + 1) * T], func=AF.Copy),
                        reads=[zb_], writes=[zlb])
                zh, zhb, _ = zhr.next()
                S.op("act", lambda e: e.activation(out=zh[:], in_=z[:], func=AF.Copy),
                     reads=[zb_], writes=[zhb])
                st2[i] = (zh, zhb, cosb, sinb)
                if s == NST - 1:
                    def cv(fn):
                        S.op("dve", fn, reads=[Pb, zlb, zib], writes=[Pb])
                    cv(lambda e: e.tensor_tensor(out=pv(22), in0=pv(20), in1=zl[:, 0, :], op=ALU.mult))
                    cv(lambda e: e.tensor_tensor(out=pv(23), in0=pv(21), in1=zl[:, 1, :], op=ALU.mult))
                    cv(lambda e: e.tensor_tensor(out=pv(24), in0=pv(20), in1=zl[:, 1, :], op=ALU.mult))
                    cv(lambda e: e.tensor_tensor(out=pv(27), in0=pv(21), in1=zl[:, 0, :], op=ALU.mult))
                    S.op("dve", lambda e: e.tensor_tensor(out=pv(25), in0=pv(22), in1=pv(23), op=ALU.subtract),
                         reads=[Pb], writes=[Pb, zib])
                    S.op("dve", lambda e: e.tensor_tensor(out=pv(26), in0=pv(24), in1=pv(27), op=ALU.add),
                         reads=[Pb], writes=[Pb, zib])

            def stage3(i):
                cg, cc, s = tiles[i]
                ut, utb = uts[cg]
                ct = s // 4
                ucols = slice(cc * T, (cc + 1) * T)
                tl0 = cg * 1024 + cc * T
                own = True
                chn = cg * 4 + cc
                zh, zhb, cosb, sinb = st2.pop(i)
                zb_ = zhb
                if own:
                    yf, yfb, yb, ybb = ych[chn]
                    q1, q1b, _ = q1r.next()
                    q2, q2b, _ = q2r.next()
                    x, xb_, _ = xr.next()
                    z3 = zh[:, :].rearrange("p (two t) -> p two t", two=2)
                    S.op("pool", lambda e: e.tensor_tensor(
                        out=q1[:, :].rearrange("p (two t) -> p two t", two=2), in0=z3, in1=cosb,
                        op=ALU.mult), reads=[zb_, tabb], writes=[q1b])
                    S.op("pool", lambda e: e.tensor_tensor(
                        out=q2[:, :].rearrange("p (two t) -> p two t", two=2), in0=z3, in1=sinb,
                        op=ALU.mult), reads=[zb_, tabb], writes=[q2b])
                    S.op("pool", lambda e: e.tensor_tensor(out=x[:, 0:T], in0=q1[:, 0:T],
                                                           in1=q2[:, T:2 * T], op=ALU.subtract),
                         reads=[q1b, q2b], writes=[xb_])
                    S.op("pool", lambda e: e.tensor_tensor(out=x[:, T:2 * T], in0=q2[:, 0:T],
                                                           in1=q1[:, T:2 * T], op=ALU.add),
                         reads=[q1b, q2b], writes=[xb_])
                    st3[i] = (x, xb_)

            def stage4(i):
                cg, cc, s = tiles[i]
                ut, utb = uts[cg]
                ct = s // 4
                ucols = slice(cc * T, (cc + 1) * T)
                tl0 = cg * 1024 + cc * T
                own = True
                chn = cg * 4 + cc
                x, xb_ = st3.pop(i)
                if own:
                    yf, yfb, yb, ybb = ych[chn]
                    yps, ypsb = ybank[yidx[0] % 2]
                    S.op("pe", lambda e: e.matmul(yps[:, 0:T], lhsT=Cm[:, 0, s, :], rhs=x[:, 0:T],
                                                  start=(s % 4 == 0), stop=False),
                         reads=[Cmb, xb_], writes=[ypsb])
                    S.op("pe", lambda e: e.matmul(yps[:, 0:T], lhsT=Cm[:, 1, s, :], rhs=x[:, T:2 * T],
                                                  start=False, stop=(s % 4 == 3)),
                         reads=[Cmb, xb_], writes=[ypsb])
                    if s % 4 == 3:
                        yidx[0] += 1
                        yv = yf[:, ct, :]
                        t1, t1b, _ = tmp.next()
                        S.op("dve", lambda e: e.scalar_tensor_tensor(
                            out=yv, in0=ut[:, ct, ucols], scalar=self.dskip[:, ct:ct + 1],
                            in1=yps[:, 0:T], op0=ALU.mult, op1=ALU.add),
                            reads=[utb, ypsb], writes=[yfb])
                        S.op("pool", lambda e: e.tensor_tensor(out=t1[:], in0=yv, in1=yv, op=ALU.mult),
                             reads=[yfb], writes=[t1b])
                        S.op("act", lambda e: e.activation(out=t1[:], in_=t1[:], func=AF.Copy,
                                                           scale=0.044715, bias=1.0),
                             reads=[t1b], writes=[t1b])
                        S.op("pool", lambda e: e.tensor_tensor(out=t1[:], in0=t1[:], in1=yv, op=ALU.mult),
                             reads=[t1b, yfb], writes=[t1b])
                        S.op("act", lambda e: e.activation(out=t1[:], in_=t1[:], func=AF.Sigmoid,
                                                           scale=1.5957691216),
                             reads=[t1b], writes=[t1b])
                        S.op("pool", lambda e: e.tensor_tensor(out=yv, in0=yv, in1=t1[:], op=ALU.mult),
                             reads=[t1b, yfb], writes=[yfb])
                        S.op("act", lambda e: e.activation(out=yb[:, ct, :], in_=yv, func=AF.Copy),
                             reads=[yfb], writes=[ybb])
                if s == NST - 1:
                    if own:
                        yf, yfb, yb, ybb = ych.pop(chn)
                        to0 = tl0 - c.OWN0
                        for co in range(CT):
                            ps, psb = glu_ps, self.pbt.b[0]
                            for ct2 in range(CT):
                                S.op("pe", lambda e, ct2=ct2: e.matmul(
                                    ps[:, 0:T], lhsT=wg[:, ct2, co * 128:(co + 1) * 128], rhs=yb[:, ct2, :],
                                    start=(ct2 == 0), stop=(ct2 == CT - 1)),
                                    reads=[wgb, ybb], writes=[psb])
                            t1, t1b, _ = tmp.next()
                            S.op("act", lambda e: e.activation(out=t1[:], in_=ps[:, 0:T], func=AF.Sigmoid,
                                                               bias=self.bglu[:, co:co + 1]),
                                 reads=[psb], writes=[t1b])
                            sg, sgb, ssem = y2s.next()
                            S.op("pool", lambda e: e.tensor_tensor(out=sg[:], in0=yf[:, co, :], in1=t1[:],
                                                                   op=ALU.mult),
                                 reads=[yfb, t1b], writes=[sgb])
                            S.dma("sp", self.y2T_d[co * 128:(co + 1) * 128, to0:to0 + T], sg[:], ssem,
                                  reads=[sgb])

            glu_ps = self.pbt.t[0][:].bitcast(F32)
            NT_ = len(tiles)
            for i in range(-2, NT_ + 2):
                if 0 <= i + 2 < NT_:
                    stage1(i + 2)
                if 0 <= i + 1 < NT_:
                    stage1b(i + 1)
                if 0 <= i < NT_:
                    stage2(i)
                if 0 <= i - 1 < NT_:
                    stage3(i - 1)
                if 0 <= i - 2 < NT_:
                    stage4(i - 2)
            S.barrier()

    def phase_B(self):
        c, S = self.c, self.S
        L, NH, NKT, OWN = c.L, c.NH, c.NKT, c.OWN
        NQT = OWN // 128
        NSB = OWN // 512
        NB = NKT // 4
        K0 = c.OWN0 // 128
        scale = 128 ** -0.5
        with ExitStack() as st:
            ktr = Ring(self, st, "ktr", 2, [128, L], BF16)
            vr = Ring(self, st, "vr", 2, [128, NKT, 129], BF16)
            qr = Ring(self, st, "qr", 2, [128, OWN], BF16)
            ptr_ = Ring(self, st, "ptr", 12, [128, 512], BF16, dma=False)
            accs = []
            for i in range(2):
                t = self.sb(st, f"acc{i}", [128, 4, 129], F32)
                accs.append((t, [Buf(f"acc{i}_{q}") for q in range(4)]))
            tabs = []
            for i in range(2):
                tabs.append(dict(
                    rE=self.sb(st, f"rE{i}", [128, NKT], F32), rM=self.sb(st, f"rM{i}", [128, NKT], F32),
                    bO=self.sb(st, f"bO{i}", [128, NKT], F32), bD=self.sb(st, f"bD{i}", [128, NKT], F32),
                    fO=self.sb(st, f"fO{i}", [128, NQT, NB], F32), fD=self.sb(st, f"fD{i}", [128, NQT, 4], F32),
                    b=Buf(f"btab{i}")))
            obr = Ring(self, st, "obr", 4, [128, 128], BF16, dma=False)
            aos = Ring(self, st, "aos", 2, [128, 512], BF16)
            recr = Ring(self, st, "recr", 4, [128, 1], F32, dma=False)
            for i in range(2):
                S.op("pool", lambda e, i=i: e.memset(vr.t[i][:, :, 128:129], 1.0), writes=[vr.b[i]])
            stb = Ring.__new__(Ring)
            st4 = self.pbt.t[1][:].bitcast(F32)
            stb.t = [self.pf.t[0][:, :], self.pf.t[1][:, :], self.pf.t[2][:, :], st4]
            stb.b, stb.s, stb.i, stb.n = self.pf.b[0:3] + [self.pbt.b[1]], [None] * 4, 0, 4
            poslots = [(self.pf.t[3], 0, self.pf.b[3]), (self.pf.t[4], 0, self.pf.b[4]),
                       (self.pf.t[5], 0, self.pf.b[5])]
            poi = [0]

            def next_po():
                t, o, b = poslots[poi[0] % len(poslots)]
                poi[0] += 1
                return t[:, o:o + 129], b

            steps = []
            for h in range(NH):
                for sb_ in range(NSB):
                    nboff = (c.OWN0 + sb_ * 512) // 512
                    for B in range(nboff):
                        steps.append((h, sb_, B, False, B == 0, False))
                    steps.append((h, sb_, nboff, True, nboff == 0, True))
            heads, fr = {}, {}

            def load_head(h):
                kt_, ktb, ks = ktr.next()
                v_, vb, vs = vr.next()
                q_, qb, qs = qr.next()
                S.dma("sp", kt_[:], self.KT_d[h, :, :], ks, writes=[ktb])
                vsrc = self.V_d.rearrange("(kt p) n -> p kt n", p=128)
                step = max(1, NKT // 4)
                for a in range(0, NKT, step):
                    S.dma("sp", v_[:, a:a + step, 0:128], vsrc[:, a:a + step, h * 128:(h + 1) * 128], vs,
                          writes=[vb])
                S.dma("sp", q_[:], self.QT_d[h, :, :], qs, writes=[qb])
                heads[h] = (kt_, ktb, v_, vb, q_, qb)

            def prologue(h):
                T_ = tabs[h % 2]
                tb = T_["b"]
                rE, rM, bO, bD, fO, fD = T_["rE"], T_["rM"], T_["bO"], T_["bD"], T_["fO"], T_["fD"]
                ch = self.c_tm[:, :, h]
                ps, psb, _ = stb.next()
                S.op("pe", lambda e: e.matmul(ps[:, 0:NKT], lhsT=self.sel127[:], rhs=ch, start=True, stop=True),
                     reads=[self.c_tmb], writes=[psb])
                S.op("pe", lambda e: e.matmul(ps[:, NKT:2 * NKT], lhsT=self.sel63[:], rhs=ch, start=True, stop=True),
                     reads=[self.c_tmb], writes=[psb])
                S.op("act", lambda e: e.activation(out=rE[:], in_=ps[:, 0:NKT], func=AF.Copy),
                     reads=[psb], writes=[tb])
                S.op("act", lambda e: e.activation(out=rM[:], in_=ps[:, NKT:2 * NKT], func=AF.Copy),
                     reads=[psb], writes=[tb])
                rE4 = rE[:, :].rearrange("p (b f) -> p b f", f=4)
                S.op("dve", lambda e: e.tensor_tensor(
                    out=bO[:, :].rearrange("p (b f) -> p b f", f=4),
                    in0=rE4[:, :, 3:4].to_broadcast([128, NB, 4]),
                    in1=ch.rearrange("p (b f) -> p b f", f=4), op=ALU.subtract),
                    reads=[tb, self.c_tmb], writes=[tb])
                S.op("dve", lambda e: e.tensor_tensor(out=bO[:], in0=bO[:], in1=self.kbias[:], op=ALU.add),
                     reads=[tb], writes=[tb])
                S.op("dve", lambda e: e.tensor_tensor(out=bD[:], in0=rM[:], in1=ch, op=ALU.subtract),
                     reads=[tb, self.c_tmb], writes=[tb])
                S.op("dve", lambda e: e.tensor_tensor(out=bD[:], in0=bD[:], in1=self.kbias[:], op=ALU.add),
                     reads=[tb], writes=[tb])
                for qt in range(NQT):
                    cq = self.c_tm[:, K0 + qt, h:h + 1]
                    S.op("act", lambda e, qt=qt, cq=cq: e.activation(
                        out=fO[:, qt, :], in_=rE4[:, :, 3], func=AF.Exp, scale=-1.0, bias=cq),
                        reads=[tb, self.c_tmb], writes=[tb])
                    kd = K0 + 4 * (qt // 4)
                    S.op("act", lambda e, qt=qt, cq=cq, kd=kd: e.activation(
                        out=fD[:, qt, :], in_=rM[:, kd:kd + 4], func=AF.Exp, scale=-1.0, bias=cq),
                        reads=[tb, self.c_tmb], writes=[tb])

            def front(i):
                h, sb_, B, diag, first, last = steps[i]
                if sb_ == 0 and B == 0:
                    if h == 0:
                        load_head(0)
                    prologue(h)
                kt_, ktb, v_, vb, q_, qb = heads[h]
                T_ = tabs[h % 2]
                btab = T_["bD"] if diag else T_["bO"]
                qcols = slice(sb_ * 512, (sb_ + 1) * 512)
                pts = []
                for j in range(4):
                    kt = 4 * B + j
                    ps, psb, _ = stb.next()
                    S.op("pe", lambda e, kt=kt: e.matmul(ps[:, 0:512], lhsT=kt_[:, kt * 128:(kt + 1) * 128],
                                                         rhs=q_[:, qcols], start=True, stop=True),
                         reads=[ktb, qb], writes=[psb])
                    pt, ptb, _ = ptr_.next()
                    S.op("act", lambda e, kt=kt: e.activation(out=pt[:], in_=ps[:, 0:512], func=AF.Exp,
                                                              scale=scale, bias=btab[:, kt:kt + 1]),
                         reads=[psb, T_["b"]], writes=[ptb])
                    if diag:
                        S.op("pool", lambda e, j=j: e.tensor_tensor(
                            out=pt[:, j * 128:(j + 1) * 128], in0=pt[:, j * 128:(j + 1) * 128],
                            in1=self.trib[:], op=ALU.mult), reads=[ptb], writes=[ptb])
                    pts.append((pt, ptb))
                fr[i] = pts

            def back(i):
                h, sb_, B, diag, first, last = steps[i]
                if sb_ == 0 and B == 0 and h + 1 < NH:
                    load_head(h + 1)
                kt_, ktb, v_, vb, q_, qb = heads[h]
                T_ = tabs[h % 2]
                tb = T_["b"]
                acc, accb = accs[(h * NSB + sb_) % 2]
                if first:
                    S.op("pool", lambda e: e.memset(acc[:], 0.0), writes=accb)
                pts = fr.pop(i)
                if not diag:
                    for qt in range(4):
                        po, pob = next_po()
                        for j in range(4):
                            pt, ptb = pts[j]
                            S.op("pe", lambda e, pt=pt, j=j, qt=qt: e.matmul(
                                po, lhsT=pt[:, qt * 128:(qt + 1) * 128], rhs=v_[:, 4 * B + j, :],
                                start=(j == 0), stop=(j == 3)), reads=[ptb, vb], writes=[pob])
                        qg = sb_ * 4 + qt
                        S.op("dve", lambda e, qt=qt, qg=qg: e.scalar_tensor_tensor(
                            out=acc[:, qt, :], in0=po, scalar=T_["fO"][:, qg, B:B + 1],
                            in1=acc[:, qt, :], op0=ALU.mult, op1=ALU.add),
                            reads=[pob, tb, accb[qt]], writes=[accb[qt]])
                else:
                    for j in range(4):
                        pt, ptb = pts[j]
                        kt = 4 * B + j
                        for qt in range(j, 4):
                            po, pob = next_po()
                            S.op("pe", lambda e, qt=qt, pt=pt, kt=kt: e.matmul(
                                po, lhsT=pt[:, qt * 128:(qt + 1) * 128], rhs=v_[:, kt, :],
                                start=True, stop=True), reads=[ptb, vb], writes=[pob])
                            qg = sb_ * 4 + qt
                            S.op("dve", lambda e, qt=qt, qg=qg, j=j: e.scalar_tensor_tensor(
                                out=acc[:, qt, :], in0=po, scalar=T_["fD"][:, qg, j:j + 1],
                                in1=acc[:, qt, :], op0=ALU.mult, op1=ALU.add),
                                reads=[pob, tb, accb[qt]], writes=[accb[qt]])
                if last:
                    pb, pbb = self.pbt.t[0], self.pbt.b[0]
                    for qt in range(4):
                        rc, rcb, _ = recr.next()
                        ob, obb, _ = obr.next()
                        S.op("dve", lambda e, qt=qt: e.reciprocal(out=rc[:], in_=acc[:, qt, 128:129]),
                             reads=[accb[qt]], writes=[rcb])
                        S.op("dve", lambda e, qt=qt: e.tensor_scalar(
                            out=ob[:], in0=acc[:, qt, 0:128], scalar1=rc[:, 0:1], scalar2=None, op0=ALU.mult),
                            reads=[accb[qt], rcb], writes=[obb])
                        S.op("pe", lambda e, qt=qt: e.transpose(out=pb[:, qt * 128:(qt + 1) * 128], in_=ob[:],
                                                                identity=self.identb[:]),
                             reads=[obb], writes=[pbb])
                    ao, aob, asem = aos.next()
                    S.op("act", lambda e: e.activation(out=ao[:], in_=pb[:, 0:512], func=AF.Copy),
                         reads=[pbb], writes=[aob])
                    S.dma("sp", self.aoT_d[h * 128:(h + 1) * 128, sb_ * 512:(sb_ + 1) * 512], ao[:], asem,
                          reads=[aob])

            front(0)
            for i in range(len(steps)):
                if i + 1 < len(steps):
                    front(i + 1)
                back(i)
            S.barrier()

    def phase_blocks(self):
        c, S = self.c, self.S
        D, KT, CT, XT, XW = c.D, c.KT, c.CT, c.XT, c.XW
        MT = c.MEM // 128
        with ExitStack() as st:
            self.slabs = Ring(self, st, "slabB", 3, [128, 32, 256], BF16)
            xcur = self.sb(st, "xcur", [128, 4, D], F32)
            xb = [Buf(f"xcur{i}") for i in range(4)]
            xsems = [self.getsem(f"xcur{i}") for i in range(4)]
            actA = self.sb(st, "actA", [128, KT, 512], BF16)
            actAb = Buf("actA")
            actB = self.sb(st, "actB", [128, 16, 512], BF16)
            actBb = Buf("actB")
            HBN = max(D, CT * 512, 4 * XW)
            hb = self.sb(st, "hbB", [128, HBN], BF16)
            hbb = Buf("hbB")
            hsem = self.getsem("hbB")
            asem = self.getsem("actBld")
            gts = Ring(self, st, "gts", 3, [128, 2, 512], BF16)
            tmp = Ring(self, st, "btmp", 4, [128, 512], F32, dma=False)
            ptx = Ring(self, st, "ptx", 4, [128, 512], BF16, dma=False)
            recr = Ring(self, st, "recx", 4, [128, 1], F32, dma=False)
            kxT = self.sb(st, "kxT", [128, XT, c.MEM], BF16)
            kxb = Buf("kxT")
            vx = self.sb(st, "vx", [128, MT, c.XH, 257], BF16)
            vxb = Buf("vx")
            gsem = self.getsem("gfin")
            actAf = actA[:].rearrange("p a b -> p (a b)").bitcast(F32)

            items = []
            xst_done = []

            def mem_norm(tile, buf):
                def get_src(tt):
                    S.dma("sp", xcur[:, tt, :], self.meml[tt * 128:(tt + 1) * 128, :], xsems[tt], writes=[xb[tt]])
                    return xcur[:, tt, :], xb[tt]
                self.norm_T(get_src, MT, 2 * KT, hb, hbb, actA, actAb)
                S.op("pool", lambda e: e.memset(vx[:, :, :, 256:257], 1.0), writes=[vxb])
            items.append((None, mem_norm))

            def epi_kx(ci, ps, psb):
                S.op("act", lambda e: e.activation(out=kxT[:, ci, :], in_=ps[:, 0:c.MEM], func=AF.Copy),
                     reads=[psb], writes=[kxb])

            def epi_vx(tt, s0, n_, ps, psb):
                hx = s0 // 256
                S.op("act", lambda e: e.activation(out=vx[:, tt, hx, 0:256], in_=ps[:, 0:n_], func=AF.Copy),
                     reads=[psb], writes=[vxb])
            self.items_fm(items, self.wk_x, 0, XW, KT, actA, actAb, c.MEM, epi_kx)
            self.items_tm(items, self.wv_x, 0, KT, 0, XW, actA, actAb, MT, epi_vx)

            for ob in range(c.OWN // 512):
                to0 = ob * 512
                tl0 = c.OWN0 + to0
                def c_load(tile, buf, to0=to0, tl0=tl0):
                    S.dma("sp", actB[:, 0:c.AW // 128, :],
                          self.aoT_d.rearrange("(kt p) t -> p kt t", p=128)[:, :, to0:to0 + 512], asem,
                          writes=[actBb])
                    S.dma("sp", hb[:, 0:CT * 512].rearrange("p (k t) -> p k t", k=CT),
                          self.y2T_d.rearrange("(kt p) t -> p kt t", p=128)[:, :, to0:to0 + 512], hsem,
                          writes=[hbb])
                    for tt in range(4):
                        S.dma("sp", xcur[:, tt, :], self.xl[tl0 + tt * 128:tl0 + (tt + 1) * 128, :], xsems[tt],
                              writes=[xb[tt]])
                items.append((None, c_load))
                y2v = hb[:, 0:CT * 512].rearrange("p (k t) -> p k t", k=CT)
                NA = c.AW // 128
                for s0 in range(0, D, 256):
                    st_ = {}

                    def fnA(tile, buf, st_=st_, s0=s0, to0=to0):
                        for ci in range(2):
                            j = s0 // 128 + ci
                            g, gb, gs = gts.next()
                            S.dma("sp", g[:, 0, :], self.gT_d[j * 128:(j + 1) * 128, to0:to0 + 512], gs, writes=[gb])
                            S.dma("sp", g[:, 1, :], self.gT_d[D + j * 128:D + (j + 1) * 128, to0:to0 + 512], gs,
                                  writes=[gb])
                            ps, psb, _ = self.pf.next()
                            for kt in range(NA):
                                S.op("pe", lambda e, kt=kt, ci=ci: e.matmul(
                                    ps[:, 0:512], lhsT=tile[:, kt, ci * 128:(ci + 1) * 128], rhs=actB[:, kt, :],
                                    start=(kt == 0), stop=(kt == NA - 1)), reads=[buf, actBb], writes=[psb])
                            t1, t1b, _ = tmp.next()
                            S.op("dve", lambda e: e.tensor_tensor(out=t1[:], in0=ps[:, 0:512], in1=g[:, 0, :],
                                                                  op=ALU.mult), reads=[psb, gb], writes=[t1b])
                            st_[ci] = (t1, t1b, g, gb)

                    def fnS(tile, buf, st_=st_, s0=s0):
                        for ci in range(2):
                            j = s0 // 128 + ci
                            t1, t1b, g, gb = st_[ci]
                            ps, psb, _ = self.pf.next()
                            for kt in range(CT):
                                S.op("pe", lambda e, kt=kt, ci=ci: e.matmul(
                                    ps[:, 0:512], lhsT=tile[:, kt, ci * 128:(ci + 1) * 128], rhs=y2v[:, kt, :],
                                    start=(kt == 0), stop=(kt == CT - 1)), reads=[buf, hbb], writes=[psb])
                            t2, t2b, _ = tmp.next()
                            S.op("dve", lambda e: e.tensor_tensor(out=t2[:], in0=ps[:, 0:512], in1=g[:, 1, :],
                                                                  op=ALU.mult), reads=[psb, gb], writes=[t2b])
                            S.op("pool", lambda e, j=j: e.tensor_tensor(out=actA[:, j, :], in0=t1[:], in1=t2[:],
                                                                        op=ALU.add),
                                 reads=[t1b, t2b], writes=[actAb])

                    items.append(((self.w_attn_up, 0, NA, s0, 256), fnA))
                    items.append(((self.w_ssm_up, 0, CT, s0, 256), fnS))

                def epi_res(tt, s0, n_, ps, psb):
                    S.op("dve", lambda e: e.tensor_tensor(out=xcur[:, tt, s0:s0 + n_], in0=ps[:, 0:n_],
                                                          in1=xcur[:, tt, s0:s0 + n_], op=ALU.add),
                         reads=[psb, xb[tt]], writes=[xb[tt]])
                self.items_tm(items, self.w_out, 0, KT, 0, D, actA, actAb, 4, epi_res)

                def d_norm(tile, buf):
                    self.norm_T(lambda tt: (xcur[:, tt, :], xb[tt]), 4, KT, hb, hbb, actA, actAb)
                items.append((None, d_norm))

                def epi_qx(ci, ps, psb):
                    S.op("act", lambda e: e.activation(out=actB[:, ci, :], in_=ps[:, 0:512], func=AF.Copy),
                         reads=[psb], writes=[actBb])
                self.items_fm(items, self.wq_x, 0, XW, KT, actA, actAb, 512, epi_qx)
                oxv = hb[:, 0:4 * XW].rearrange("p (t w) -> p t w", t=4)

                def d_attn(tile, buf):
                    for hx in range(c.XH):
                        pts = []
                        for mt in range(MT):
                            ps, psb, _ = self.pf.next()
                            for dt_ in range(2):
                                S.op("pe", lambda e, dt_=dt_, mt=mt: e.matmul(
                                    ps[:, 0:512], lhsT=kxT[:, hx * 2 + dt_, mt * 128:(mt + 1) * 128],
                                    rhs=actB[:, hx * 2 + dt_, :], start=(dt_ == 0), stop=(dt_ == 1)),
                                    reads=[kxb, actBb], writes=[psb])
                            pt, ptb, _ = ptx.next()
                            S.op("act", lambda e: e.activation(out=pt[:], in_=ps[:, 0:512], func=AF.Exp,
                                                               scale=1.0 / 16.0), reads=[psb], writes=[ptb])
                            pts.append((pt, ptb))
                        for qt in range(4):
                            po, pob, _ = self.pf.next()
                            for mt in range(MT):
                                pt, ptb = pts[mt]
                                S.op("pe", lambda e, mt=mt, pt=pt, qt=qt: e.matmul(
                                    po[:, 0:257], lhsT=pt[:, qt * 128:(qt + 1) * 128], rhs=vx[:, mt, hx, :],
                                    start=(mt == 0), stop=(mt == MT - 1)), reads=[ptb, vxb], writes=[pob])
                            rc, rcb, _ = recr.next()
                            S.op("dve", lambda e: e.reciprocal(out=rc[:], in_=po[:, 256:257]),
                                 reads=[pob], writes=[rcb])
                            S.op("dve", lambda e, qt=qt: e.tensor_scalar(
                                out=oxv[:, qt, hx * 256:(hx + 1) * 256], in0=po[:, 0:256], scalar1=rc[:, 0:1],
                                scalar2=None, op0=ALU.mult), reads=[pob, rcb], writes=[hbb])
                    for qt in range(4):
                        pb, pbb, _ = self.pbt.next()
                        for j in range(XT):
                            S.op("pe", lambda e, j=j, qt=qt: e.transpose(
                                out=pb[:, j * 128:(j + 1) * 128], in_=oxv[:, qt, j * 128:(j + 1) * 128],
                                identity=self.identb[:]), reads=[hbb], writes=[pbb])
                        S.op("act", lambda e, qt=qt: e.activation(
                            out=actB[:, 8:8 + XT, qt * 128:(qt + 1) * 128],
                            in_=pb[:, 0:XT * 128].rearrange("p (k t) -> p k t", k=XT), func=AF.Copy),
                            reads=[pbb], writes=[actBb])
                items.append((None, d_attn))
                self.items_tm(items, self.wo_x, 0, XT, 0, D, actB[:, 8:8 + XT, :], actBb, 4, epi_res)

                def e_norm(tile, buf):
                    self.norm_T(lambda tt: (xcur[:, tt, :], xb[tt]), 4, 3 * KT, hb, hbb, actA, actAb)
                items.append((None, e_norm))
                FE = c.FE
                for q in range(c.NE):
                    def epi_h(ci, ps, psb):
                        t1, t1b, _ = tmp.next()
                        S.op("act", lambda e: e.activation(out=t1[:], in_=ps[:, 0:512], func=AF.Relu),
                             reads=[psb], writes=[t1b])
                        S.op("pool", lambda e: e.tensor_tensor(out=actB[:, ci, :], in0=t1[:], in1=t1[:],
                                                               op=ALU.mult), reads=[t1b], writes=[actBb])
                    self.items_fm(items, self.w_ff1, q * FE * 128, FE * 128, KT, actA, actAb, 512, epi_h)
                    self.items_tm(items, self.w_ff2, q * FE, FE, 0, D, actB, actBb, 4, epi_res)

                def fin(tile, buf, to0=to0):
                    gv = actAf[:, 0:D]
                    S.dma("sp", gv, self.g_final_d.partition_broadcast(128), gsem, reads=[], writes=[actAb])
                    for tt in range(4):
                        ss, ssb, _ = self.small.next()
                        S.op("act", lambda e, tt=tt: e.activation(out=hb[:, 0:D], in_=xcur[:, tt, :], func=AF.Square,
                                                                   accum_out=ss[:, 0:1]),
                             reads=[xb[tt]], writes=[hbb, ssb])
                        S.op("dve", lambda e: e.tensor_scalar(out=ss[:, 1:2], in0=ss[:, 0:1], scalar1=1.0 / D,
                                                              scalar2=EPS, op0=ALU.mult, op1=ALU.add),
                             reads=[ssb], writes=[ssb])
                        S.op("pool", lambda e: e.tensor_tensor(out=ss[:, 2:3], in0=ss[:, 1:2],
                                                               in1=self.cst[:, 0:1], op=ALU.pow),
                             reads=[ssb], writes=[ssb])
                        S.op("dve", lambda e, tt=tt: e.scalar_tensor_tensor(
                            out=xcur[:, tt, :], in0=xcur[:, tt, :], scalar=ss[:, 2:3], in1=gv,
                            op0=ALU.mult, op1=ALU.mult), reads=[xb[tt], ssb, actAb], writes=[xb[tt]])
                        S.dma("sp", self.out_d[to0 + tt * 128:to0 + (tt + 1) * 128, :], xcur[:, tt, :], xsems[tt],
                              reads=[xb[tt]])
                items.append((None, fin))
            self.run_items(items)
            S.barrier()


def _const_mats():
    ident = np.eye(128, dtype=np.float32)
    k = np.arange(128)
    tri = (k[:, None] <= k[None, :]).astype(np.float32)
    sel127 = np.zeros((128, 128), np.float32)
    sel127[127, :] = 1.0
    sel63 = np.zeros((128, 128), np.float32)
    sel63[63, :] = 1.0
    return np.concatenate([ident, tri, sel127, sel63], axis=1)


def _col(v, kt):
    return np.ascontiguousarray(np.asarray(v, np.float32).reshape(kt, 128).T)


def make_core_inputs(cfg, xl, kvalid0, meml, p):
    c = cfg
    kb = np.zeros(c.L, np.float32)
    kb[:kvalid0] = -30000.0
    f32 = lambda a: np.ascontiguousarray(np.asarray(a, np.float32))
    sm = lambda a: np.ascontiguousarray(np.asarray(a, np.float32).reshape(c.NST, 128).T)
    ldt = np.repeat(np.asarray(p["log_dt"], np.float32)[:, None], 64, axis=1)
    s5p = np.concatenate([sm(p["A_re"]), sm(p["A_im"]), sm(ldt)], axis=1)

    def bl(a):
        a = np.asarray(a, np.float32).reshape(c.NST, 2, 64, 16)
        return a.transpose(1, 2, 0, 3).reshape(128, c.NST, 16)

    def cl(a):
        a = np.asarray(a, np.float32).transpose(0, 2, 1).reshape(c.NST, 2, 64, 16)
        return a.transpose(1, 2, 0, 3).reshape(128, c.NST, 16)

    return {
        "xl": f32(xl), "meml": f32(meml), "kbias": _col(kb, c.NKT),
        "w_in": f32(p["w_in"]), "w_glu": f32(p["w_glu"]), "w_attn_up": f32(p["w_attn_up"]),
        "w_ssm_up": f32(p["w_ssm_up"]), "w_out": f32(p["w_out"]), "wq_x": f32(p["wq_x"]),
        "wk_x": f32(p["wk_x"]), "wv_x": f32(p["wv_x"]), "wo_x": f32(p["wo_x"]),
        "w_ff1": f32(p["w_ff1"]), "w_ff2": f32(p["w_ff2"]),
        "gcols": np.concatenate([_col(p["g_mix"], c.KT), _col(p["g_xattn"], c.KT),
                                 _col(p["g_mem"], c.KT), _col(p["g_mlp"], c.KT)], axis=1),
        "g_final": f32(p["g_final"]),
        "b_f": f32(np.asarray(p["b_f"]).reshape(c.NH, 1)),
        "b_gate_t": _col(p["b_gate"], 2 * c.KT), "b_glu_t": _col(p["b_glu"], c.CT),
        "dskip_t": _col(np.asarray(p["D_skip"]).reshape(-1), c.CT),
        "s5p": np.ascontiguousarray(s5p),
        "s5b": np.ascontiguousarray(np.stack([bl(p["B_re"]), bl(p["B_im"])], axis=1)),
        "s5c": np.ascontiguousarray(np.stack([cl(p["C_re"]), cl(p["C_im"])], axis=1)),
        "cmat": _const_mats(),
    }


_NC_CACHE = {}


def kernel(x, mem, g_mix, w_in, b_f, b_gate, A_re, A_im, log_dt, B_re, B_im, C_re, C_im,
           D_skip, w_glu, b_glu, w_attn_up, w_ssm_up, w_out, g_xattn, g_mem, wq_x, wk_x,
           wv_x, wo_x, g_mlp, w_ff1, w_ff2, g_final):
    cfg = FULL
    x = np.asarray(x, np.float32)
    mem = np.asarray(mem, np.float32)
    p = dict(g_mix=g_mix[0], w_in=w_in[0], b_f=b_f[0], b_gate=b_gate[0], A_re=A_re[0], A_im=A_im[0],
             log_dt=log_dt[0], B_re=B_re[0], B_im=B_im[0], C_re=C_re[0], C_im=C_im[0], D_skip=D_skip[0],
             w_glu=w_glu[0], b_glu=b_glu[0], w_attn_up=w_attn_up[0], w_ssm_up=w_ssm_up[0], w_out=w_out[0],
             g_xattn=g_xattn[0], g_mem=g_mem[0], wq_x=wq_x[0], wk_x=wk_x[0], wv_x=wv_x[0], wo_x=wo_x[0],
             g_mlp=g_mlp[0], w_ff1=w_ff1[0], w_ff2=w_ff2[0], g_final=g_final)
    p = {k: np.asarray(v, np.float32) for k, v in p.items()}
    B, SEQ, D = x.shape
    nq = SEQ // cfg.OWN
    in_maps = []
    shared = None
    for core in range(8):
        b, j = core // nq, core % nq
        xl = np.zeros((cfg.L, D), np.float32)
        n = (j + 1) * cfg.OWN
        xl[cfg.L - n:] = x[b, :n]
        m = make_core_inputs(cfg, xl, cfg.L - n, mem[b], p) if shared is None else None
        if shared is None:
            shared = m
        else:
            m = dict(shared)
            kb = np.zeros(cfg.L, np.float32)
            kb[:cfg.L - n] = -30000.0
            m["xl"] = xl
            m["meml"] = np.ascontiguousarray(mem[b])
            m["kbias"] = _col(kb, cfg.NKT)
        in_maps.append(m)
    if "full" not in _NC_CACHE:
        _NC_CACHE["full"] = Builder(cfg).build()
    nc = _NC_CACHE["full"]
    res = run_bass_kernel_spmd(nc, in_maps, core_ids=list(range(8)))
    out = np.zeros((B, SEQ, D), np.float32)
    for core in range(8):
        b, j = core // nq, core % nq
        out[b, j * cfg.OWN:(j + 1) * cfg.OWN] = res.results[core]["out"]
    return out
```

```python
import math
from contextlib import ExitStack

import numpy as np
import concourse.bass as bass
import concourse.mybir as mybir
from concourse.bass_utils import run_bass_kernel_spmd

F32 = mybir.dt.float32
BF16 = mybir.dt.bfloat16
AF = mybir.ActivationFunctionType
ALU = mybir.AluOpType
EPS = 1e-6
PI = math.pi


class Sem:
    def __init__(self, nc, name):
        self.h = nc.alloc_semaphore(name=name)
        self.c = 0


class Buf:
    __slots__ = ("name", "w", "r")

    def __init__(self, name):
        self.name = name
        self.w = None
        self.r = {}


class Sched:
    def __init__(self, nc):
        self.nc = nc
        self.eng = {"pe": nc.tensor, "act": nc.scalar, "dve": nc.vector,
                    "pool": nc.gpsimd, "sp": nc.sync}
        self.sem = {k: Sem(nc, "e_" + k) for k in self.eng}
        self.seen = {k: {} for k in self.eng}
        self.dsems = []
        self.ninst = 0

    def dsem(self, name):
        s = Sem(self.nc, name)
        self.dsems.append(s)
        return s

    def _deps(self, e, reads, writes, gen=None):
        need = {}
        own = self.sem.get(e)
        for b in reads:
            if b.w is not None:
                sm, v = b.w
                if need.get(sm, 0) < v:
                    need[sm] = v
        for b in writes:
            if b.w is not None and b.w[0] is not own and b.w[0] is not gen:
                sm, v = b.w
                if need.get(sm, 0) < v:
                    need[sm] = v
            for sm, v in b.r.items():
                if sm is not own and need.get(sm, 0) < v:
                    need[sm] = v
        seen = self.seen[e]
        for sm, v in need.items():
            if seen.get(sm, 0) < v:
                self.eng[e].wait_ge(sm.h, v)
                seen[sm] = v

    def _record(self, ev, reads, writes):
        sm, v = ev
        for b in reads:
            if b.r.get(sm, 0) < v:
                b.r[sm] = v
        for b in writes:
            b.w = ev
            b.r = {}

    def op(self, e, fn, reads=(), writes=()):
        self._deps(e, reads, writes)
        inst = fn(self.eng[e])
        sm = self.sem[e]
        sm.c += 1
        inst.then_inc(sm.h, 1)
        self._record((sm, sm.c), reads, writes)
        self.ninst += 1

    def dma(self, q, out, in_, sem, reads=(), writes=()):
        self._deps(q, reads, writes, gen=sem)
        inst = self.eng[q].dma_start(out=out, in_=in_)
        sem.c += 16
        inst.then_inc(sem.h, 16)
        self._record((sem, sem.c), reads, writes)
        self.ninst += 1

    def barrier(self):
        allsems = list(self.sem.values()) + self.dsems
        for e in self.eng:
            seen = self.seen[e]
            for sm in allsems:
                if sm.c > 0 and seen.get(sm, 0) < sm.c:
                    self.eng[e].wait_ge(sm.h, sm.c)
                    seen[sm] = sm.c


class Ring:
    def __init__(self, bld, stack, name, n, shape, dt, dma=True, psum=False):
        self.t, self.b, self.s = [], [], []
        for i in range(n):
            if psum:
                t = stack.enter_context(bld.nc.psum_tensor(f"ps_{name}{i}", list(shape), dt))
            else:
                t = stack.enter_context(bld.nc.sbuf_tensor(f"sb_{name}{i}", list(shape), dt))
            self.t.append(t)
            self.b.append(Buf(f"{name}{i}"))
            self.s.append(bld.getsem(f"{name}{i}") if dma else None)
        self.i = 0
        self.n = n

    def next(self):
        i = self.i % self.n
        self.i += 1
        return self.t[i], self.b[i], self.s[i]


class Cfg:
    def __init__(self, D, L, OWN, NH, SW, XH, MEM, DFF, debug=False):
        self.D, self.L, self.OWN, self.NH, self.SW = D, L, OWN, NH, SW
        self.XH, self.MEM, self.DFF, self.debug = XH, MEM, DFF, debug
        self.AW = NH * 128
        self.KT = D // 128
        self.G = SW // 16
        self.NST = self.G // 2
        self.CT = SW // 128
        self.XW = XH * 256
        self.XT = self.XW // 128
        self.OWN0 = L - OWN
        self.NKT = L // 128
        self.OFF_K = self.AW
        self.OFF_V = 2 * self.AW
        self.OFF_F = 3 * self.AW
        self.OFF_U = self.OFF_F + NH
        self.OFF_G = self.OFF_U + SW
        self.INW = self.OFF_G + 2 * D
        self.FE = min(16, DFF // 128)
        self.NE = DFF // (128 * self.FE)


FULL = Cfg(D=4096, L=8192, OWN=2048, NH=16, SW=1024, XH=4, MEM=256, DFF=16384)


class Builder:
    def __init__(self, cfg):
        self.c = c = cfg
        self.nc = nc = bass.Bass("TRN2", target_bir_lowering=False)
        self.S = Sched(nc)
        self._sempool = {}
        D, L, OWN = c.D, c.L, c.OWN

        def din(name, shape, dt=F32):
            return nc.dram_tensor(name, list(shape), dt, kind="ExternalInput").ap()

        def dscr(name, shape, dt):
            if c.debug:
                return nc.dram_tensor(name, list(shape), dt, kind="ExternalOutput").ap()
            return nc.dram_tensor(name, list(shape), dt).ap()

        self.xl = din("xl", [L, D])
        self.meml = din("meml", [c.MEM, D])
        self.kbias_d = din("kbias", [128, c.NKT])
        self.w_in = din("w_in", [D, c.INW])
        self.w_glu = din("w_glu", [c.SW, c.SW])
        self.w_attn_up = din("w_attn_up", [c.AW, D])
        self.w_ssm_up = din("w_ssm_up", [c.SW, D])
        self.w_out = din("w_out", [D, D])
        self.wq_x = din("wq_x", [D, c.XW])
        self.wk_x = din("wk_x", [D, c.XW])
        self.wv_x = din("wv_x", [D, c.XW])
        self.wo_x = din("wo_x", [c.XW, D])
        self.w_ff1 = din("w_ff1", [D, c.DFF])
        self.w_ff2 = din("w_ff2", [c.DFF, D])
        self.gcols_d = din("gcols", [128, 4 * c.KT])
        self.g_final_d = din("g_final", [D])
        self.bf_d = din("b_f", [c.NH, 1])
        self.bgate_d = din("b_gate_t", [128, 2 * c.KT])
        self.bglu_d = din("b_glu_t", [128, c.CT])
        self.dskip_d = din("dskip_t", [128, c.CT])
        self.s5p_d = din("s5p", [128, 3 * c.NST])
        self.s5b_d = din("s5b", [128, 2, c.NST, 16])
        self.s5c_d = din("s5c", [128, 2, c.NST, 16])
        self.cmat_d = din("cmat", [128, 4 * 128])
        self.out_d = nc.dram_tensor("out", [OWN, D], F32, kind="ExternalOutput").ap()

        self.KT_d = dscr("KT_d", [c.NH, 128, L], BF16)
        self.V_d = dscr("V_d", [L, c.AW], BF16)
        self.uT_d = dscr("uT_d", [c.SW, L], BF16)
        self.f_d = dscr("f_d", [c.NH, L], F32)
        self.QT_d = dscr("QT_d", [c.NH, 128, OWN], BF16)
        self.gT_d = dscr("gT_d", [2 * D, OWN], BF16)
        self.y2T_d = dscr("y2T_d", [c.SW, OWN], BF16)
        self.aoT_d = dscr("aoT_d", [c.AW, OWN], BF16)

    def getsem(self, name):
        return self.S.dsem(name)

    def sb(self, stack, name, shape, dt):
        self._uid = getattr(self, "_uid", 0) + 1
        return stack.enter_context(self.nc.sbuf_tensor(f"sb_{name}_{self._uid}", list(shape), dt))

    def build(self):
        c, nc, S = self.c, self.nc, self.S
        with ExitStack() as top:
            self.top = top
            self.pf = Ring(self, top, "pf", 6, [128, 512], F32, dma=False, psum=True)
            self.pbt = Ring(self, top, "pbt", 2, [128, 1024], BF16, dma=False, psum=True)
            self.identf = self.sb(top, "identf", [128, 128], F32)
            self.identb = self.sb(top, "identb", [128, 128], BF16)
            self.trib = self.sb(top, "trib", [128, 128], BF16)
            self.gcols = self.sb(top, "gcols", [128, 4 * c.KT], F32)
            self.bgate = self.sb(top, "bgate", [128, 2 * c.KT], F32)
            self.bglu = self.sb(top, "bglu", [128, c.CT], F32)
            self.dskip = self.sb(top, "dskip", [128, c.CT], F32)
            self.nbf = self.sb(top, "nbf", [c.NH, 1], F32)
            self.kbias = self.sb(top, "kbias", [128, c.NKT], F32)
            self.cst = self.sb(top, "cst", [128, 4], F32)
            self.small = Ring(self, top, "small", 8, [128, 4], F32, dma=False)
            self.constb = Buf("const")
            cs = self.getsem("const")
            cm = self.cmat_d
            S.dma("sp", self.identf[:], cm[:, 0:128], cs, writes=[self.constb])
            S.dma("pool", self.identb[:], cm[:, 0:128], cs, writes=[self.constb])
            S.dma("pool", self.trib[:], cm[:, 128:256], cs, writes=[self.constb])
            S.dma("sp", self.gcols[:], self.gcols_d[:, :], cs, writes=[self.constb])
            S.dma("sp", self.bgate[:], self.bgate_d[:, :], cs, writes=[self.constb])
            S.dma("sp", self.bglu[:], self.bglu_d[:, :], cs, writes=[self.constb])
            S.dma("sp", self.dskip[:], self.dskip_d[:, :], cs, writes=[self.constb])
            S.dma("sp", self.nbf[:], self.bf_d[:, :], cs, writes=[self.constb])
            S.dma("sp", self.kbias[:], self.kbias_d[:, :], cs, writes=[self.constb])
            S.op("dve", lambda e: e.memset(self.cst[:, 0:1], -0.5), writes=[self.constb])
            S.op("dve", lambda e: e.memset(self.cst[:, 1:2], 1.0), writes=[self.constb])
            S.op("dve", lambda e: e.tensor_scalar(out=self.nbf[:], in0=self.nbf[:], scalar1=-1.0,
                                                  scalar2=None, op0=ALU.mult),
                 reads=[self.constb], writes=[self.constb])
            S.barrier()

            self.phase_A()
            S.barrier()
            with ExitStack() as mid:
                self.sel127 = self.sb(mid, "sel127", [128, 128], F32)
                self.sel63 = self.sb(mid, "sel63", [128, 128], F32)
                self.c_tm = self.sb(mid, "c_tm", [128, c.NKT, c.NH], F32)
                self.c_tmb = Buf("c_tm")
                S.dma("sp", self.sel127[:], cm[:, 256:384], cs, writes=[self.constb])
                S.dma("sp", self.sel63[:], cm[:, 384:512], cs, writes=[self.constb])
                S.barrier()
                self.phase_S5()
                S.barrier()
                self.phase_B()
                S.barrier()
            self.phase_blocks()
            S.barrier()
        return nc

    def slab_load(self, W, k0, nk, c0, ncols):
        S = self.S
        tile, buf, sem = self.slabs.next()
        src = W.rearrange("(kt p) n -> p kt n", p=128)
        nd = min(4, nk)
        step = -(-nk // nd)
        for a in range(0, nk, step):
            b = min(nk, a + step)
            S.dma("pool", tile[:, a:b, 0:ncols], src[:, k0 + a:k0 + b, c0:c0 + ncols], sem,
                  writes=[buf])
        return tile, buf

    def run_items(self, items):
        idx = [i for i, it in enumerate(items) if it[0] is not None]
        loaded = {}
        ptr = 0

        def prefetch(upto):
            nonlocal ptr
            while ptr < len(idx) and ptr < upto:
                loaded[idx[ptr]] = self.slab_load(*items[idx[ptr]][0])
                ptr += 1

        pos = 0
        prefetch(2)
        for i, (spec, fn) in enumerate(items):
            if spec is not None:
                pos += 1
                prefetch(pos + 2)
                tile, buf = loaded.pop(i)
                fn(tile, buf)
            else:
                fn(None, None)

    def items_fm(self, items, W, c0, ncols, nk, actT, actb, ntok, epi, k0=0):
        S = self.S
        for s0 in range(0, ncols, 256):
            n_ = min(256, ncols - s0)

            def fn(tile, buf, s0=s0, n_=n_):
                for ci in range(n_ // 128):
                    ps, psb, _ = self.pf.next()
                    for kt in range(nk):
                        S.op("pe", lambda e, kt=kt, ci=ci: e.matmul(
                            ps[:, 0:ntok], lhsT=tile[:, kt, ci * 128:(ci + 1) * 128],
                            rhs=actT[:, kt, 0:ntok], start=(kt == 0), stop=(kt == nk - 1)),
                            reads=[buf, actb], writes=[psb])
                    epi(s0 // 128 + ci, ps, psb)

            items.append(((W, k0, nk, c0 + s0, n_), fn))

    def items_tm(self, items, W, k0, nk, c0, ncols, actT, actb, ntt, epi):
        S = self.S
        for s0 in range(0, ncols, 256):
            n_ = min(256, ncols - s0)

            def fn(tile, buf, s0=s0, n_=n_):
                for tt in range(ntt):
                    ps, psb, _ = self.pf.next()
                    for kt in range(nk):
                        S.op("pe", lambda e, kt=kt, tt=tt: e.matmul(
                            ps[:, 0:n_], lhsT=actT[:, kt, tt * 128:(tt + 1) * 128],
                            rhs=tile[:, kt, 0:n_], start=(kt == 0), stop=(kt == nk - 1)),
                            reads=[buf, actb], writes=[psb])
                    epi(tt, s0, n_, ps, psb)

            items.append(((W, k0, nk, c0 + s0, n_), fn))

    def norm_front(self, get_src, tt, hb, hbb):
        c, S = self.c, self.S
        D = c.D
        xs, xsb = get_src(tt)
        ss, ssb, _ = self.small.next()
        S.op("act", lambda e: e.activation(out=hb[:, 0:D], in_=xs, func=AF.Square,
                                           accum_out=ss[:, 0:1]),
             reads=[xsb], writes=[hbb, ssb])
        S.op("dve", lambda e: e.tensor_scalar(out=ss[:, 1:2], in0=ss[:, 0:1], scalar1=1.0 / D,
                                              scalar2=EPS, op0=ALU.mult, op1=ALU.add),
             reads=[ssb], writes=[ssb])
        S.op("pool", lambda e: e.tensor_tensor(out=ss[:, 2:3], in0=ss[:, 1:2],
                                               in1=self.cst[:, 0:1], op=ALU.pow),
             reads=[ssb], writes=[ssb])
        S.op("act", lambda e: e.activation(out=hb[:, 0:D], in_=xs, func=AF.Copy, scale=ss[:, 2:3]),
             reads=[xsb, ssb], writes=[hbb])

    def norm_back(self, tt, gofs, hb, hbb, hT, hTb):
        c, S = self.c, self.S
        KT = c.KT
        grp = min(8, KT)
        for k0 in range(0, KT, grp):
            pb, pbb, _ = self.pbt.next()
            for k in range(grp):
                S.op("pe", lambda e, k=k: e.transpose(
                    out=pb[:, k * 128:(k + 1) * 128],
                    in_=hb[:, (k0 + k) * 128:(k0 + k + 1) * 128], identity=self.identb[:]),
                    reads=[hbb], writes=[pbb])
            S.op("dve", lambda e: e.tensor_tensor(
                out=hT[:, k0:k0 + grp, tt * 128:(tt + 1) * 128],
                in0=pb[:, 0:grp * 128].rearrange("p (k t) -> p k t", k=grp),
                in1=self.gcols[:, gofs + k0:gofs + k0 + grp].unsqueeze(2).to_broadcast(
                    [128, grp, 128]),
                op=ALU.mult), reads=[pbb], writes=[hTb])

    def norm_T(self, get_src, nt, gofs, hb, hbb, hT, hTb):
        for tt in range(nt):
            self.norm_front(get_src, tt, hb, hbb)
            self.norm_back(tt, gofs, hb, hbb, hT, hTb)

    def phase_A(self):
        c, S = self.c, self.S
        D, L, KT = c.D, c.L, c.KT
        with ExitStack() as st:
            self.slabs = Ring(self, st, "slabA", 3, [128, 32, 256], BF16)
            xst = Ring(self, st, "xst", 2, [128, D], F32)
            hbs = [(self.sb(st, f"hbA{i}", [128, D], BF16), Buf(f"hbA{i}")) for i in range(4)]
            hTs = [(self.sb(st, f"hTA{i}", [128, KT, 512], BF16), Buf(f"hTA{i}")) for i in range(2)]
            stb = Ring(self, st, "stb", 4, [128, 512], BF16)
            stf = Ring(self, st, "stf", 2, [128, 512], F32)
            items = []
            for bi in range(L // 512):
                t0 = bi * 512
                own = t0 >= c.OWN0
                to0 = t0 - c.OWN0

                hT, hTb = hTs[bi % 2]

                def nfront(tile, buf, bi=bi):
                    t0_ = bi * 512

                    def get_src(tt):
                        xs, xsb, sem = xst.next()
                        S.dma("sp", xs[:], self.xl[t0_ + tt * 128:t0_ + (tt + 1) * 128, :], sem,
                              writes=[xsb])
                        return xs[:], xsb
                    for tt in range(4):
                        self.norm_front(get_src, tt, hbs[tt][0], hbs[tt][1])

                def nback(tile, buf, bi=bi):
                    hT_, hTb_ = hTs[bi % 2]
                    for tt in range(4):
                        self.norm_back(tt, 0, hbs[tt][0], hbs[tt][1], hT_, hTb_)

                if bi == 0:
                    items.append((None, nfront))
                    items.append((None, nback))
                if bi + 1 < L // 512:
                    nf_next = (lambda tile, buf, f=nfront, b=bi + 1: f(tile, buf, bi=b))
                    nb_next = (lambda tile, buf, f=nback, b=bi + 1: f(tile, buf, bi=b))
                    items.append((None, nf_next))
                else:
                    nb_next = None

                def epi_K(ci, ps, psb, t0=t0):
                    sg, sgb, sem = stb.next()
                    S.op("act", lambda e: e.activation(out=sg[:], in_=ps[:, 0:512], func=AF.Copy),
                         reads=[psb], writes=[sgb])
                    S.dma("sp", self.KT_d[ci, :, t0:t0 + 512], sg[:], sem, reads=[sgb])

                def epi_V(tt, s0, n_, ps, psb, t0=t0):
                    sg, sgb, sem = stb.next()
                    S.op("act", lambda e: e.activation(out=sg[:, 0:n_], in_=ps[:, 0:n_],
                                                       func=AF.Copy),
                         reads=[psb], writes=[sgb])
                    S.dma("sp", self.V_d[t0 + tt * 128:t0 + (tt + 1) * 128, s0:s0 + n_],
                          sg[:, 0:n_], sem, reads=[sgb])

                def epi_F(ci, ps, psb, t0=t0):
                    sg, sgb, sem = stf.next()
                    S.op("act", lambda e: e.activation(out=sg[:], in_=ps[:, 0:512], func=AF.Copy),
                         reads=[psb], writes=[sgb])
                    S.dma("sp", self.f_d[0:c.NH, t0:t0 + 512], sg[0:c.NH, :], sem, reads=[sgb])

                def epi_U(ci, ps, psb, t0=t0):
                    sg, sgb, sem = stb.next()
                    S.op("act", lambda e: e.activation(out=sg[:], in_=ps[:, 0:512], func=AF.Copy),
                         reads=[psb], writes=[sgb])
                    S.dma("sp", self.uT_d[ci * 128:(ci + 1) * 128, t0:t0 + 512], sg[:], sem,
                          reads=[sgb])

                def epi_Q(ci, ps, psb, to0=to0):
                    sg, sgb, sem = stb.next()
                    S.op("act", lambda e: e.activation(out=sg[:], in_=ps[:, 0:512], func=AF.Copy),
                         reads=[psb], writes=[sgb])
                    S.dma("sp", self.QT_d[ci, :, to0:to0 + 512], sg[:], sem, reads=[sgb])

                def epi_G(ci, ps, psb, to0=to0):
                    sg, sgb, sem = stb.next()
                    S.op("act", lambda e: e.activation(out=sg[:], in_=ps[:, 0:512],
                                                       func=AF.Sigmoid,
                                                       bias=self.bgate[:, ci:ci + 1]),
                         reads=[psb], writes=[sgb])
                    S.dma("sp", self.gT_d[ci * 128:(ci + 1) * 128, to0:to0 + 512], sg[:], sem,
                          reads=[sgb])

                self.items_fm(items, self.w_in, c.OFF_K, c.AW, KT, hT, hTb, 512, epi_K)
                self.items_tm(items, self.w_in, 0, KT, c.OFF_V, c.AW, hT, hTb, 4, epi_V)
                self.items_fm(items, self.w_in, c.OFF_F, 128, KT, hT, hTb, 512, epi_F)
                self.items_fm(items, self.w_in, c.OFF_U, c.SW, KT, hT, hTb, 512, epi_U)
                if own:
                    self.items_fm(items, self.w_in, 0, c.AW, KT, hT, hTb, 512, epi_Q)
                    self.items_fm(items, self.w_in, c.OFF_G, 2 * D, KT, hT, hTb, 512, epi_G)
                if nb_next is not None:
                    items.append((None, nb_next))
            self.run_items(items)
            S.barrier()

    def phase_S5(self):
        c, S = self.c, self.S
        L, NH, NST, CT, NKT = c.L, c.NH, c.NST, c.CT, c.NKT
        T = 256
        with ExitStack() as st:
            fl = self.sb(st, "fl", [NH, L], F32)
            fl2 = self.sb(st, "fl2", [NH, L], F32)
            flb = Buf("fl")
            fl2b = Buf("fl2")
            sem = self.getsem("fl")
            S.dma("sp", fl[:], self.f_d[:, :], sem, writes=[flb])
            S.op("act", lambda e: e.activation(out=fl[:], in_=fl[:], func=AF.Exp, scale=-1.0,
                                               bias=self.nbf[:, 0:1]),
                 reads=[flb], writes=[flb])
            S.op("act", lambda e: e.activation(out=fl[:], in_=fl[:], func=AF.Ln, bias=1.0),
                 reads=[flb], writes=[flb])
            CH = min(2048, L)
            for i in range(0, L, CH):
                init = 0.0 if i == 0 else fl2[:, i - 1:i]
                S.op("dve", lambda e, i=i, init=init: e.tensor_tensor_scan(
                    out=fl2[:, i:i + CH], data0=self.cst[0:NH, 1:2].to_broadcast([NH, CH]),
                    data1=fl[:, i:i + CH], initial=init, op0=ALU.mult, op1=ALU.subtract),
                    reads=[flb, fl2b], writes=[fl2b])
            per = 512 // NH
            for k0 in range(0, NKT, per):
                ps, psb, _ = self.pf.next()
                n = min(per, NKT - k0)
                for k in range(n):
                    S.op("pe", lambda e, k=k: e.transpose(
                        out=ps[:, k * NH:(k + 1) * NH], in_=fl2[:, (k0 + k) * 128:(k0 + k + 1) * 128],
                        identity=self.identf[0:NH, 0:NH]), reads=[fl2b], writes=[psb])
                S.op("act", lambda e, n=n: e.activation(
                    out=self.c_tm[:, k0:k0 + n, :],
                    in_=ps[:, 0:n * NH].rearrange("p (k h) -> p k h", h=NH), func=AF.Copy),
                    reads=[psb], writes=[self.c_tmb])
            S.barrier()

        with ExitStack() as st:
            P = self.sb(st, "s5P", [128, 34, NST], F32)
            Pb = Buf("s5P")
            BC = self.sb(st, "s5BC", [128, 2, NST, 16], F32)
            CC = self.sb(st, "s5CC", [128, 2, NST, 16], F32)
            BB = self.sb(st, "s5BB", [128, 2, NST, 16], F32)
            BCb = Buf("s5BC")
            Bm = self.sb(st, "Bm", [128, 2, NST, 128], BF16)
            Cm = self.sb(st, "Cm", [128, 3, NST, 128], BF16)
            Bmb, Cmb = Buf("Bm"), Buf("Cm")
            tabb = Buf("tab")
            sem = self.getsem("s5ld")
            S.dma("sp", P[:, 0:3, :], self.s5p_d.rearrange("p (a s) -> p a s", a=3), sem, writes=[Pb])
            S.dma("sp", BC[:], self.s5b_d[:, :, :, :], sem, writes=[BCb])
            S.dma("sp", CC[:], self.s5c_d[:, :, :, :], sem, writes=[BCb])
            Pb.w = (sem, sem.c)
            BCb.w = (sem, sem.c)

            def pv(i):
                return P[:, i, :]

            def dv(fn):
                S.op("dve", fn, reads=[Pb], writes=[Pb])

            def av(fn):
                S.op("act", fn, reads=[Pb], writes=[Pb])

            TT = lambda o, a, b, op: dv(lambda e: e.tensor_tensor(out=pv(o), in0=pv(a), in1=pv(b), op=op))
            TS = lambda o, a, s1, s2, op0, op1: dv(lambda e: e.tensor_scalar(
                out=pv(o), in0=pv(a), scalar1=s1, scalar2=s2, op0=op0, op1=op1))
            av(lambda e: e.activation(out=pv(3), in_=pv(2), func=AF.Exp))
            TT(4, 3, 0, ALU.mult)
            TT(5, 3, 1, ALU.mult)
            av(lambda e: e.activation(out=pv(6), in_=pv(4), func=AF.Exp))
            TS(7, 5, 0.0, None, ALU.add, ALU.bypass)
            TS(8, 5, PI / 2, None, ALU.add, ALU.bypass)
            for _ in range(4):
                for a in (7, 8):
                    TS(9, a, PI, 2 * PI, ALU.is_gt, ALU.mult)
                    TT(a, a, 9, ALU.subtract)
            av(lambda e: e.activation(out=pv(9), in_=pv(7), func=AF.Sin))
            av(lambda e: e.activation(out=pv(10), in_=pv(8), func=AF.Sin))
            TT(11, 6, 10, ALU.mult)
            TT(12, 6, 9, ALU.mult)
            TT(13, 0, 0, ALU.mult)
            TT(14, 1, 1, ALU.mult)
            TT(13, 13, 14, ALU.add)
            dv(lambda e: e.reciprocal(out=pv(14), in_=pv(13)))
            TS(15, 11, -1.0, None, ALU.add, ALU.bypass)
            TT(16, 15, 0, ALU.mult)
            TT(17, 12, 1, ALU.mult)
            TT(16, 16, 17, ALU.add)
            TT(16, 16, 14, ALU.mult)
            TT(17, 12, 0, ALU.mult)
            TT(18, 15, 1, ALU.mult)
            TT(17, 17, 18, ALU.subtract)
            TT(17, 17, 14, ALU.mult)
            fre = P[:, 16, :].unsqueeze(2).to_broadcast([128, NST, 16])
            fim = P[:, 17, :].unsqueeze(2).to_broadcast([128, NST, 16])

            def bop(o, a, b, op):
                S.op("dve", lambda e: e.tensor_tensor(out=o, in0=a, in1=b, op=op),
                     reads=[Pb, BCb], writes=[BCb])
            bop(BB[:, 0], BC[:, 0], fre, ALU.mult)
            bop(BB[:, 1], BC[:, 1], fim, ALU.mult)
            bop(BB[:, 0], BB[:, 0], BB[:, 1], ALU.subtract)
            bop(BB[:, 1], BC[:, 1], fre, ALU.mult)
            bop(BC[:, 1], BC[:, 0], fim, ALU.mult)
            bop(BB[:, 1], BB[:, 1], BC[:, 1], ALU.add)
            zb = self.sb(st, "zb", [128, 8, 128], BF16)
            zbb = Buf("zb")
            S.op("pool", lambda e: e.memset(zb[:], 0.0), writes=[zbb])
            S.op("pool", lambda e: e.memset(Cm[:], 0.0), writes=[Cmb])
            for s in range(NST):
                off = 32 * (s % 4)
                for ri in range(2):
                    z = zb[:, (s % 4) * 2 + ri, :]
                    S.op("act", lambda e: e.activation(out=z[0:64, off:off + 16],
                                                       in_=BB[0:64, ri, s, :], func=AF.Copy),
                         reads=[BCb], writes=[zbb])
                    S.op("act", lambda e: e.activation(out=z[64:128, off + 16:off + 32],
                                                       in_=BB[64:128, ri, s, :], func=AF.Copy),
                         reads=[BCb], writes=[zbb])
                    pb, pbb, _ = self.pbt.next()
                    S.op("pe", lambda e: e.transpose(out=pb[:, 0:128], in_=z, identity=self.identb[:]),
                         reads=[zbb], writes=[pbb])
                    S.op("dve", lambda e: e.tensor_copy(out=Bm[:, ri, s, :], in_=pb[:, 0:128]),
                         reads=[pbb], writes=[Bmb])
                    sc = 1.0 if ri == 0 else -1.0
                    S.op("act", lambda e: e.activation(out=Cm[0:64, ri, s, off:off + 16],
                                                       in_=CC[0:64, ri, s, :], func=AF.Copy, scale=sc),
                         reads=[BCb], writes=[Cmb])
                    S.op("act", lambda e: e.activation(out=Cm[64:128, ri, s, off + 16:off + 32],
                                                       in_=CC[64:128, ri, s, :], func=AF.Copy, scale=sc),
                         reads=[BCb], writes=[Cmb])
                    if ri == 0:
                        S.op("act", lambda e: e.activation(out=Cm[0:64, 2, s, off:off + 16],
                                                           in_=CC[0:64, 0, s, :], func=AF.Copy, scale=-1.0),
                             reads=[BCb], writes=[Cmb])
                        S.op("act", lambda e: e.activation(out=Cm[64:128, 2, s, off + 16:off + 32],
                                                           in_=CC[64:128, 0, s, :], func=AF.Copy, scale=-1.0),
                             reads=[BCb], writes=[Cmb])
            def build_tables(p1_re, p1_im, reverse, writer):
                H = max(1, NST // 2)
                halves = [(0, H), (H, NST)] if NST > 1 else [(0, NST)]
                for (s0, s1) in halves:
                    n = s1 - s0
                    with ExitStack() as st2:
                        ER = self.sb(st2, "ER", [128, n, T], F32)
                        EI = self.sb(st2, "EI", [128, n, T], F32)
                        T1 = self.sb(st2, "T1", [128, n, T // 2], F32)
                        T2 = self.sb(st2, "T2", [128, n, T // 2], F32)
                        Eb = Buf("E")

                        def ev(fn):
                            S.op("dve", fn, reads=[Pb, Eb], writes=[Eb, Pb])
                        i0 = T - 1 if reverse else 0
                        ev(lambda e: e.memset(ER[:, :, i0:i0 + 1], 1.0))
                        ev(lambda e: e.memset(EI[:, :, i0:i0 + 1], 0.0))
                        TS(20, p1_re, 0.0, None, ALU.add, ALU.bypass)
                        TS(21, p1_im, 0.0, None, ALU.add, ALU.bypass)
                        k = 1
                        while k < T:
                            pr = P[:, 20, s0:s1].unsqueeze(2).to_broadcast([128, n, k])
                            pi_ = P[:, 21, s0:s1].unsqueeze(2).to_broadcast([128, n, k])
                            if reverse:
                                src, dst = slice(T - k, T), slice(T - 2 * k, T - k)
                            else:
                                src, dst = slice(0, k), slice(k, 2 * k)
                            a_r, a_i = ER[:, :, src], EI[:, :, src]
                            t1, t2 = T1[:, :, 0:k], T2[:, :, 0:k]
                            ev(lambda e: e.tensor_tensor(out=t1, in0=a_r, in1=pr, op=ALU.mult))
                            ev(lambda e: e.tensor_tensor(out=t2, in0=a_i, in1=pi_, op=ALU.mult))
                            ev(lambda e: e.tensor_tensor(out=ER[:, :, dst], in0=t1, in1=t2, op=ALU.subtract))
                            ev(lambda e: e.tensor_tensor(out=t1, in0=a_r, in1=pi_, op=ALU.mult))
                            ev(lambda e: e.tensor_tensor(out=t2, in0=a_i, in1=pr, op=ALU.mult))
                            ev(lambda e: e.tensor_tensor(out=EI[:, :, dst], in0=t1, in1=t2, op=ALU.add))
                            TT(22, 20, 20, ALU.mult)
                            TT(23, 21, 21, ALU.mult)
                            TT(24, 20, 21, ALU.mult)
                            TT(20, 22, 23, ALU.subtract)
                            TS(21, 24, 2.0, None, ALU.mult, ALU.bypass)
                            k *= 2
                        writer(ER, EI, s0, s1, Eb)
                        S.barrier()

            burn = Ring.__new__(Ring)
            burn.t, burn.b, burn.s, burn.i, burn.n = self.pf.t[0:4], self.pf.b[0:4], [None] * 4, 0, 4
            utsrc = self.uT_d.rearrange("(ct p) t -> p ct t", p=128)
            dv(lambda e: e.memset(P[:, 28:30, :], 0.0))
            NPRE = c.OWN0 // T
            with ExitStack() as sa:
                WA = self.sb(sa, "WA", [128, NST, 2 * T], BF16)
                WB = self.sb(sa, "WB", [128, NST, 2 * T], BF16)

                def wr_ab(ER, EI, s0, s1, Eb):
                    S.op("dve", lambda e: e.tensor_copy(out=WA[:, s0:s1, 0:T], in_=ER[:]), reads=[Eb], writes=[tabb])
                    S.op("act", lambda e: e.activation(out=WA[:, s0:s1, T:2 * T], in_=EI[:], func=AF.Copy,
                                                       scale=-1.0), reads=[Eb], writes=[tabb])
                    S.op("dve", lambda e: e.tensor_copy(out=WB[:, s0:s1, 0:T], in_=EI[:]), reads=[Eb], writes=[tabb])
                    S.op("act", lambda e: e.activation(out=WB[:, s0:s1, T:2 * T], in_=ER[:], func=AF.Copy),
                         reads=[Eb], writes=[tabb])
                if NPRE > 0:
                    build_tables(11, 12, True, wr_ab)
                    TS(30, 20, 0.0, None, ALU.add, ALU.bypass)
                    TS(31, 21, 0.0, None, ALU.add, ALU.bypass)
                    utr = Ring(self, sa, "utrA", 2, [128, CT, 1024], BF16)
                    junk = Ring(self, sa, "junk", 2, [128, 2 * T], BF16, dma=False)
                    locr = Ring(self, sa, "locr", 2, [128, 2, NST], F32, dma=False)
                    uts = {}
                    for chn in range(NPRE):
                        cg, cc = divmod(chn, 4)
                        if cg not in uts:
                            ut, utb, usem = utr.next()
                            S.dma("sp", ut[:], utsrc[:, :, cg * 1024:(cg + 1) * 1024], usem, writes=[utb])
                            uts[cg] = (ut, utb)
                        ut, utb = uts[cg]
                        ucols = slice(cc * T, (cc + 1) * T)
                        loc, locb, _ = locr.next()
                        for s in range(NST):
                            ct = s // 4
                            ps, psb, _ = burn.next()
                            S.op("pe", lambda e: e.matmul(ps[:, 0:T], lhsT=Bm[:, 0, s, :], rhs=ut[:, ct, ucols],
                                                          start=True, stop=True), reads=[Bmb, utb], writes=[psb])
                            S.op("pe", lambda e: e.matmul(ps[:, T:2 * T], lhsT=Bm[:, 1, s, :], rhs=ut[:, ct, ucols],
                                                          start=True, stop=True), reads=[Bmb, utb], writes=[psb])
                            for ri, W_ in enumerate((WA, WB)):
                                jk, jkb, _ = junk.next()
                                S.op("dve", lambda e, ri=ri, W_=W_, jk=jk: e.scalar_tensor_tensor(
                                    out=jk[:], in0=ps[:, 0:2 * T], scalar=1.0, in1=W_[:, s, :],
                                    op0=ALU.mult, op1=ALU.mult, accum_out=loc[:, ri, s:s + 1]),
                                    reads=[psb, tabb], writes=[jkb, locb])
                        TT(22, 30, 28, ALU.mult)
                        TT(23, 31, 29, ALU.mult)
                        TT(32, 30, 29, ALU.mult)
                        TT(33, 31, 28, ALU.mult)
                        TT(22, 22, 23, ALU.subtract)
                        TT(32, 32, 33, ALU.add)
                        S.op("dve", lambda e: e.tensor_tensor(out=pv(28), in0=pv(22), in1=loc[:, 0, :], op=ALU.add),
                             reads=[Pb, locb], writes=[Pb])
                        S.op("dve", lambda e: e.tensor_tensor(out=pv(29), in0=pv(32), in1=loc[:, 1, :], op=ALU.add),
                             reads=[Pb, locb], writes=[Pb])
                    S.barrier()
            cosT = self.sb(st, "cosT", [128, NST, T], BF16)
            sinT = self.sb(st, "sinT", [128, NST, T], BF16)

            def wr_cs(ER, EI, s0, s1, Eb):
                S.op("dve", lambda e: e.tensor_copy(out=cosT[:, s0:s1, :], in_=ER[:]), reads=[Eb], writes=[tabb])
                S.op("act", lambda e: e.activation(out=sinT[:, s0:s1, :], in_=EI[:], func=AF.Copy),
                     reads=[Eb], writes=[tabb])
            build_tables(10, 9, False, wr_cs)
            TT(22, 10, 28, ALU.mult)
            TT(23, 9, 29, ALU.mult)
            TT(25, 22, 23, ALU.subtract)
            TT(22, 10, 29, ALU.mult)
            TT(23, 9, 28, ALU.mult)
            TT(26, 22, 23, ALU.add)

            wg = self.sb(st, "wg", [128, CT, c.SW], BF16)
            wgb = Buf("wg")
            semw = self.getsem("wgld")
            for ct in range(CT):
                S.dma("pool", wg[:, ct, :], self.w_glu[ct * 128:(ct + 1) * 128, :], semw, writes=[wgb])
            wgb.w = (semw, semw.c)
            zl = self.sb(st, "zl", [128, 2, NST], F32)
            zlb = Buf("zl")
            zib = Buf("zi")
            utr = Ring(self, st, "utr", 2, [128, CT, 1024], BF16)
            wk1 = Ring(self, st, "wk1", 3, [128, 512], F32, dma=False)
            wk2 = Ring(self, st, "wk2", 3, [128, 512], F32, dma=False)
            wwr = Ring(self, st, "wwr", 2, [128, 512], F32, dma=False)
            zr = Ring(self, st, "zr", 3, [128, 512], F32, dma=False)
            q1r = Ring(self, st, "q1r", 4, [128, 512], BF16, dma=False)
            q2r = Ring(self, st, "q2r", 4, [128, 512], BF16, dma=False)
            zhr = Ring(self, st, "zhr", 3, [128, 512], BF16, dma=False)
            ygf = Ring(self, st, "ygf", 2, [128, CT, T], F32, dma=False)
            ygb = Ring(self, st, "ygb", 2, [128, CT, T], BF16, dma=False)
            tmp = Ring(self, st, "s5tmp", 3, [128, T], F32, dma=False)
            y2s = Ring(self, st, "y2s", 3, [128, T], BF16)
            ybank = [(self.pf.t[4], self.pf.b[4]), (self.pf.t[5], self.pf.b[5])]
            tiles = [(chn // 4, chn % 4, s) for chn in range(NPRE, L // T) for s in range(NST)]
            uts, st0, st1, st2, st3, ych = {}, {}, {}, {}, {}, {}
            yidx = [0]

            def stage1(i):
                cg, cc, s = tiles[i]
                if cg not in uts:
                    ut, utb, usem = utr.next()
                    S.dma("sp", ut[:], utsrc[:, :, cg * 1024:(cg + 1) * 1024], usem, writes=[utb])
                    uts[cg] = (ut, utb)
                ut, utb = uts[cg]
                ct = s // 4
                ucols = slice(cc * T, (cc + 1) * T)
                ps, psb, _ = burn.next()
                S.op("pe", lambda e: e.matmul(ps[:, 0:T], lhsT=Bm[:, 0, s, :], rhs=ut[:, ct, ucols],
                                              start=True, stop=True), reads=[Bmb, utb], writes=[psb])
                S.op("pe", lambda e: e.matmul(ps[:, T:2 * T], lhsT=Bm[:, 1, s, :], rhs=ut[:, ct, ucols],
                                              start=True, stop=True), reads=[Bmb, utb], writes=[psb])
                st0[i] = (ps, psb)

            def stage1b(i):
                cg, cc, s = tiles[i]
                ps, psb = st0.pop(i)
                t13, t13b, _ = wk1.next()
                t42, t42b, _ = wk2.next()
                ps3 = ps[:, :].rearrange("p (two t) -> p two t", two=2)
                cosb = cosT[:, s:s + 1, :].to_broadcast([128, 2, T])
                sinb = sinT[:, s:s + 1, :].to_broadcast([128, 2, T])
                S.op("dve", lambda e: e.tensor_tensor(
                    out=t13[:, :].rearrange("p (two t) -> p two t", two=2), in0=ps3, in1=cosb,
                    op=ALU.mult), reads=[psb, tabb], writes=[t13b])
                S.op("dve", lambda e: e.tensor_tensor(
                    out=t42[:, :].rearrange("p (two t) -> p two t", two=2), in0=ps3, in1=sinb,
                    op=ALU.mult), reads=[psb, tabb], writes=[t42b])
                st1[i] = (t13, t13b, t42, t42b, cosb, sinb)

            def stage2(i):
                cg, cc, s = tiles[i]
                ut, utb = uts[cg]
                ct = s // 4
                ucols = slice(cc * T, (cc + 1) * T)
                tl0 = cg * 1024 + cc * T
                own = tl0 >= c.OWN0
                chn = cg * 4 + cc
                if own and s == 0:
                    yf, yfb, _ = ygf.next()
                    yb, ybb, _ = ygb.next()
                    ych[chn] = (yf, yfb, yb, ybb)
                t13, t13b, t42, t42b, cosb, sinb = st1.pop(i)
                ww, wwb, _ = wwr.next()
                z, zb_, _ = zr.next()
                S.op("dve", lambda e: e.scalar_tensor_tensor(out=ww[:, 0:T], in0=t13[:, 0:T], scalar=1.0,
                                                             in1=t42[:, T:2 * T], op0=ALU.mult, op1=ALU.add),
                     reads=[t13b, t42b], writes=[wwb])
                S.op("dve", lambda e: e.scalar_tensor_tensor(out=ww[:, T:2 * T], in0=t13[:, T:2 * T], scalar=1.0,
                                                             in1=t42[:, 0:T], op0=ALU.mult, op1=ALU.subtract),
                     reads=[t13b, t42b], writes=[wwb])
                rb = P[:, 6, s:s + 1].to_broadcast([128, T])
                for ri in range(2):
                    S.op("dve", lambda e, ri=ri: e.tensor_tensor_scan(
                        out=z[:, ri * T:(ri + 1) * T], data0=rb, data1=ww[:, ri * T:(ri + 1) * T],
                        initial=P[:, 25 + ri, s:s + 1], op0=ALU.mult, op1=ALU.add),
                        reads=[wwb, Pb, zib], writes=[zb_])
                    S.op("act", lambda e, ri=ri: e.activation(
                        out=zl[:, ri, s:s + 1], in_=z[:, (ri + 1) * T - 1:(ri + 1) * T], func=AF.Copy),
                        reads=[zb_], writes=[zlb])
                zh, zhb, _ = zhr.next()
                S.op("act", lambda e: e.activation(out=zh[:], in_=z[:], func=AF.Copy),
                     reads=[zb_], writes=[zhb])
                st2[i] = (zh, zhb, cosb, sinb)
                if s == NST - 1:
                    def cv(fn):
                        S.op("dve", fn, reads=[Pb, zlb, zib], writes=[Pb])
                    cv(lambda e: e.tensor_tensor(out=pv(22), in0=pv(20), in1=zl[:, 0, :], op=ALU.mult))
                    cv(lambda e: e.tensor_tensor(out=pv(23), in0=pv(21), in1=zl[:, 1, :], op=ALU.mult))
                    cv(lambda e: e.tensor_tensor(out=pv(24), in0=pv(20), in1=zl[:, 1, :], op=ALU.mult))
                    cv(lambda e: e.tensor_tensor(out=pv(27), in0=pv(21), in1=zl[:, 0, :], op=ALU.mult))
                    S.op("dve", lambda e: e.tensor_tensor(out=pv(25), in0=pv(22), in1=pv(23), op=ALU.subtract),
                         reads=[Pb], writes=[Pb, zib])
                    S.op("dve", lambda e: e.tensor_tensor(out=pv(26), in0=pv(24), in1=pv(27), op=ALU.add),
                         reads=[Pb], writes=[Pb, zib])

            def stage3(i):
                cg, cc, s = tiles[i]
                ut, utb = uts[cg]
                ct = s // 4
                ucols = slice(cc * T, (cc + 1) * T)
                tl0 = cg * 1024 + cc * T
                own = True
                chn = cg * 4 + cc
                zh, zhb, cosb, sinb = st2.pop(i)
                zb_ = zhb
                if own:
                    yf, yfb, yb, ybb = ych[chn]
                    q1, q1b, _ = q1r.next()
                    q2, q2b, _ = q2r.next()
                    z3 = zh[:, :].rearrange("p (two t) -> p two t", two=2)
                    S.op("dve", lambda e: e.tensor_tensor(
                        out=q1[:, :].rearrange("p (two t) -> p two t", two=2), in0=z3, in1=cosb,
                        op=ALU.mult), reads=[zb_, tabb], writes=[q1b])
                    S.op("dve", lambda e: e.tensor_tensor(
                        out=q2[:, :].rearrange("p (two t) -> p two t", two=2), in0=z3, in1=sinb,
                        op=ALU.mult), reads=[zb_, tabb], writes=[q2b])
                    st3[i] = (q1, q1b, q2, q2b)

            def stage4(i):
                cg, cc, s = tiles[i]
                ut, utb = uts[cg]
                ct = s // 4
                ucols = slice(cc * T, (cc + 1) * T)
                tl0 = cg * 1024 + cc * T
                own = True
                chn = cg * 4 + cc
                q1, q1b, q2, q2b = st3.pop(i)
                if own:
                    yf, yfb, yb, ybb = ych[chn]
                    yps, ypsb = ybank[yidx[0] % 2]
                    terms = [(0, q1, q1b, 0), (2, q2, q2b, T), (1, q2, q2b, 0), (1, q1, q1b, T)]
                    for ti, (cp_, q_, qb_, o_) in enumerate(terms):
                        S.op("pe", lambda e, cp_=cp_, q_=q_, o_=o_, ti=ti: e.matmul(
                            yps[:, 0:T], lhsT=Cm[:, cp_, s, :], rhs=q_[:, o_:o_ + T],
                            start=(s % 4 == 0 and ti == 0), stop=(s % 4 == 3 and ti == 3)),
                            reads=[Cmb, qb_], writes=[ypsb])
                    if s % 4 == 3:
                        yidx[0] += 1
                        yv = yf[:, ct, :]
                        t1, t1b, _ = tmp.next()
                        S.op("dve", lambda e: e.scalar_tensor_tensor(
                            out=yv, in0=ut[:, ct, ucols], scalar=self.dskip[:, ct:ct + 1],
                            in1=yps[:, 0:T], op0=ALU.mult, op1=ALU.add),
                            reads=[utb, ypsb], writes=[yfb])
                        S.op("pool", lambda e: e.tensor_tensor(out=t1[:], in0=yv, in1=yv, op=ALU.mult),
                             reads=[yfb], writes=[t1b])
                        S.op("act", lambda e: e.activation(out=t1[:], in_=t1[:], func=AF.Copy,
                                                           scale=0.044715, bias=1.0),
                             reads=[t1b], writes=[t1b])
                        S.op("pool", lambda e: e.tensor_tensor(out=t1[:], in0=t1[:], in1=yv, op=ALU.mult),
                             reads=[t1b, yfb], writes=[t1b])
                        S.op("act", lambda e: e.activation(out=t1[:], in_=t1[:], func=AF.Sigmoid,
                                                           scale=1.5957691216),
                             reads=[t1b], writes=[t1b])
                        S.op("pool", lambda e: e.tensor_tensor(out=yv, in0=yv, in1=t1[:], op=ALU.mult),
                             reads=[t1b, yfb], writes=[yfb])
                        S.op("act", lambda e: e.activation(out=yb[:, ct, :], in_=yv, func=AF.Copy),
                             reads=[yfb], writes=[ybb])
                if s == NST - 1:
                    if own:
                        yf, yfb, yb, ybb = ych.pop(chn)
                        to0 = tl0 - c.OWN0
                        for co in range(CT):
                            ps, psb = glu_ps, self.pbt.b[0]
                            for ct2 in range(CT):
                                S.op("pe", lambda e, ct2=ct2: e.matmul(
                                    ps[:, 0:T], lhsT=wg[:, ct2, co * 128:(co + 1) * 128], rhs=yb[:, ct2, :],
                                    start=(ct2 == 0), stop=(ct2 == CT - 1)),
                                    reads=[wgb, ybb], writes=[psb])
                            t1, t1b, _ = tmp.next()
                            S.op("act", lambda e: e.activation(out=t1[:], in_=ps[:, 0:T], func=AF.Sigmoid,
                                                               bias=self.bglu[:, co:co + 1]),
                                 reads=[psb], writes=[t1b])
                            sg, sgb, ssem = y2s.next()
                            S.op("pool", lambda e: e.tensor_tensor(out=sg[:], in0=yf[:, co, :], in1=t1[:],
                                                                   op=ALU.mult),
                                 reads=[yfb, t1b], writes=[sgb])
                            S.dma("sp", self.y2T_d[co * 128:(co + 1) * 128, to0:to0 + T], sg[:], ssem,
                                  reads=[sgb])

            glu_ps = self.pbt.t[0][:].bitcast(F32)
            NT_ = len(tiles)
            for i in range(-2, NT_ + 2):
                if 0 <= i + 2 < NT_:
                    stage1(i + 2)
                if 0 <= i + 1 < NT_:
                    stage1b(i + 1)
                if 0 <= i < NT_:
                    stage2(i)
                if 0 <= i - 1 < NT_:
                    stage3(i - 1)
                if 0 <= i - 2 < NT_:
                    stage4(i - 2)
            S.barrier()

    def phase_B(self):
        c, S = self.c, self.S
        L, NH, NKT, OWN = c.L, c.NH, c.NKT, c.OWN
        NQT = OWN // 128
        NSB = OWN // 512
        NB = NKT // 4
        K0 = c.OWN0 // 128
        scale = 128 ** -0.5
        with ExitStack() as st:
            ktr = Ring(self, st, "ktr", 2, [128, L], BF16)
            vr = Ring(self, st, "vr", 2, [128, NKT, 129], BF16)
            qr = Ring(self, st, "qr", 2, [128, OWN], BF16)
            ptr_ = Ring(self, st, "ptr", 12, [128, 512], BF16, dma=False)
            accs = []
            for i in range(2):
                t = self.sb(st, f"acc{i}", [128, 4, 129], F32)
                accs.append((t, [Buf(f"acc{i}_{q}") for q in range(4)]))
            tabs = []
            for i in range(2):
                tabs.append(dict(
                    rE=self.sb(st, f"rE{i}", [128, NKT], F32), rM=self.sb(st, f"rM{i}", [128, NKT], F32),
                    bO=self.sb(st, f"bO{i}", [128, NKT], F32), bD=self.sb(st, f"bD{i}", [128, NKT], F32),
                    fO=self.sb(st, f"fO{i}", [128, NQT, NB], F32), fD=self.sb(st, f"fD{i}", [128, NQT, 4], F32),
                    b=Buf(f"btab{i}")))
            obr = Ring(self, st, "obr", 4, [128, 128], BF16, dma=False)
            aos = Ring(self, st, "aos", 2, [128, 512], BF16)
            recr = Ring(self, st, "recr", 4, [128, 1], F32, dma=False)
            for i in range(2):
                S.op("pool", lambda e, i=i: e.memset(vr.t[i][:, :, 128:129], 1.0), writes=[vr.b[i]])
            stb = Ring.__new__(Ring)
            st4 = self.pbt.t[1][:].bitcast(F32)
            stb.t = [self.pf.t[0][:, :], self.pf.t[1][:, :], self.pf.t[2][:, :], st4]
            stb.b, stb.s, stb.i, stb.n = self.pf.b[0:3] + [self.pbt.b[1]], [None] * 4, 0, 4
            poslots = [(self.pf.t[3], 0, self.pf.b[3]), (self.pf.t[4], 0, self.pf.b[4]),
                       (self.pf.t[5], 0, self.pf.b[5])]
            poi = [0]

            def next_po():
                t, o, b = poslots[poi[0] % len(poslots)]
                poi[0] += 1
                return t[:, o:o + 129], b

            steps = []
            for h in range(NH):
                for sb_ in range(NSB):
                    nboff = (c.OWN0 + sb_ * 512) // 512
                    for B in range(nboff):
                        steps.append((h, sb_, B, False, B == 0, False))
                    steps.append((h, sb_, nboff, True, nboff == 0, True))
            heads, fr = {}, {}

            def load_head(h):
                kt_, ktb, ks = ktr.next()
                v_, vb, vs = vr.next()
                q_, qb, qs = qr.next()
                S.dma("sp", kt_[:], self.KT_d[h, :, :], ks, writes=[ktb])
                vsrc = self.V_d.rearrange("(kt p) n -> p kt n", p=128)
                step = max(1, NKT // 4)
                for a in range(0, NKT, step):
                    S.dma("sp", v_[:, a:a + step, 0:128], vsrc[:, a:a + step, h * 128:(h + 1) * 128], vs,
                          writes=[vb])
                S.dma("sp", q_[:], self.QT_d[h, :, :], qs, writes=[qb])
                heads[h] = (kt_, ktb, v_, vb, q_, qb)

            def prologue(h):
                T_ = tabs[h % 2]
                tb = T_["b"]
                rE, rM, bO, bD, fO, fD = T_["rE"], T_["rM"], T_["bO"], T_["bD"], T_["fO"], T_["fD"]
                ch = self.c_tm[:, :, h]
                ps, psb, _ = stb.next()
                S.op("pe", lambda e: e.matmul(ps[:, 0:NKT], lhsT=self.sel127[:], rhs=ch, start=True, stop=True),
                     reads=[self.c_tmb], writes=[psb])
                S.op("pe", lambda e: e.matmul(ps[:, NKT:2 * NKT], lhsT=self.sel63[:], rhs=ch, start=True, stop=True),
                     reads=[self.c_tmb], writes=[psb])
                S.op("act", lambda e: e.activation(out=rE[:], in_=ps[:, 0:NKT], func=AF.Copy),
                     reads=[psb], writes=[tb])
                S.op("act", lambda e: e.activation(out=rM[:], in_=ps[:, NKT:2 * NKT], func=AF.Copy),
                     reads=[psb], writes=[tb])
                rE4 = rE[:, :].rearrange("p (b f) -> p b f", f=4)
                S.op("dve", lambda e: e.tensor_tensor(
                    out=bO[:, :].rearrange("p (b f) -> p b f", f=4),
                    in0=rE4[:, :, 3:4].to_broadcast([128, NB, 4]),
                    in1=ch.rearrange("p (b f) -> p b f", f=4), op=ALU.subtract),
                    reads=[tb, self.c_tmb], writes=[tb])
                S.op("dve", lambda e: e.tensor_tensor(out=bO[:], in0=bO[:], in1=self.kbias[:], op=ALU.add),
                     reads=[tb], writes=[tb])
                S.op("dve", lambda e: e.tensor_tensor(out=bD[:], in0=rM[:], in1=ch, op=ALU.subtract),
                     reads=[tb, self.c_tmb], writes=[tb])
                S.op("dve", lambda e: e.tensor_tensor(out=bD[:], in0=bD[:], in1=self.kbias[:], op=ALU.add),
                     reads=[tb], writes=[tb])
                for qt in range(NQT):
                    cq = self.c_tm[:, K0 + qt, h:h + 1]
                    S.op("act", lambda e, qt=qt, cq=cq: e.activation(
                        out=fO[:, qt, :], in_=rE4[:, :, 3], func=AF.Exp, scale=-1.0, bias=cq),
                        reads=[tb, self.c_tmb], writes=[tb])
                    kd = K0 + 4 * (qt // 4)
                    S.op("act", lambda e, qt=qt, cq=cq, kd=kd: e.activation(
                        out=fD[:, qt, :], in_=rM[:, kd:kd + 4], func=AF.Exp, scale=-1.0, bias=cq),
                        reads=[tb, self.c_tmb], writes=[tb])

            def front(i):
                h, sb_, B, diag, first, last = steps[i]
                if sb_ == 0 and B == 0:
                    if h == 0:
                        load_head(0)
                    prologue(h)
                kt_, ktb, v_, vb, q_, qb = heads[h]
                T_ = tabs[h % 2]
                btab = T_["bD"] if diag else T_["bO"]
                qcols = slice(sb_ * 512, (sb_ + 1) * 512)
                pts = []
                for j in range(4):
                    kt = 4 * B + j
                    ps, psb, _ = stb.next()
                    S.op("pe", lambda e, kt=kt: e.matmul(ps[:, 0:512], lhsT=kt_[:, kt * 128:(kt + 1) * 128],
                                                         rhs=q_[:, qcols], start=True, stop=True),
                         reads=[ktb, qb], writes=[psb])
                    pt, ptb, _ = ptr_.next()
                    S.op("act", lambda e, kt=kt: e.activation(out=pt[:], in_=ps[:, 0:512], func=AF.Exp,
                                                              scale=scale, bias=btab[:, kt:kt + 1]),
                         reads=[psb, T_["b"]], writes=[ptb])
                    if diag:
                        S.op("pool", lambda e, j=j: e.tensor_tensor(
                            out=pt[:, j * 128:(j + 1) * 128], in0=pt[:, j * 128:(j + 1) * 128],
                            in1=self.trib[:], op=ALU.mult), reads=[ptb], writes=[ptb])
                    pts.append((pt, ptb))
                fr[i] = pts

            def back(i):
                h, sb_, B, diag, first, last = steps[i]
                if sb_ == 0 and B == 0 and h + 1 < NH:
                    load_head(h + 1)
                kt_, ktb, v_, vb, q_, qb = heads[h]
                T_ = tabs[h % 2]
                tb = T_["b"]
                acc, accb = accs[(h * NSB + sb_) % 2]
                if first:
                    S.op("pool", lambda e: e.memset(acc[:], 0.0), writes=accb)
                pts = fr.pop(i)
                if not diag:
                    for qt in range(4):
                        po, pob = next_po()
                        for j in range(4):
                            pt, ptb = pts[j]
                            S.op("pe", lambda e, pt=pt, j=j, qt=qt: e.matmul(
                                po, lhsT=pt[:, qt * 128:(qt + 1) * 128], rhs=v_[:, 4 * B + j, :],
                                start=(j == 0), stop=(j == 3)), reads=[ptb, vb], writes=[pob])
                        qg = sb_ * 4 + qt
                        S.op("dve", lambda e, qt=qt, qg=qg: e.scalar_tensor_tensor(
                            out=acc[:, qt, :], in0=po, scalar=T_["fO"][:, qg, B:B + 1],
                            in1=acc[:, qt, :], op0=ALU.mult, op1=ALU.add),
                            reads=[pob, tb, accb[qt]], writes=[accb[qt]])
                else:
                    for j in range(4):
                        pt, ptb = pts[j]
                        kt = 4 * B + j
                        for qt in range(j, 4):
                            po, pob = next_po()
                            S.op("pe", lambda e, qt=qt, pt=pt, kt=kt: e.matmul(
                                po, lhsT=pt[:, qt * 128:(qt + 1) * 128], rhs=v_[:, kt, :],
                                start=True, stop=True), reads=[ptb, vb], writes=[pob])
                            qg = sb_ * 4 + qt
                            S.op("dve", lambda e, qt=qt, qg=qg, j=j: e.scalar_tensor_tensor(
                                out=acc[:, qt, :], in0=po, scalar=T_["fD"][:, qg, j:j + 1],
                                in1=acc[:, qt, :], op0=ALU.mult, op1=ALU.add),
                                reads=[pob, tb, accb[qt]], writes=[accb[qt]])
                if last:
                    pb, pbb = self.pbt.t[0], self.pbt.b[0]
                    for qt in range(4):
                        rc, rcb, _ = recr.next()
                        ob, obb, _ = obr.next()
                        S.op("dve", lambda e, qt=qt: e.reciprocal(out=rc[:], in_=acc[:, qt, 128:129]),
                             reads=[accb[qt]], writes=[rcb])
                        S.op("dve", lambda e, qt=qt: e.tensor_scalar(
                            out=ob[:], in0=acc[:, qt, 0:128], scalar1=rc[:, 0:1], scalar2=None, op0=ALU.mult),
                            reads=[accb[qt], rcb], writes=[obb])
                        S.op("pe", lambda e, qt=qt: e.transpose(out=pb[:, qt * 128:(qt + 1) * 128], in_=ob[:],
                                                                identity=self.identb[:]),
                             reads=[obb], writes=[pbb])
                    ao, aob, asem = aos.next()
                    S.op("act", lambda e: e.activation(out=ao[:], in_=pb[:, 0:512], func=AF.Copy),
                         reads=[pbb], writes=[aob])
                    S.dma("sp", self.aoT_d[h * 128:(h + 1) * 128, sb_ * 512:(sb_ + 1) * 512], ao[:], asem,
                          reads=[aob])

            front(0)
            for i in range(len(steps)):
                if i + 1 < len(steps):
                    front(i + 1)
                back(i)
            S.barrier()

    def phase_blocks(self):
        c, S = self.c, self.S
        D, KT, CT, XT, XW = c.D, c.KT, c.CT, c.XT, c.XW
        MT = c.MEM // 128
        with ExitStack() as st:
            self.slabs = Ring(self, st, "slabB", 3, [128, 32, 256], BF16)
            xcur = self.sb(st, "xcur", [128, 4, D], F32)
            xb = [Buf(f"xcur{i}") for i in range(4)]
            xsems = [self.getsem(f"xcur{i}") for i in range(4)]
            actA = self.sb(st, "actA", [128, KT, 512], BF16)
            actAb = Buf("actA")
            actB = self.sb(st, "actB", [128, 16, 512], BF16)
            actBb = Buf("actB")
            HBN = max(D, CT * 512, 4 * XW)
            hb = self.sb(st, "hbB", [128, HBN], BF16)
            hbb = Buf("hbB")
            hsem = self.getsem("hbB")
            asem = self.getsem("actBld")
            gts = Ring(self, st, "gts", 3, [128, 2, 512], BF16)
            tmp = Ring(self, st, "btmp", 4, [128, 512], F32, dma=False)
            ptx = Ring(self, st, "ptx", 4, [128, 512], BF16, dma=False)
            recr = Ring(self, st, "recx", 4, [128, 1], F32, dma=False)
            kxT = self.sb(st, "kxT", [128, XT, c.MEM], BF16)
            kxb = Buf("kxT")
            vx = self.sb(st, "vx", [128, MT, c.XH, 257], BF16)
            vxb = Buf("vx")
            gsem = self.getsem("gfin")
            actAf = actA[:].rearrange("p a b -> p (a b)").bitcast(F32)

            items = []
            xst_done = []

            def mem_norm(tile, buf):
                def get_src(tt):
                    S.dma("sp", xcur[:, tt, :], self.meml[tt * 128:(tt + 1) * 128, :], xsems[tt], writes=[xb[tt]])
                    return xcur[:, tt, :], xb[tt]
                self.norm_T(get_src, MT, 2 * KT, hb, hbb, actA, actAb)
                S.op("pool", lambda e: e.memset(vx[:, :, :, 256:257], 1.0), writes=[vxb])
            items.append((None, mem_norm))

            def epi_kx(ci, ps, psb):
                S.op("act", lambda e: e.activation(out=kxT[:, ci, :], in_=ps[:, 0:c.MEM], func=AF.Copy),
                     reads=[psb], writes=[kxb])

            def epi_vx(tt, s0, n_, ps, psb):
                hx = s0 // 256
                S.op("act", lambda e: e.activation(out=vx[:, tt, hx, 0:256], in_=ps[:, 0:n_], func=AF.Copy),
                     reads=[psb], writes=[vxb])
            self.items_fm(items, self.wk_x, 0, XW, KT, actA, actAb, c.MEM, epi_kx)
            self.items_tm(items, self.wv_x, 0, KT, 0, XW, actA, actAb, MT, epi_vx)

            for ob in range(c.OWN // 512):
                to0 = ob * 512
                tl0 = c.OWN0 + to0
                def c_load(tile, buf, to0=to0, tl0=tl0):
                    S.dma("sp", actB[:, 0:c.AW // 128, :],
                          self.aoT_d.rearrange("(kt p) t -> p kt t", p=128)[:, :, to0:to0 + 512], asem,
                          writes=[actBb])
                    S.dma("sp", hb[:, 0:CT * 512].rearrange("p (k t) -> p k t", k=CT),
                          self.y2T_d.rearrange("(kt p) t -> p kt t", p=128)[:, :, to0:to0 + 512], hsem,
                          writes=[hbb])
                    for tt in range(4):
                        S.dma("sp", xcur[:, tt, :], self.xl[tl0 + tt * 128:tl0 + (tt + 1) * 128, :], xsems[tt],
                              writes=[xb[tt]])
                items.append((None, c_load))
                y2v = hb[:, 0:CT * 512].rearrange("p (k t) -> p k t", k=CT)
                NA = c.AW // 128
                for s0 in range(0, D, 256):
                    st_ = {}

                    def fnA(tile, buf, st_=st_, s0=s0, to0=to0):
                        for ci in range(2):
                            j = s0 // 128 + ci
                            g, gb, gs = gts.next()
                            S.dma("sp", g[:, 0, :], self.gT_d[j * 128:(j + 1) * 128, to0:to0 + 512], gs, writes=[gb])
                            S.dma("sp", g[:, 1, :], self.gT_d[D + j * 128:D + (j + 1) * 128, to0:to0 + 512], gs,
                                  writes=[gb])
                            ps, psb, _ = self.pf.next()
                            for kt in range(NA):
                                S.op("pe", lambda e, kt=kt, ci=ci: e.matmul(
                                    ps[:, 0:512], lhsT=tile[:, kt, ci * 128:(ci + 1) * 128], rhs=actB[:, kt, :],
                                    start=(kt == 0), stop=(kt == NA - 1)), reads=[buf, actBb], writes=[psb])
                            t1, t1b, _ = tmp.next()
                            S.op("dve", lambda e: e.tensor_tensor(out=t1[:], in0=ps[:, 0:512], in1=g[:, 0, :],
                                                                  op=ALU.mult), reads=[psb, gb], writes=[t1b])
                            st_[ci] = (t1, t1b, g, gb)

                    def fnS(tile, buf, st_=st_, s0=s0):
                        for ci in range(2):
                            j = s0 // 128 + ci
                            t1, t1b, g, gb = st_[ci]
                            ps, psb, _ = self.pf.next()
                            for kt in range(CT):
                                S.op("pe", lambda e, kt=kt, ci=ci: e.matmul(
                                    ps[:, 0:512], lhsT=tile[:, kt, ci * 128:(ci + 1) * 128], rhs=y2v[:, kt, :],
                                    start=(kt == 0), stop=(kt == CT - 1)), reads=[buf, hbb], writes=[psb])
                            t2, t2b, _ = tmp.next()
                            S.op("dve", lambda e: e.tensor_tensor(out=t2[:], in0=ps[:, 0:512], in1=g[:, 1, :],
                                                                  op=ALU.mult), reads=[psb, gb], writes=[t2b])
                            S.op("pool", lambda e, j=j: e.tensor_tensor(out=actA[:, j, :], in0=t1[:], in1=t2[:],
                                                                        op=ALU.add),
                                 reads=[t1b, t2b], writes=[actAb])

                    items.append(((self.w_attn_up, 0, NA, s0, 256), fnA))
                    items.append(((self.w_ssm_up, 0, CT, s0, 256), fnS))

                def epi_res(tt, s0, n_, ps, psb):
                    S.op("dve", lambda e: e.tensor_tensor(out=xcur[:, tt, s0:s0 + n_], in0=ps[:, 0:n_],
                                                          in1=xcur[:, tt, s0:s0 + n_], op=ALU.add),
                         reads=[psb, xb[tt]], writes=[xb[tt]])
                self.items_tm(items, self.w_out, 0, KT, 0, D, actA, actAb, 4, epi_res)

                def d_norm(tile, buf):
                    self.norm_T(lambda tt: (xcur[:, tt, :], xb[tt]), 4, KT, hb, hbb, actA, actAb)
                items.append((None, d_norm))

                def epi_qx(ci, ps, psb):
                    S.op("act", lambda e: e.activation(out=actB[:, ci, :], in_=ps[:, 0:512], func=AF.Copy),
                         reads=[psb], writes=[actBb])
                self.items_fm(items, self.wq_x, 0, XW, KT, actA, actAb, 512, epi_qx)
                oxv = hb[:, 0:4 * XW].rearrange("p (t w) -> p t w", t=4)

                def d_attn(tile, buf):
                    for hx in range(c.XH):
                        pts = []
                        for mt in range(MT):
                            ps, psb, _ = self.pf.next()
                            for dt_ in range(2):
                                S.op("pe", lambda e, dt_=dt_, mt=mt: e.matmul(
                                    ps[:, 0:512], lhsT=kxT[:, hx * 2 + dt_, mt * 128:(mt + 1) * 128],
                                    rhs=actB[:, hx * 2 + dt_, :], start=(dt_ == 0), stop=(dt_ == 1)),
                                    reads=[kxb, actBb], writes=[psb])
                            pt, ptb, _ = ptx.next()
                            S.op("act", lambda e: e.activation(out=pt[:], in_=ps[:, 0:512], func=AF.Exp,
                                                               scale=1.0 / 16.0), reads=[psb], writes=[ptb])
                            pts.append((pt, ptb))
                        for qt in range(4):
                            po, pob, _ = self.pf.next()
                            for mt in range(MT):
                                pt, ptb = pts[mt]
                                S.op("pe", lambda e, mt=mt, pt=pt, qt=qt: e.matmul(
                                    po[:, 0:257], lhsT=pt[:, qt * 128:(qt + 1) * 128], rhs=vx[:, mt, hx, :],
                                    start=(mt == 0), stop=(mt == MT - 1)), reads=[ptb, vxb], writes=[pob])
                            rc, rcb, _ = recr.next()
                            S.op("dve", lambda e: e.reciprocal(out=rc[:], in_=po[:, 256:257]),
                                 reads=[pob], writes=[rcb])
                            S.op("dve", lambda e, qt=qt: e.tensor_scalar(
                                out=oxv[:, qt, hx * 256:(hx + 1) * 256], in0=po[:, 0:256], scalar1=rc[:, 0:1],
                                scalar2=None, op0=ALU.mult), reads=[pob, rcb], writes=[hbb])
                    for qt in range(4):
                        pb, pbb, _ = self.pbt.next()
                        for j in range(XT):
                            S.op("pe", lambda e, j=j, qt=qt: e.transpose(
                                out=pb[:, j * 128:(j + 1) * 128], in_=oxv[:, qt, j * 128:(j + 1) * 128],
                                identity=self.identb[:]), reads=[hbb], writes=[pbb])
                        S.op("act", lambda e, qt=qt: e.activation(
                            out=actB[:, 8:8 + XT, qt * 128:(qt + 1) * 128],
                            in_=pb[:, 0:XT * 128].rearrange("p (k t) -> p k t", k=XT), func=AF.Copy),
                            reads=[pbb], writes=[actBb])
                items.append((None, d_attn))
                self.items_tm(items, self.wo_x, 0, XT, 0, D, actB[:, 8:8 + XT, :], actBb, 4, epi_res)

                def e_norm(tile, buf):
                    self.norm_T(lambda tt: (xcur[:, tt, :], xb[tt]), 4, 3 * KT, hb, hbb, actA, actAb)
                items.append((None, e_norm))
                FE = c.FE
                for q in range(c.NE):
                    def epi_h(ci, ps, psb):
                        t1, t1b, _ = tmp.next()
                        S.op("act", lambda e: e.activation(out=t1[:], in_=ps[:, 0:512], func=AF.Relu),
                             reads=[psb], writes=[t1b])
                        S.op("pool", lambda e: e.tensor_tensor(out=actB[:, ci, :], in0=t1[:], in1=t1[:],
                                                               op=ALU.mult), reads=[t1b], writes=[actBb])
                    self.items_fm(items, self.w_ff1, q * FE * 128, FE * 128, KT, actA, actAb, 512, epi_h)
                    self.items_tm(items, self.w_ff2, q * FE, FE, 0, D, actB, actBb, 4, epi_res)

                def fin(tile, buf, to0=to0):
                    gv = actAf[:, 0:D]
                    S.dma("sp", gv, self.g_final_d.partition_broadcast(128), gsem, reads=[], writes=[actAb])
                    for tt in range(4):
                        ss, ssb, _ = self.small.next()
                        S.op("act", lambda e, tt=tt: e.activation(out=hb[:, 0:D], in_=xcur[:, tt, :], func=AF.Square,
                                                                   accum_out=ss[:, 0:1]),
                             reads=[xb[tt]], writes=[hbb, ssb])
                        S.op("dve", lambda e: e.tensor_scalar(out=ss[:, 1:2], in0=ss[:, 0:1], scalar1=1.0 / D,
                                                              scalar2=EPS, op0=ALU.mult, op1=ALU.add),
                             reads=[ssb], writes=[ssb])
                        S.op("pool", lambda e: e.tensor_tensor(out=ss[:, 2:3], in0=ss[:, 1:2],
                                                               in1=self.cst[:, 0:1], op=ALU.pow),
                             reads=[ssb], writes=[ssb])
                        S.op("dve", lambda e, tt=tt: e.scalar_tensor_tensor(
                            out=xcur[:, tt, :], in0=xcur[:, tt, :], scalar=ss[:, 2:3], in1=gv,
                            op0=ALU.mult, op1=ALU.mult), reads=[xb[tt], ssb, actAb], writes=[xb[tt]])
                        S.dma("sp", self.out_d[to0 + tt * 128:to0 + (tt + 1) * 128, :], xcur[:, tt, :], xsems[tt],
                              reads=[xb[tt]])
                items.append((None, fin))
            self.run_items(items)
            S.barrier()


def _const_mats():
    ident = np.eye(128, dtype=np.float32)
    k = np.arange(128)
    tri = (k[:, None] <= k[None, :]).astype(np.float32)
    sel127 = np.zeros((128, 128), np.float32)
    sel127[127, :] = 1.0
    sel63 = np.zeros((128, 128), np.float32)
    sel63[63, :] = 1.0
    return np.concatenate([ident, tri, sel127, sel63], axis=1)


def _col(v, kt):
    return np.ascontiguousarray(np.asarray(v, np.float32).reshape(kt, 128).T)


def make_core_inputs(cfg, xl, kvalid0, meml, p):
    c = cfg
    kb = np.zeros(c.L, np.float32)
    kb[:kvalid0] = -30000.0
    f32 = lambda a: np.ascontiguousarray(np.asarray(a, np.float32))
    sm = lambda a: np.ascontiguousarray(np.asarray(a, np.float32).reshape(c.NST, 128).T)
    ldt = np.repeat(np.asarray(p["log_dt"], np.float32)[:, None], 64, axis=1)
    s5p = np.concatenate([sm(p["A_re"]), sm(p["A_im"]), sm(ldt)], axis=1)

    def bl(a):
        a = np.asarray(a, np.float32).reshape(c.NST, 2, 64, 16)
        return a.transpose(1, 2, 0, 3).reshape(128, c.NST, 16)

    def cl(a):
        a = np.asarray(a, np.float32).transpose(0, 2, 1).reshape(c.NST, 2, 64, 16)
        return a.transpose(1, 2, 0, 3).reshape(128, c.NST, 16)

    return {
        "xl": f32(xl), "meml": f32(meml), "kbias": _col(kb, c.NKT),
        "w_in": f32(p["w_in"]), "w_glu": f32(p["w_glu"]), "w_attn_up": f32(p["w_attn_up"]),
        "w_ssm_up": f32(p["w_ssm_up"]), "w_out": f32(p["w_out"]), "wq_x": f32(p["wq_x"]),
        "wk_x": f32(p["wk_x"]), "wv_x": f32(p["wv_x"]), "wo_x": f32(p["wo_x"]),
        "w_ff1": f32(p["w_ff1"]), "w_ff2": f32(p["w_ff2"]),
        "gcols": np.concatenate([_col(p["g_mix"], c.KT), _col(p["g_xattn"], c.KT),
                                 _col(p["g_mem"], c.KT), _col(p["g_mlp"], c.KT)], axis=1),
        "g_final": f32(p["g_final"]),
        "b_f": f32(np.asarray(p["b_f"]).reshape(c.NH, 1)),
        "b_gate_t": _col(p["b_gate"], 2 * c.KT), "b_glu_t": _col(p["b_glu"], c.CT),
        "dskip_t": _col(np.asarray(p["D_skip"]).reshape(-1), c.CT),
        "s5p": np.ascontiguousarray(s5p),
        "s5b": np.ascontiguousarray(np.stack([bl(p["B_re"]), bl(p["B_im"])], axis=1)),
        "s5c": np.ascontiguousarray(np.stack([cl(p["C_re"]), cl(p["C_im"])], axis=1)),
        "cmat": _const_mats(),
    }


_NC_CACHE = {}


def kernel(x, mem, g_mix, w_in, b_f, b_gate, A_re, A_im, log_dt, B_re, B_im, C_re, C_im,
           D_skip, w_glu, b_glu, w_attn_up, w_ssm_up, w_out, g_xattn, g_mem, wq_x, wk_x,
           wv_x, wo_x, g_mlp, w_ff1, w_ff2, g_final):
    cfg = FULL
    x = np.asarray(x, np.float32)
    mem = np.asarray(mem, np.float32)
    p = dict(g_mix=g_mix[0], w_in=w_in[0], b_f=b_f[0], b_gate=b_gate[0], A_re=A_re[0], A_im=A_im[0],
             log_dt=log_dt[0], B_re=B_re[0], B_im=B_im[0], C_re=C_re[0], C_im=C_im[0], D_skip=D_skip[0],
             w_glu=w_glu[0], b_glu=b_glu[0], w_attn_up=w_attn_up[0], w_ssm_up=w_ssm_up[0], w_out=w_out[0],
             g_xattn=g_xattn[0], g_mem=g_mem[0], wq_x=wq_x[0], wk_x=wk_x[0], wv_x=wv_x[0], wo_x=wo_x[0],
             g_mlp=g_mlp[0], w_ff1=w_ff1[0], w_ff2=w_ff2[0], g_final=g_final)
    p = {k: np.asarray(v, np.float32) for k, v in p.items()}
    B, SEQ, D = x.shape
    nq = SEQ // cfg.OWN
    in_maps = []
    shared = None
    for core in range(8):
        b, j = core // nq, core % nq
        xl = np.zeros((cfg.L, D), np.float32)
        n = (j + 1) * cfg.OWN
        xl[cfg.L - n:] = x[b, :n]
        m = make_core_inputs(cfg, xl, cfg.L - n, mem[b], p) if shared is None else None
        if shared is None:
            shared = m
        else:
            m = dict(shared)
            kb = np.zeros(cfg.L, np.float32)
            kb[:cfg.L - n] = -30000.0
            m["xl"] = xl
            m["meml"] = np.ascontiguousarray(mem[b])
            m["kbias"] = _col(kb, cfg.NKT)
        in_maps.append(m)
    if "full" not in _NC_CACHE:
        _NC_CACHE["full"] = Builder(cfg).build()
    nc = _NC_CACHE["full"]
    res = run_bass_kernel_spmd(nc, in_maps, core_ids=list(range(8)))
    out = np.zeros((B, SEQ, D), np.float32)
    for core in range(8):
        b, j = core // nq, core % nq
        out[b, j * cfg.OWN:(j + 1) * cfg.OWN] = res.results[core]["out"]
    return out
```

```python
import math
from contextlib import ExitStack

import numpy as np
import concourse.bass as bass
import concourse.mybir as mybir
from concourse.bass_utils import run_bass_kernel_spmd

F32 = mybir.dt.float32
BF16 = mybir.dt.bfloat16
AF = mybir.ActivationFunctionType
ALU = mybir.AluOpType
EPS = 1e-6
PI = math.pi


class Sem:
    def __init__(self, nc, name):
        self.h = nc.alloc_semaphore(name=name)
        self.c = 0


class Buf:
    __slots__ = ("name", "w", "r")

    def __init__(self, name):
        self.name = name
        self.w = None
        self.r = {}


class Sched:
    def __init__(self, nc):
        self.nc = nc
        self.eng = {"pe": nc.tensor, "act": nc.scalar, "dve": nc.vector,
                    "pool": nc.gpsimd, "sp": nc.sync}
        self.sem = {k: Sem(nc, "e_" + k) for k in self.eng}
        self.seen = {k: {} for k in self.eng}
        self.dsems = []
        self.ninst = 0

    def dsem(self, name):
        s = Sem(self.nc, name)
        self.dsems.append(s)
        return s

    def _deps(self, e, reads, writes, gen=None):
        need = {}
        own = self.sem.get(e)
        for b in reads:
            if b.w is not None:
                sm, v = b.w
                if need.get(sm, 0) < v:
                    need[sm] = v
        for b in writes:
            if b.w is not None and b.w[0] is not own and b.w[0] is not gen:
                sm, v = b.w
                if need.get(sm, 0) < v:
                    need[sm] = v
            for sm, v in b.r.items():
                if sm is not own and need.get(sm, 0) < v:
                    need[sm] = v
        seen = self.seen[e]
        for sm, v in need.items():
            if seen.get(sm, 0) < v:
                self.eng[e].wait_ge(sm.h, v)
                seen[sm] = v

    def _record(self, ev, reads, writes):
        sm, v = ev
        for b in reads:
            if b.r.get(sm, 0) < v:
                b.r[sm] = v
        for b in writes:
            b.w = ev
            b.r = {}

    def op(self, e, fn, reads=(), writes=()):
        self._deps(e, reads, writes)
        inst = fn(self.eng[e])
        sm = self.sem[e]
        sm.c += 1
        inst.then_inc(sm.h, 1)
        self._record((sm, sm.c), reads, writes)
        self.ninst += 1

    def dma(self, q, out, in_, sem, reads=(), writes=()):
        self._deps(q, reads, writes, gen=sem)
        inst = self.eng[q].dma_start(out=out, in_=in_)
        sem.c += 16
        inst.then_inc(sem.h, 16)
        self._record((sem, sem.c), reads, writes)
        self.ninst += 1

    def barrier(self):
        allsems = list(self.sem.values()) + self.dsems
        for e in self.eng:
            seen = self.seen[e]
            for sm in allsems:
                if sm.c > 0 and seen.get(sm, 0) < sm.c:
                    self.eng[e].wait_ge(sm.h, sm.c)
                    seen[sm] = sm.c


class Ring:
    def __init__(self, bld, stack, name, n, shape, dt, dma=True, psum=False):
        self.t, self.b, self.s = [], [], []
        for i in range(n):
            if psum:
                t = stack.enter_context(bld.nc.psum_tensor(f"ps_{name}{i}", list(shape), dt))
            else:
                t = stack.enter_context(bld.nc.sbuf_tensor(f"sb_{name}{i}", list(shape), dt))
            self.t.append(t)
            self.b.append(Buf(f"{name}{i}"))
            self.s.append(bld.getsem(f"{name}{i}") if dma else None)
        self.i = 0
        self.n = n

    def next(self):
        i = self.i % self.n
        self.i += 1
        return self.t[i], self.b[i], self.s[i]


class Cfg:
    def __init__(self, D, L, OWN, NH, SW, XH, MEM, DFF, debug=False):
        self.D, self.L, self.OWN, self.NH, self.SW = D, L, OWN, NH, SW
        self.XH, self.MEM, self.DFF, self.debug = XH, MEM, DFF, debug
        self.AW = NH * 128
        self.KT = D // 128
        self.G = SW // 16
        self.NST = self.G // 2
        self.CT = SW // 128
        self.XW = XH * 256
        self.XT = self.XW // 128
        self.OWN0 = L - OWN
        self.NKT = L // 128
        self.OFF_K = self.AW
        self.OFF_V = 2 * self.AW
        self.OFF_F = 3 * self.AW
        self.OFF_U = self.OFF_F + NH
        self.OFF_G = self.OFF_U + SW
        self.INW = self.OFF_G + 2 * D
        self.FE = min(16, DFF // 128)
        self.NE = DFF // (128 * self.FE)


FULL = Cfg(D=4096, L=8192, OWN=2048, NH=16, SW=1024, XH=4, MEM=256, DFF=16384)


class Builder:
    def __init__(self, cfg):
        self.c = c = cfg
        self.nc = nc = bass.Bass("TRN2", target_bir_lowering=False)
        self.S = Sched(nc)
        self._sempool = {}
        D, L, OWN = c.D, c.L, c.OWN

        def din(name, shape, dt=F32):
            return nc.dram_tensor(name, list(shape), dt, kind="ExternalInput").ap()

        def dscr(name, shape, dt):
            if c.debug:
                return nc.dram_tensor(name, list(shape), dt, kind="ExternalOutput").ap()
            return nc.dram_tensor(name, list(shape), dt).ap()

        self.xl = din("xl", [L, D])
        self.meml = din("meml", [c.MEM, D])
        self.kbias_d = din("kbias", [128, c.NKT])
        self.w_in = din("w_in", [D, c.INW])
        self.w_glu = din("w_glu", [c.SW, c.SW])
        self.w_attn_up = din("w_attn_up", [c.AW, D])
        self.w_ssm_up = din("w_ssm_up", [c.SW, D])
        self.w_out = din("w_out", [D, D])
        self.wq_x = din("wq_x", [D, c.XW])
        self.wk_x = din("wk_x", [D, c.XW])
        self.wv_x = din("wv_x", [D, c.XW])
        self.wo_x = din("wo_x", [c.XW, D])
        self.w_ff1 = din("w_ff1", [D, c.DFF])
        self.w_ff2 = din("w_ff2", [c.DFF, D])
        self.gcols_d = din("gcols", [128, 4 * c.KT])
        self.g_final_d = din("g_final", [D])
        self.bf_d = din("b_f", [c.NH, 1])
        self.bgate_d = din("b_gate_t", [128, 2 * c.KT])
        self.bglu_d = din("b_glu_t", [128, c.CT])
        self.dskip_d = din("dskip_t", [128, c.CT])
        self.s5p_d = din("s5p", [128, 3 * c.NST])
        self.s5b_d = din("s5b", [128, 2, c.NST, 16])
        self.s5c_d = din("s5c", [128, 2, c.NST, 16])
        self.cmat_d = din("cmat", [128, 4 * 128])
        self.out_d = nc.dram_tensor("out", [OWN, D], F32, kind="ExternalOutput").ap()

        self.KT_d = dscr("KT_d", [c.NH, 128, L], BF16)
        self.V_d = dscr("V_d", [L, c.AW], BF16)
        self.uT_d = dscr("uT_d", [c.SW, L], BF16)
        self.f_d = dscr("f_d", [c.NH, L], F32)
        self.QT_d = dscr("QT_d", [c.NH, 128, OWN], BF16)
        self.gT_d = dscr("gT_d", [2 * D, OWN], BF16)
        self.y2T_d = dscr("y2T_d", [c.SW, OWN], BF16)
        self.aoT_d = dscr("aoT_d", [c.AW, OWN], BF16)

    def getsem(self, name):
        return self.S.dsem(name)

    def sb(self, stack, name, shape, dt):
        self._uid = getattr(self, "_uid", 0) + 1
        return stack.enter_context(self.nc.sbuf_tensor(f"sb_{name}_{self._uid}", list(shape), dt))

    def build(self):
        c, nc, S = self.c, self.nc, self.S
        with ExitStack() as top:
            self.top = top
            self.pf = Ring(self, top, "pf", 6, [128, 512], F32, dma=False, psum=True)
            self.pbt = Ring(self, top, "pbt", 2, [128, 1024], BF16, dma=False, psum=True)
            self.identf = self.sb(top, "identf", [128, 128], F32)
            self.identb = self.sb(top, "identb", [128, 128], BF16)
            self.trib = self.sb(top, "trib", [128, 128], BF16)
            self.gcols = self.sb(top, "gcols", [128, 4 * c.KT], F32)
            self.bgate = self.sb(top, "bgate", [128, 2 * c.KT], F32)
            self.bglu = self.sb(top, "bglu", [128, c.CT], F32)
            self.dskip = self.sb(top, "dskip", [128, c.CT], F32)
            self.nbf = self.sb(top, "nbf", [c.NH, 1], F32)
            self.kbias = self.sb(top, "kbias", [128, c.NKT], F32)
            self.cst = self.sb(top, "cst", [128, 4], F32)
            self.small = Ring(self, top, "small", 8, [128, 4], F32, dma=False)
            self.constb = Buf("const")
            cs = self.getsem("const")
            cm = self.cmat_d
            S.dma("sp", self.identf[:], cm[:, 0:128], cs, writes=[self.constb])
            S.dma("pool", self.identb[:], cm[:, 0:128], cs, writes=[self.constb])
            S.dma("pool", self.trib[:], cm[:, 128:256], cs, writes=[self.constb])
            S.dma("sp", self.gcols[:], self.gcols_d[:, :], cs, writes=[self.constb])
            S.dma("sp", self.bgate[:], self.bgate_d[:, :], cs, writes=[self.constb])
            S.dma("sp", self.bglu[:], self.bglu_d[:, :], cs, writes=[self.constb])
            S.dma("sp", self.dskip[:], self.dskip_d[:, :], cs, writes=[self.constb])
            S.dma("sp", self.nbf[:], self.bf_d[:, :], cs, writes=[self.constb])
            S.dma("sp", self.kbias[:], self.kbias_d[:, :], cs, writes=[self.constb])
            S.op("dve", lambda e: e.memset(self.cst[:, 0:1], -0.5), writes=[self.constb])
            S.op("dve", lambda e: e.memset(self.cst[:, 1:2], 1.0), writes=[self.constb])
            S.op("dve", lambda e: e.tensor_scalar(out=self.nbf[:], in0=self.nbf[:], scalar1=-1.0,
                                                  scalar2=None, op0=ALU.mult),
                 reads=[self.constb], writes=[self.constb])
            S.barrier()

            self.phase_A()
            S.barrier()
            with ExitStack() as mid:
                self.sel127 = self.sb(mid, "sel127", [128, 128], F32)
                self.sel63 = self.sb(mid, "sel63", [128, 128], F32)
                self.c_tm = self.sb(mid, "c_tm", [128, c.NKT, c.NH], F32)
                self.c_tmb = Buf("c_tm")
                S.dma("sp", self.sel127[:], cm[:, 256:384], cs, writes=[self.constb])
                S.dma("sp", self.sel63[:], cm[:, 384:512], cs, writes=[self.constb])
                S.barrier()
                self.phase_S5()
                S.barrier()
                self.phase_B()
                S.barrier()
            self.phase_blocks()
            S.barrier()
        return nc

    def slab_load(self, W, k0, nk, c0, ncols):
        S = self.S
        tile, buf, sem = self.slabs.next()
        src = W.rearrange("(kt p) n -> p kt n", p=128)
        nd = min(4, nk)
        step = -(-nk // nd)
        for a in range(0, nk, step):
            b = min(nk, a + step)
            S.dma("pool", tile[:, a:b, 0:ncols], src[:, k0 + a:k0 + b, c0:c0 + ncols], sem,
                  writes=[buf])
        return tile, buf

    def run_items(self, items):
        idx = [i for i, it in enumerate(items) if it[0] is not None]
        loaded = {}
        ptr = 0

        def prefetch(upto):
            nonlocal ptr
            while ptr < len(idx) and ptr < upto:
                loaded[idx[ptr]] = self.slab_load(*items[idx[ptr]][0])
                ptr += 1

        pos = 0
        prefetch(2)
        for i, (spec, fn) in enumerate(items):
            if spec is not None:
                pos += 1
                prefetch(pos + 2)
                tile, buf = loaded.pop(i)
                fn(tile, buf)
            else:
                fn(None, None)

    def items_fm(self, items, W, c0, ncols, nk, actT, actb, ntok, epi, k0=0):
        S = self.S
        for s0 in range(0, ncols, 256):
            n_ = min(256, ncols - s0)

            def fn(tile, buf, s0=s0, n_=n_):
                for ci in range(n_ // 128):
                    ps, psb, _ = self.pf.next()
                    for kt in range(nk):
                        S.op("pe", lambda e, kt=kt, ci=ci: e.matmul(
                            ps[:, 0:ntok], lhsT=tile[:, kt, ci * 128:(ci + 1) * 128],
                            rhs=actT[:, kt, 0:ntok], start=(kt == 0), stop=(kt == nk - 1)),
                            reads=[buf, actb], writes=[psb])
                    epi(s0 // 128 + ci, ps, psb)

            items.append(((W, k0, nk, c0 + s0, n_), fn))

    def items_tm(self, items, W, k0, nk, c0, ncols, actT, actb, ntt, epi):
        S = self.S
        for s0 in range(0, ncols, 256):
            n_ = min(256, ncols - s0)

            def fn(tile, buf, s0=s0, n_=n_):
                for tt in range(ntt):
                    ps, psb, _ = self.pf.next()
                    for kt in range(nk):
                        S.op("pe", lambda e, kt=kt, tt=tt: e.matmul(
                            ps[:, 0:n_], lhsT=actT[:, kt, tt * 128:(tt + 1) * 128],
                            rhs=tile[:, kt, 0:n_], start=(kt == 0), stop=(kt == nk - 1)),
                            reads=[buf, actb], writes=[psb])
                    epi(tt, s0, n_, ps, psb)

            items.append(((W, k0, nk, c0 + s0, n_), fn))

    def norm_front(self, get_src, tt, hb, hbb):
        c, S = self.c, self.S
        D = c.D
        xs, xsb = get_src(tt)
        ss, ssb, _ = self.small.next()
        S.op("act", lambda e: e.activation(out=hb[:, 0:D], in_=xs, func=AF.Square,
                                           accum_out=ss[:, 0:1]),
             reads=[xsb], writes=[hbb, ssb])
        S.op("dve", lambda e: e.tensor_scalar(out=ss[:, 1:2], in0=ss[:, 0:1], scalar1=1.0 / D,
                                              scalar2=EPS, op0=ALU.mult, op1=ALU.add),
             reads=[ssb], writes=[ssb])
        S.op("pool", lambda e: e.tensor_tensor(out=ss[:, 2:3], in0=ss[:, 1:2],
                                               in1=self.cst[:, 0:1], op=ALU.pow),
             reads=[ssb], writes=[ssb])
        S.op("act", lambda e: e.activation(out=hb[:, 0:D], in_=xs, func=AF.Copy, scale=ss[:, 2:3]),
             reads=[xsb, ssb], writes=[hbb])

    def norm_back(self, tt, gofs, hb, hbb, hT, hTb):
        c, S = self.c, self.S
        KT = c.KT
        grp = min(8, KT)
        for k0 in range(0, KT, grp):
            pb, pbb, _ = self.pbt.next()
            for k in range(grp):
                S.op("pe", lambda e, k=k: e.transpose(
                    out=pb[:, k * 128:(k + 1) * 128],
                    in_=hb[:, (k0 + k) * 128:(k0 + k + 1) * 128], identity=self.identb[:]),
                    reads=[hbb], writes=[pbb])
            S.op("dve", lambda e: e.tensor_tensor(
                out=hT[:, k0:k0 + grp, tt * 128:(tt + 1) * 128],
                in0=pb[:, 0:grp * 128].rearrange("p (k t) -> p k t", k=grp),
                in1=self.gcols[:, gofs + k0:gofs + k0 + grp].unsqueeze(2).to_broadcast(
                    [128, grp, 128]),
                op=ALU.mult), reads=[pbb], writes=[hTb])

    def norm_T(self, get_src, nt, gofs, hb, hbb, hT, hTb):
        for tt in range(nt):
            self.norm_front(get_src, tt, hb, hbb)
            self.norm_back(tt, gofs, hb, hbb, hT, hTb)

    def phase_A(self):
        c, S = self.c, self.S
        D, L, KT = c.D, c.L, c.KT
        with ExitStack() as st:
            self.slabs = Ring(self, st, "slabA", 3, [128, 32, 256], BF16)
            xst = Ring(self, st, "xst", 2, [128, D], F32)
            hbs = [(self.sb(st, f"hbA{i}", [128, D], BF16), Buf(f"hbA{i}")) for i in range(4)]
            hTs = [(self.sb(st, f"hTA{i}", [128, KT, 512], BF16), Buf(f"hTA{i}")) for i in range(2)]
            stb = Ring(self, st, "stb", 4, [128, 512], BF16)
            stf = Ring(self, st, "stf", 2, [128, 512], F32)
            items = []
            for bi in range(L // 512):
                t0 = bi * 512
                own = t0 >= c.OWN0
                to0 = t0 - c.OWN0

                hT, hTb = hTs[bi % 2]

                def nfront(tile, buf, bi=bi):
                    t0_ = bi * 512

                    def get_src(tt):
                        xs, xsb, sem = xst.next()
                        S.dma("sp", xs[:], self.xl[t0_ + tt * 128:t0_ + (tt + 1) * 128, :], sem,
                              writes=[xsb])
                        return xs[:], xsb
                    for tt in range(4):
                        self.norm_front(get_src, tt, hbs[tt][0], hbs[tt][1])

                def nback(tile, buf, bi=bi):
                    hT_, hTb_ = hTs[bi % 2]
                    for tt in range(4):
                        self.norm_back(tt, 0, hbs[tt][0], hbs[tt][1], hT_, hTb_)

                if bi == 0:
                    items.append((None, nfront))
                    items.append((None, nback))
                if bi + 1 < L // 512:
                    nf_next = (lambda tile, buf, f=nfront, b=bi + 1: f(tile, buf, bi=b))
                    nb_next = (lambda tile, buf, f=nback, b=bi + 1: f(tile, buf, bi=b))
                    items.append((None, nf_next))
                else:
                    nb_next = None

                def epi_K(ci, ps, psb, t0=t0):
                    sg, sgb, sem = stb.next()
                    S.op("act", lambda e: e.activation(out=sg[:], in_=ps[:, 0:512], func=AF.Copy),
                         reads=[psb], writes=[sgb])
                    S.dma("sp", self.KT_d[ci, :, t0:t0 + 512], sg[:], sem, reads=[sgb])

                def epi_V(tt, s0, n_, ps, psb, t0=t0):
                    sg, sgb, sem = stb.next()
                    S.op("act", lambda e: e.activation(out=sg[:, 0:n_], in_=ps[:, 0:n_],
                                                       func=AF.Copy),
                         reads=[psb], writes=[sgb])
                    S.dma("sp", self.V_d[t0 + tt * 128:t0 + (tt + 1) * 128, s0:s0 + n_],
                          sg[:, 0:n_], sem, reads=[sgb])

                def epi_F(ci, ps, psb, t0=t0):
                    sg, sgb, sem = stf.next()
                    S.op("act", lambda e: e.activation(out=sg[:], in_=ps[:, 0:512], func=AF.Copy),
                         reads=[psb], writes=[sgb])
                    S.dma("sp", self.f_d[0:c.NH, t0:t0 + 512], sg[0:c.NH, :], sem, reads=[sgb])

                def epi_U(ci, ps, psb, t0=t0):
                    sg, sgb, sem = stb.next()
                    S.op("act", lambda e: e.activation(out=sg[:], in_=ps[:, 0:512], func=AF.Copy),
                         reads=[psb], writes=[sgb])
                    S.dma("sp", self.uT_d[ci * 128:(ci + 1) * 128, t0:t0 + 512], sg[:], sem,
                          reads=[sgb])

                def epi_Q(ci, ps, psb, to0=to0):
                    sg, sgb, sem = stb.next()
                    S.op("act", lambda e: e.activation(out=sg[:], in_=ps[:, 0:512], func=AF.Copy),
                         reads=[psb], writes=[sgb])
                    S.dma("sp", self.QT_d[ci, :, to0:to0 + 512], sg[:], sem, reads=[sgb])

                def epi_G(ci, ps, psb, to0=to0):
                    sg, sgb, sem = stb.next()
                    S.op("act", lambda e: e.activation(out=sg[:], in_=ps[:, 0:512],
                                                       func=AF.Sigmoid,
                                                       bias=self.bgate[:, ci:ci + 1]),
                         reads=[psb], writes=[sgb])
                    S.dma("sp", self.gT_d[ci * 128:(ci + 1) * 128, to0:to0 + 512], sg[:], sem,
                          reads=[sgb])

                self.items_fm(items, self.w_in, c.OFF_K, c.AW, KT, hT, hTb, 512, epi_K)
                self.items_tm(items, self.w_in, 0, KT, c.OFF_V, c.AW, hT, hTb, 4, epi_V)
                self.items_fm(items, self.w_in, c.OFF_F, 128, KT, hT, hTb, 512, epi_F)
                self.items_fm(items, self.w_in, c.OFF_U, c.SW, KT, hT, hTb, 512, epi_U)
                if own:
                    self.items_fm(items, self.w_in, 0, c.AW, KT, hT, hTb, 512, epi_Q)
                    self.items_fm(items, self.w_in, c.OFF_G, 2 * D, KT, hT, hTb, 512, epi_G)
                if nb_next is not None:
                    items.append((None, nb_next))
            self.run_items(items)
            S.barrier()

    def phase_S5(self):
        c, S = self.c, self.S
        L, NH, NST, CT, NKT = c.L, c.NH, c.NST, c.CT, c.NKT
        T = 256
        with ExitStack() as st:
            fl = self.sb(st, "fl", [NH, L], F32)
            fl2 = self.sb(st, "fl2", [NH, L], F32)
            flb = Buf("fl")
            fl2b = Buf("fl2")
            sem = self.getsem("fl")
            S.dma("sp", fl[:], self.f_d[:, :], sem, writes=[flb])
            S.op("act", lambda e: e.activation(out=fl[:], in_=fl[:], func=AF.Exp, scale=-1.0,
                                               bias=self.nbf[:, 0:1]),
                 reads=[flb], writes=[flb])
            S.op("act", lambda e: e.activation(out=fl[:], in_=fl[:], func=AF.Ln, bias=1.0),
                 reads=[flb], writes=[flb])
            CH = min(2048, L)
            for i in range(0, L, CH):
                init = 0.0 if i == 0 else fl2[:, i - 1:i]
                S.op("dve", lambda e, i=i, init=init: e.tensor_tensor_scan(
                    out=fl2[:, i:i + CH], data0=self.cst[0:NH, 1:2].to_broadcast([NH, CH]),
                    data1=fl[:, i:i + CH], initial=init, op0=ALU.mult, op1=ALU.subtract),
                    reads=[flb, fl2b], writes=[fl2b])
            per = 512 // NH
            for k0 in range(0, NKT, per):
                ps, psb, _ = self.pf.next()
                n = min(per, NKT - k0)
                for k in range(n):
                    S.op("pe", lambda e, k=k: e.transpose(
                        out=ps[:, k * NH:(k + 1) * NH], in_=fl2[:, (k0 + k) * 128:(k0 + k + 1) * 128],
                        identity=self.identf[0:NH, 0:NH]), reads=[fl2b], writes=[psb])
                S.op("act", lambda e, n=n: e.activation(
                    out=self.c_tm[:, k0:k0 + n, :],
                    in_=ps[:, 0:n * NH].rearrange("p (k h) -> p k h", h=NH), func=AF.Copy),
                    reads=[psb], writes=[self.c_tmb])
            S.barrier()

        with ExitStack() as st:
            P = self.sb(st, "s5P", [128, 34, NST], F32)
            Pb = Buf("s5P")
            BC = self.sb(st, "s5BC", [128, 2, NST, 16], F32)
            CC = self.sb(st, "s5CC", [128, 2, NST, 16], F32)
            BB = self.sb(st, "s5BB", [128, 2, NST, 16], F32)
            BCb = Buf("s5BC")
            Bm = self.sb(st, "Bm", [128, 2, NST, 128], BF16)
            Cm = self.sb(st, "Cm", [128, 3, NST, 128], BF16)
            Bmb, Cmb = Buf("Bm"), Buf("Cm")
            tabb = Buf("tab")
            sem = self.getsem("s5ld")
            S.dma("sp", P[:, 0:3, :], self.s5p_d.rearrange("p (a s) -> p a s", a=3), sem, writes=[Pb])
            S.dma("sp", BC[:], self.s5b_d[:, :, :, :], sem, writes=[BCb])
            S.dma("sp", CC[:], self.s5c_d[:, :, :, :], sem, writes=[BCb])
            Pb.w = (sem, sem.c)
            BCb.w = (sem, sem.c)

            def pv(i):
                return P[:, i, :]

            def dv(fn):
                S.op("dve", fn, reads=[Pb], writes=[Pb])

            def av(fn):
                S.op("act", fn, reads=[Pb], writes=[Pb])

            TT = lambda o, a, b, op: dv(lambda e: e.tensor_tensor(out=pv(o), in0=pv(a), in1=pv(b), op=op))
            TS = lambda o, a, s1, s2, op0, op1: dv(lambda e: e.tensor_scalar(
                out=pv(o), in0=pv(a), scalar1=s1, scalar2=s2, op0=op0, op1=op1))
            av(lambda e: e.activation(out=pv(3), in_=pv(2), func=AF.Exp))
            TT(4, 3, 0, ALU.mult)
            TT(5, 3, 1, ALU.mult)
            av(lambda e: e.activation(out=pv(6), in_=pv(4), func=AF.Exp))
            TS(7, 5, 0.0, None, ALU.add, ALU.bypass)
            TS(8, 5, PI / 2, None, ALU.add, ALU.bypass)
            for _ in range(4):
                for a in (7, 8):
                    TS(9, a, PI, 2 * PI, ALU.is_gt, ALU.mult)
                    TT(a, a, 9, ALU.subtract)
            av(lambda e: e.activation(out=pv(9), in_=pv(7), func=AF.Sin))
            av(lambda e: e.activation(out=pv(10), in_=pv(8), func=AF.Sin))
            TT(11, 6, 10, ALU.mult)
            TT(12, 6, 9, ALU.mult)
            TT(13, 0, 0, ALU.mult)
            TT(14, 1, 1, ALU.mult)
            TT(13, 13, 14, ALU.add)
            dv(lambda e: e.reciprocal(out=pv(14), in_=pv(13)))
            TS(15, 11, -1.0, None, ALU.add, ALU.bypass)
            TT(16, 15, 0, ALU.mult)
            TT(17, 12, 1, ALU.mult)
            TT(16, 16, 17, ALU.add)
            TT(16, 16, 14, ALU.mult)
            TT(17, 12, 0, ALU.mult)
            TT(18, 15, 1, ALU.mult)
            TT(17, 17, 18, ALU.subtract)
            TT(17, 17, 14, ALU.mult)
            fre = P[:, 16, :].unsqueeze(2).to_broadcast([128, NST, 16])
            fim = P[:, 17, :].unsqueeze(2).to_broadcast([128, NST, 16])

            def bop(o, a, b, op):
                S.op("dve", lambda e: e.tensor_tensor(out=o, in0=a, in1=b, op=op),
                     reads=[Pb, BCb], writes=[BCb])
            bop(BB[:, 0], BC[:, 0], fre, ALU.mult)
            bop(BB[:, 1], BC[:, 1], fim, ALU.mult)
            bop(BB[:, 0], BB[:, 0], BB[:, 1], ALU.subtract)
            bop(BB[:, 1], BC[:, 1], fre, ALU.mult)
            bop(BC[:, 1], BC[:, 0], fim, ALU.mult)
            bop(BB[:, 1], BB[:, 1], BC[:, 1], ALU.add)
            zb = self.sb(st, "zb", [128, 8, 128], BF16)
            zbb = Buf("zb")
            S.op("pool", lambda e: e.memset(zb[:], 0.0), writes=[zbb])
            S.op("pool", lambda e: e.memset(Cm[:], 0.0), writes=[Cmb])
            for s in range(NST):
                off = 32 * (s % 4)
                for ri in range(2):
                    z = zb[:, (s % 4) * 2 + ri, :]
                    S.op("act", lambda e: e.activation(out=z[0:64, off:off + 16],
                                                       in_=BB[0:64, ri, s, :], func=AF.Copy),
                         reads=[BCb], writes=[zbb])
                    S.op("act", lambda e: e.activation(out=z[64:128, off + 16:off + 32],
                                                       in_=BB[64:128, ri, s, :], func=AF.Copy),
                         reads=[BCb], writes=[zbb])
                    pb, pbb, _ = self.pbt.next()
                    S.op("pe", lambda e: e.transpose(out=pb[:, 0:128], in_=z, identity=self.identb[:]),
                         reads=[zbb], writes=[pbb])
                    S.op("dve", lambda e: e.tensor_copy(out=Bm[:, ri, s, :], in_=pb[:, 0:128]),
                         reads=[pbb], writes=[Bmb])
                    sc = 1.0 if ri == 0 else -1.0
                    S.op("act", lambda e: e.activation(out=Cm[0:64, ri, s, off:off + 16],
                                                       in_=CC[0:64, ri, s, :], func=AF.Copy, scale=sc),
                         reads=[BCb], writes=[Cmb])
                    S.op("act", lambda e: e.activation(out=Cm[64:128, ri, s, off + 16:off + 32],
                                                       in_=CC[64:128, ri, s, :], func=AF.Copy, scale=sc),
                         reads=[BCb], writes=[Cmb])
                    if ri == 0:
                        S.op("act", lambda e: e.activation(out=Cm[0:64, 2, s, off:off + 16],
                                                           in_=CC[0:64, 0, s, :], func=AF.Copy, scale=-1.0),
                             reads=[BCb], writes=[Cmb])
                        S.op("act", lambda e: e.activation(out=Cm[64:128, 2, s, off + 16:off + 32],
                                                           in_=CC[64:128, 0, s, :], func=AF.Copy, scale=-1.0),
                             reads=[BCb], writes=[Cmb])
            def build_tables(p1_re, p1_im, reverse, writer):
                H = max(1, NST // 2)
                halves = [(0, H), (H, NST)] if NST > 1 else [(0, NST)]
                for (s0, s1) in halves:
                    n = s1 - s0
                    with ExitStack() as st2:
                        ER = self.sb(st2, "ER", [128, n, T], F32)
                        EI = self.sb(st2, "EI", [128, n, T], F32)
                        T1 = self.sb(st2, "T1", [128, n, T // 2], F32)
                        T2 = self.sb(st2, "T2", [128, n, T // 2], F32)
                        Eb = Buf("E")

                        def ev(fn):
                            S.op("dve", fn, reads=[Pb, Eb], writes=[Eb, Pb])
                        i0 = T - 1 if reverse else 0
                        ev(lambda e: e.memset(ER[:, :, i0:i0 + 1], 1.0))
                        ev(lambda e: e.memset(EI[:, :, i0:i0 + 1], 0.0))
                        TS(20, p1_re, 0.0, None, ALU.add, ALU.bypass)
                        TS(21, p1_im, 0.0, None, ALU.add, ALU.bypass)
                        k = 1
                        while k < T:
                            pr = P[:, 20, s0:s1].unsqueeze(2).to_broadcast([128, n, k])
                            pi_ = P[:, 21, s0:s1].unsqueeze(2).to_broadcast([128, n, k])
                            if reverse:
                                src, dst = slice(T - k, T), slice(T - 2 * k, T - k)
                            else:
                                src, dst = slice(0, k), slice(k, 2 * k)
                            a_r, a_i = ER[:, :, src], EI[:, :, src]
                            t1, t2 = T1[:, :, 0:k], T2[:, :, 0:k]
                            ev(lambda e: e.tensor_tensor(out=t1, in0=a_r, in1=pr, op=ALU.mult))
                            ev(lambda e: e.tensor_tensor(out=t2, in0=a_i, in1=pi_, op=ALU.mult))
                            ev(lambda e: e.tensor_tensor(out=ER[:, :, dst], in0=t1, in1=t2, op=ALU.subtract))
                            ev(lambda e: e.tensor_tensor(out=t1, in0=a_r, in1=pi_, op=ALU.mult))
                            ev(lambda e: e.tensor_tensor(out=t2, in0=a_i, in1=pr, op=ALU.mult))
                            ev(lambda e: e.tensor_tensor(out=EI[:, :, dst], in0=t1, in1=t2, op=ALU.add))
                            TT(22, 20, 20, ALU.mult)
                            TT(23, 21, 21, ALU.mult)
                            TT(24, 20, 21, ALU.mult)
                            TT(20, 22, 23, ALU.subtract)
                            TS(21, 24, 2.0, None, ALU.mult, ALU.bypass)
                            k *= 2
                        writer(ER, EI, s0, s1, Eb)
                        S.barrier()

            burn = Ring.__new__(Ring)
            burn.t, burn.b, burn.s, burn.i, burn.n = self.pf.t[0:4], self.pf.b[0:4], [None] * 4, 0, 4
            utsrc = self.uT_d.rearrange("(ct p) t -> p ct t", p=128)
            dv(lambda e: e.memset(P[:, 28:30, :], 0.0))
            NPRE = c.OWN0 // T
            with ExitStack() as sa:
                WA = self.sb(sa, "WA", [128, NST, 2 * T], BF16)
                WB = self.sb(sa, "WB", [128, NST, 2 * T], BF16)

                def wr_ab(ER, EI, s0, s1, Eb):
                    S.op("dve", lambda e: e.tensor_copy(out=WA[:, s0:s1, 0:T], in_=ER[:]), reads=[Eb], writes=[tabb])
                    S.op("act", lambda e: e.activation(out=WA[:, s0:s1, T:2 * T], in_=EI[:], func=AF.Copy,
                                                       scale=-1.0), reads=[Eb], writes=[tabb])
                    S.op("dve", lambda e: e.tensor_copy(out=WB[:, s0:s1, 0:T], in_=EI[:]), reads=[Eb], writes=[tabb])
                    S.op("act", lambda e: e.activation(out=WB[:, s0:s1, T:2 * T], in_=ER[:], func=AF.Copy),
                         reads=[Eb], writes=[tabb])
                if NPRE > 0:
                    build_tables(11, 12, True, wr_ab)
                    TS(30, 20, 0.0, None, ALU.add, ALU.bypass)
                    TS(31, 21, 0.0, None, ALU.add, ALU.bypass)
                    utr = Ring(self, sa, "utrA", 2, [128, CT, 1024], BF16)
                    junk = Ring(self, sa, "junk", 2, [128, 2 * T], BF16, dma=False)
                    locr = Ring(self, sa, "locr", 2, [128, 2, NST], F32, dma=False)
                    uts = {}
                    for chn in range(NPRE):
                        cg, cc = divmod(chn, 4)
                        if cg not in uts:
                            ut, utb, usem = utr.next()
                            S.dma("sp", ut[:], utsrc[:, :, cg * 1024:(cg + 1) * 1024], usem, writes=[utb])
                            uts[cg] = (ut, utb)
                        ut, utb = uts[cg]
                        ucols = slice(cc * T, (cc + 1) * T)
                        loc, locb, _ = locr.next()
                        for s in range(NST):
                            ct = s // 4
                            ps, psb, _ = burn.next()
                            S.op("pe", lambda e: e.matmul(ps[:, 0:T], lhsT=Bm[:, 0, s, :], rhs=ut[:, ct, ucols],
                                                          start=True, stop=True), reads=[Bmb, utb], writes=[psb])
                            S.op("pe", lambda e: e.matmul(ps[:, T:2 * T], lhsT=Bm[:, 1, s, :], rhs=ut[:, ct, ucols],
                                                          start=True, stop=True), reads=[Bmb, utb], writes=[psb])
                            for ri, W_ in enumerate((WA, WB)):
                                jk, jkb, _ = junk.next()
                                S.op("dve", lambda e, ri=ri, W_=W_, jk=jk: e.scalar_tensor_tensor(
                                    out=jk[:], in0=ps[:, 0:2 * T], scalar=1.0, in1=W_[:, s, :],
                                    op0=ALU.mult, op1=ALU.mult, accum_out=loc[:, ri, s:s + 1]),
                                    reads=[psb, tabb], writes=[jkb, locb])
                        TT(22, 30, 28, ALU.mult)
                        TT(23, 31, 29, ALU.mult)
                        TT(32, 30, 29, ALU.mult)
                        TT(33, 31, 28, ALU.mult)
                        TT(22, 22, 23, ALU.subtract)
                        TT(32, 32, 33, ALU.add)
                        S.op("dve", lambda e: e.tensor_tensor(out=pv(28), in0=pv(22), in1=loc[:, 0, :], op=ALU.add),
                             reads=[Pb, locb], writes=[Pb])
                        S.op("dve", lambda e: e.tensor_tensor(out=pv(29), in0=pv(32), in1=loc[:, 1, :], op=ALU.add),
                             reads=[Pb, locb], writes=[Pb])
                    S.barrier()
            cosT = self.sb(st, "cosT", [128, NST, T], BF16)
            sinT = self.sb(st, "sinT", [128, NST, T], BF16)

            def wr_cs(ER, EI, s0, s1, Eb):
                S.op("dve", lambda e: e.tensor_copy(out=cosT[:, s0:s1, :], in_=ER[:]), reads=[Eb], writes=[tabb])
                S.op("act", lambda e: e.activation(out=sinT[:, s0:s1, :], in_=EI[:], func=AF.Copy),
                     reads=[Eb], writes=[tabb])
            build_tables(10, 9, False, wr_cs)
            TT(22, 10, 28, ALU.mult)
            TT(23, 9, 29, ALU.mult)
            TT(25, 22, 23, ALU.subtract)
            TT(22, 10, 29, ALU.mult)
            TT(23, 9, 28, ALU.mult)
            TT(26, 22, 23, ALU.add)

            wg = self.sb(st, "wg", [128, CT, c.SW], BF16)
            wgb = Buf("wg")
            semw = self.getsem("wgld")
            for ct in range(CT):
                S.dma("pool", wg[:, ct, :], self.w_glu[ct * 128:(ct + 1) * 128, :], semw, writes=[wgb])
            wgb.w = (semw, semw.c)
            zl = self.sb(st, "zl", [128, 2, NST], F32)
            zlb = Buf("zl")
            zib = Buf("zi")
            utr = Ring(self, st, "utr", 2, [128, CT, 1024], BF16)
            wk1 = Ring(self, st, "wk1", 3, [128, 512], F32, dma=False)
            wk2 = Ring(self, st, "wk2", 3, [128, 512], F32, dma=False)
            wwr = Ring(self, st, "wwr", 2, [128, 512], F32, dma=False)
            zr = Ring(self, st, "zr", 3, [128, 512], F32, dma=False)
            q1r = Ring(self, st, "q1r", 4, [128, 512], BF16, dma=False)
            q2r = Ring(self, st, "q2r", 4, [128, 512], BF16, dma=False)
            zhr = Ring(self, st, "zhr", 3, [128, 512], BF16, dma=False)
            ygf = Ring(self, st, "ygf", 2, [128, CT, T], F32, dma=False)
            ygb = Ring(self, st, "ygb", 2, [128, CT, T], BF16, dma=False)
            tmp = Ring(self, st, "s5tmp", 3, [128, T], F32, dma=False)
            y2s = Ring(self, st, "y2s", 3, [128, T], BF16)
            ybank = [(self.pf.t[4], self.pf.b[4]), (self.pf.t[5], self.pf.b[5])]
            tiles = [(chn // 4, chn % 4, s) for chn in range(NPRE, L // T) for s in range(NST)]
            uts, st0, st1, st2, st3, ych = {}, {}, {}, {}, {}, {}
            yidx = [0]

            def stage1(i):
                cg, cc, s = tiles[i]
                if cg not in uts:
                    ut, utb, usem = utr.next()
                    S.dma("sp", ut[:], utsrc[:, :, cg * 1024:(cg + 1) * 1024], usem, writes=[utb])
                    uts[cg] = (ut, utb)
                ut, utb = uts[cg]
                ct = s // 4
                ucols = slice(cc * T, (cc + 1) * T)
                ps, psb, _ = burn.next()
                S.op("pe", lambda e: e.matmul(ps[:, 0:T], lhsT=Bm[:, 0, s, :], rhs=ut[:, ct, ucols],
                                              start=True, stop=True), reads=[Bmb, utb], writes=[psb])
                S.op("pe", lambda e: e.matmul(ps[:, T:2 * T], lhsT=Bm[:, 1, s, :], rhs=ut[:, ct, ucols],
                                              start=True, stop=True), reads=[Bmb, utb], writes=[psb])
                st0[i] = (ps, psb)

            def stage1b(i):
                cg, cc, s = tiles[i]
                ps, psb = st0.pop(i)
                t13, t13b, _ = wk1.next()
                t42, t42b, _ = wk2.next()
                ps3 = ps[:, :].rearrange("p (two t) -> p two t", two=2)
                cosb = cosT[:, s:s + 1, :].to_broadcast([128, 2, T])
                sinb = sinT[:, s:s + 1, :].to_broadcast([128, 2, T])
                S.op("dve", lambda e: e.tensor_tensor(
                    out=t13[:, :].rearrange("p (two t) -> p two t", two=2), in0=ps3, in1=cosb,
                    op=ALU.mult), reads=[psb, tabb], writes=[t13b])
                S.op("dve", lambda e: e.tensor_tensor(
                    out=t42[:, :].rearrange("p (two t) -> p two t", two=2), in0=ps3, in1=sinb,
                    op=ALU.mult), reads=[psb, tabb], writes=[t42b])
                st1[i] = (t13, t13b, t42, t42b, cosb, sinb)

            def stage2(i):
                cg, cc, s = tiles[i]
                ut, utb = uts[cg]
                ct = s // 4
                ucols = slice(cc * T, (cc + 1) * T)
                tl0 = cg * 1024 + cc * T
                own = tl0 >= c.OWN0
                chn = cg * 4 + cc
                if own and s == 0:
                    yf, yfb, _ = ygf.next()
                    yb, ybb, _ = ygb.next()
                    ych[chn] = (yf, yfb, yb, ybb)
                t13, t13b, t42, t42b, cosb, sinb = st1.pop(i)
                ww, wwb, _ = wwr.next()
                z, zb_, _ = zr.next()
                S.op("dve", lambda e: e.scalar_tensor_tensor(out=ww[:, 0:T], in0=t13[:, 0:T], scalar=1.0,
                                                             in1=t42[:, T:2 * T], op0=ALU.mult, op1=ALU.add),
                     reads=[t13b, t42b], writes=[wwb])
                S.op("dve", lambda e: e.scalar_tensor_tensor(out=ww[:, T:2 * T], in0=t13[:, T:2 * T], scalar=1.0,
                                                             in1=t42[:, 0:T], op0=ALU.mult, op1=ALU.subtract),
                     reads=[t13b, t42b], writes=[wwb])
                rb = P[:, 6, s:s + 1].to_broadcast([128, T])
                for ri in range(2):
                    S.op("dve", lambda e, ri=ri: e.tensor_tensor_scan(
                        out=z[:, ri * T:(ri + 1) * T], data0=rb, data1=ww[:, ri * T:(ri + 1) * T],
                        initial=P[:, 25 + ri, s:s + 1], op0=ALU.mult, op1=ALU.add),
                        reads=[wwb, Pb, zib], writes=[zb_])
                    S.op("act", lambda e, ri=ri: e.activation(
                        out=zl[:, ri, s:s + 1], in_=z[:, (ri + 1) * T - 1:(ri + 1) * T], func=AF.Copy),
                        reads=[zb_], writes=[zlb])
                zh, zhb, _ = zhr.next()
                S.op("act", lambda e: e.activation(out=zh[:], in_=z[:], func=AF.Copy),
                     reads=[zb_], writes=[zhb])
                st2[i] = (zh, zhb, cosb, sinb)
                if s == NST - 1:
                    def cv(fn):
                        S.op("dve", fn, reads=[Pb, zlb, zib], writes=[Pb])
                    cv(lambda e: e.tensor_tensor(out=pv(22), in0=pv(20), in1=zl[:, 0, :], op=ALU.mult))
                    cv(lambda e: e.tensor_tensor(out=pv(23), in0=pv(21), in1=zl[:, 1, :], op=ALU.mult))
                    cv(lambda e: e.tensor_tensor(out=pv(24), in0=pv(20), in1=zl[:, 1, :], op=ALU.mult))
                    cv(lambda e: e.tensor_tensor(out=pv(27), in0=pv(21), in1=zl[:, 0, :], op=ALU.mult))
                    S.op("dve", lambda e: e.tensor_tensor(out=pv(25), in0=pv(22), in1=pv(23), op=ALU.subtract),
                         reads=[Pb], writes=[Pb, zib])
                    S.op("dve", lambda e: e.tensor_tensor(out=pv(26), in0=pv(24), in1=pv(27), op=ALU.add),
                         reads=[Pb], writes=[Pb, zib])

            def stage3(i):
                cg, cc, s = tiles[i]
                ut, utb = uts[cg]
                ct = s // 4
                ucols = slice(cc * T, (cc + 1) * T)
                tl0 = cg * 1024 + cc * T
                own = True
                chn = cg * 4 + cc
                zh, zhb, cosb, sinb = st2.pop(i)
                zb_ = zhb
                if own:
                    yf, yfb, yb, ybb = ych[chn]
                    q1, q1b, _ = q1r.next()
                    q2, q2b, _ = q2r.next()
                    z3 = zh[:, :].rearrange("p (two t) -> p two t", two=2)
                    S.op("dve", lambda e: e.tensor_tensor(
                        out=q1[:, :].rearrange("p (two t) -> p two t", two=2), in0=z3, in1=cosb,
                        op=ALU.mult), reads=[zb_, tabb], writes=[q1b])
                    S.op("dve", lambda e: e.tensor_tensor(
                        out=q2[:, :].rearrange("p (two t) -> p two t", two=2), in0=z3, in1=sinb,
                        op=ALU.mult), reads=[zb_, tabb], writes=[q2b])
                    st3[i] = (q1, q1b, q2, q2b)

            def stage4(i):
                cg, cc, s = tiles[i]
                ut, utb = uts[cg]
                ct = s // 4
                ucols = slice(cc * T, (cc + 1) * T)
                tl0 = cg * 1024 + cc * T
                own = True
                chn = cg * 4 + cc
                q1, q1b, q2, q2b = st3.pop(i)
                if own:
                    yf, yfb, yb, ybb = ych[chn]
                    yps, ypsb = ybank[yidx[0] % 2]
                    terms = [(0, q1, q1b, 0), (2, q2, q2b, T), (1, q2, q2b, 0), (1, q1, q1b, T)]
                    for ti, (cp_, q_, qb_, o_) in enumerate(terms):
                        S.op("pe", lambda e, cp_=cp_, q_=q_, o_=o_, ti=ti: e.matmul(
                            yps[:, 0:T], lhsT=Cm[:, cp_, s, :], rhs=q_[:, o_:o_ + T],
                            start=(s % 4 == 0 and ti == 0), stop=(s % 4 == 3 and ti == 3)),
                            reads=[Cmb, qb_], writes=[ypsb])
                    if s % 4 == 3:
                        yidx[0] += 1
                        yv = yf[:, ct, :]
                        t1, t1b, _ = tmp.next()
                        S.op("dve", lambda e: e.scalar_tensor_tensor(
                            out=yv, in0=ut[:, ct, ucols], scalar=self.dskip[:, ct:ct + 1],
                            in1=yps[:, 0:T], op0=ALU.mult, op1=ALU.add),
                            reads=[utb, ypsb], writes=[yfb])
                        S.op("act", lambda e: e.activation(out=t1[:], in_=yv, func=AF.Square,
                                                           scale=0.21145921592600805),
                             reads=[yfb], writes=[t1b])
                        S.op("dve", lambda e: e.scalar_tensor_tensor(out=t1[:], in0=t1[:], scalar=1.0, in1=yv,
                                                                     op0=ALU.add, op1=ALU.mult),
                             reads=[t1b, yfb], writes=[t1b])
                        S.op("act", lambda e: e.activation(out=t1[:], in_=t1[:], func=AF.Sigmoid,
                                                           scale=1.5957691216),
                             reads=[t1b], writes=[t1b])
                        S.op("pool", lambda e: e.tensor_tensor(out=yv, in0=yv, in1=t1[:], op=ALU.mult),
                             reads=[t1b, yfb], writes=[yfb])
                        S.op("act", lambda e: e.activation(out=yb[:, ct, :], in_=yv, func=AF.Copy),
                             reads=[yfb], writes=[ybb])
                if s == NST - 1:
                    if own:
                        yf, yfb, yb, ybb = ych.pop(chn)
                        to0 = tl0 - c.OWN0
                        for co in range(CT):
                            ps, psb = glu_ps, self.pbt.b[0]
                            for ct2 in range(CT):
                                S.op("pe", lambda e, ct2=ct2: e.matmul(
                                    ps[:, 0:T], lhsT=wg[:, ct2, co * 128:(co + 1) * 128], rhs=yb[:, ct2, :],
                                    start=(ct2 == 0), stop=(ct2 == CT - 1)),
                                    reads=[wgb, ybb], writes=[psb])
                            t1, t1b, _ = tmp.next()
                            S.op("act", lambda e: e.activation(out=t1[:], in_=ps[:, 0:T], func=AF.Sigmoid,
                                                               bias=self.bglu[:, co:co + 1]),
                                 reads=[psb], writes=[t1b])
                            sg, sgb, ssem = y2s.next()
                            S.op("pool", lambda e: e.tensor_tensor(out=sg[:], in0=yf[:, co, :], in1=t1[:],
                                                                   op=ALU.mult),
                                 reads=[yfb, t1b], writes=[sgb])
                            S.dma("sp", self.y2T_d[co * 128:(co + 1) * 128, to0:to0 + T], sg[:], ssem,
                                  reads=[sgb])

            glu_ps = self.pbt.t[0][:].bitcast(F32)
            NT_ = len(tiles)
            for i in range(-2, NT_ + 2):
                if 0 <= i + 2 < NT_:
                    stage1(i + 2)
                if 0 <= i + 1 < NT_:
                    stage1b(i + 1)
                if 0 <= i < NT_:
                    stage2(i)
                if 0 <= i - 1 < NT_:
                    stage3(i - 1)
                if 0 <= i - 2 < NT_:
                    stage4(i - 2)
            S.barrier()

    def phase_B(self):
        c, S = self.c, self.S
        L, NH, NKT, OWN = c.L, c.NH, c.NKT, c.OWN
        NQT = OWN // 128
        NSB = OWN // 512
        NB = NKT // 4
        K0 = c.OWN0 // 128
        scale = 128 ** -0.5
        with ExitStack() as st:
            ktr = Ring(self, st, "ktr", 2, [128, L], BF16)
            vr = Ring(self, st, "vr", 2, [128, NKT, 129], BF16)
            qr = Ring(self, st, "qr", 2, [128, OWN], BF16)
            ptr_ = Ring(self, st, "ptr", 12, [128, 512], BF16, dma=False)
            accs = []
            for i in range(2):
                t = self.sb(st, f"acc{i}", [128, 4, 129], F32)
                accs.append((t, [Buf(f"acc{i}_{q}") for q in range(4)]))
            tabs = []
            for i in range(2):
                tabs.append(dict(
                    rE=self.sb(st, f"rE{i}", [128, NKT], F32), rM=self.sb(st, f"rM{i}", [128, NKT], F32),
                    bO=self.sb(st, f"bO{i}", [128, NKT], F32), bD=self.sb(st, f"bD{i}", [128, NKT], F32),
                    fO=self.sb(st, f"fO{i}", [128, NQT, NB], F32), fD=self.sb(st, f"fD{i}", [128, NQT, 4], F32),
                    b=Buf(f"btab{i}")))
            obr = Ring(self, st, "obr", 4, [128, 128], BF16, dma=False)
            aos = Ring(self, st, "aos", 2, [128, 512], BF16)
            recr = Ring(self, st, "recr", 4, [128, 1], F32, dma=False)
            for i in range(2):
                S.op("pool", lambda e, i=i: e.memset(vr.t[i][:, :, 128:129], 1.0), writes=[vr.b[i]])
            stb = Ring.__new__(Ring)
            st4 = self.pbt.t[1][:].bitcast(F32)
            stb.t = [self.pf.t[0][:, :], self.pf.t[1][:, :], self.pf.t[2][:, :], st4]
            stb.b, stb.s, stb.i, stb.n = self.pf.b[0:3] + [self.pbt.b[1]], [None] * 4, 0, 4
            poslots = [(self.pf.t[3], 0, self.pf.b[3]), (self.pf.t[4], 0, self.pf.b[4]),
                       (self.pf.t[5], 0, self.pf.b[5])]
            poi = [0]

            def next_po():
                t, o, b = poslots[poi[0] % len(poslots)]
                poi[0] += 1
                return t[:, o:o + 129], b

            steps = []
            for h in range(NH):
                for sb_ in range(NSB):
                    nboff = (c.OWN0 + sb_ * 512) // 512
                    for B in range(nboff):
                        steps.append((h, sb_, B, False, B == 0, False))
                    steps.append((h, sb_, nboff, True, nboff == 0, True))
            heads, fr = {}, {}

            def load_head(h):
                kt_, ktb, ks = ktr.next()
                v_, vb, vs = vr.next()
                q_, qb, qs = qr.next()
                S.dma("sp", kt_[:], self.KT_d[h, :, :], ks, writes=[ktb])
                vsrc = self.V_d.rearrange("(kt p) n -> p kt n", p=128)
                step = max(1, NKT // 4)
                for a in range(0, NKT, step):
                    S.dma("sp", v_[:, a:a + step, 0:128], vsrc[:, a:a + step, h * 128:(h + 1) * 128], vs,
                          writes=[vb])
                S.dma("sp", q_[:], self.QT_d[h, :, :], qs, writes=[qb])
                heads[h] = (kt_, ktb, v_, vb, q_, qb)

            def prologue(h):
                T_ = tabs[h % 2]
                tb = T_["b"]
                rE, rM, bO, bD, fO, fD = T_["rE"], T_["rM"], T_["bO"], T_["bD"], T_["fO"], T_["fD"]
                ch = self.c_tm[:, :, h]
                ps, psb, _ = stb.next()
                S.op("pe", lambda e: e.matmul(ps[:, 0:NKT], lhsT=self.sel127[:], rhs=ch, start=True, stop=True),
                     reads=[self.c_tmb], writes=[psb])
                S.op("pe", lambda e: e.matmul(ps[:, NKT:2 * NKT], lhsT=self.sel63[:], rhs=ch, start=True, stop=True),
                     reads=[self.c_tmb], writes=[psb])
                S.op("act", lambda e: e.activation(out=rE[:], in_=ps[:, 0:NKT], func=AF.Copy),
                     reads=[psb], writes=[tb])
                S.op("act", lambda e: e.activation(out=rM[:], in_=ps[:, NKT:2 * NKT], func=AF.Copy),
                     reads=[psb], writes=[tb])
                rE4 = rE[:, :].rearrange("p (b f) -> p b f", f=4)
                S.op("dve", lambda e: e.tensor_tensor(
                    out=bO[:, :].rearrange("p (b f) -> p b f", f=4),
                    in0=rE4[:, :, 3:4].to_broadcast([128, NB, 4]),
                    in1=ch.rearrange("p (b f) -> p b f", f=4), op=ALU.subtract),
                    reads=[tb, self.c_tmb], writes=[tb])
                S.op("dve", lambda e: e.tensor_tensor(out=bO[:], in0=bO[:], in1=self.kbias[:], op=ALU.add),
                     reads=[tb], writes=[tb])
                S.op("dve", lambda e: e.tensor_tensor(out=bD[:], in0=rM[:], in1=ch, op=ALU.subtract),
                     reads=[tb, self.c_tmb], writes=[tb])
                S.op("dve", lambda e: e.tensor_tensor(out=bD[:], in0=bD[:], in1=self.kbias[:], op=ALU.add),
                     reads=[tb], writes=[tb])
                for qt in range(NQT):
                    cq = self.c_tm[:, K0 + qt, h:h + 1]
                    S.op("act", lambda e, qt=qt, cq=cq: e.activation(
                        out=fO[:, qt, :], in_=rE4[:, :, 3], func=AF.Exp, scale=-1.0, bias=cq),
                        reads=[tb, self.c_tmb], writes=[tb])
                    kd = K0 + 4 * (qt // 4)
                    S.op("act", lambda e, qt=qt, cq=cq, kd=kd: e.activation(
                        out=fD[:, qt, :], in_=rM[:, kd:kd + 4], func=AF.Exp, scale=-1.0, bias=cq),
                        reads=[tb, self.c_tmb], writes=[tb])

            def front(i):
                h, sb_, B, diag, first, last = steps[i]
                if sb_ == 0 and B == 0:
                    if h == 0:
                        load_head(0)
                    prologue(h)
                kt_, ktb, v_, vb, q_, qb = heads[h]
                T_ = tabs[h % 2]
                btab = T_["bD"] if diag else T_["bO"]
                qcols = slice(sb_ * 512, (sb_ + 1) * 512)
                pts = []
                for j in range(4):
                    kt = 4 * B + j
                    ps, psb, _ = stb.next()
                    S.op("pe", lambda e, kt=kt: e.matmul(ps[:, 0:512], lhsT=kt_[:, kt * 128:(kt + 1) * 128],
                                                         rhs=q_[:, qcols], start=True, stop=True),
                         reads=[ktb, qb], writes=[psb])
                    pt, ptb, _ = ptr_.next()
                    S.op("act", lambda e, kt=kt: e.activation(out=pt[:], in_=ps[:, 0:512], func=AF.Exp,
                                                              scale=scale, bias=btab[:, kt:kt + 1]),
                         reads=[psb, T_["b"]], writes=[ptb])
                    if diag:
                        S.op("pool", lambda e, j=j: e.tensor_tensor(
                            out=pt[:, j * 128:(j + 1) * 128], in0=pt[:, j * 128:(j + 1) * 128],
                            in1=self.trib[:], op=ALU.mult), reads=[ptb], writes=[ptb])
                    pts.append((pt, ptb))
                fr[i] = pts

            def back(i):
                h, sb_, B, diag, first, last = steps[i]
                if sb_ == 0 and B == 0 and h + 1 < NH:
                    load_head(h + 1)
                kt_, ktb, v_, vb, q_, qb = heads[h]
                T_ = tabs[h % 2]
                tb = T_["b"]
                acc, accb = accs[(h * NSB + sb_) % 2]
                if first:
                    S.op("pool", lambda e: e.memset(acc[:], 0.0), writes=accb)
                pts = fr.pop(i)
                if not diag:
                    for qt in range(4):
                        po, pob = next_po()
                        for j in range(4):
                            pt, ptb = pts[j]
                            S.op("pe", lambda e, pt=pt, j=j, qt=qt: e.matmul(
                                po, lhsT=pt[:, qt * 128:(qt + 1) * 128], rhs=v_[:, 4 * B + j, :],
                                start=(j == 0), stop=(j == 3)), reads=[ptb, vb], writes=[pob])
                        qg = sb_ * 4 + qt
                        S.op("dve", lambda e, qt=qt, qg=qg: e.scalar_tensor_tensor(
                            out=acc[:, qt, :], in0=po, scalar=T_["fO"][:, qg, B:B + 1],
                            in1=acc[:, qt, :], op0=ALU.mult, op1=ALU.add),
                            reads=[pob, tb, accb[qt]], writes=[accb[qt]])
                else:
                    for j in range(4):
                        pt, ptb = pts[j]
                        kt = 4 * B + j
                        for qt in range(j, 4):
                            po, pob = next_po()
                            S.op("pe", lambda e, qt=qt, pt=pt, kt=kt: e.matmul(
                                po, lhsT=pt[:, qt * 128:(qt + 1) * 128], rhs=v_[:, kt, :],
                                start=True, stop=True), reads=[ptb, vb], writes=[pob])
                            qg = sb_ * 4 + qt
                            S.op("dve", lambda e, qt=qt, qg=qg, j=j: e.scalar_tensor_tensor(
                                out=acc[:, qt, :], in0=po, scalar=T_["fD"][:, qg, j:j + 1],
                                in1=acc[:, qt, :], op0=ALU.mult, op1=ALU.add),
                                reads=[pob, tb, accb[qt]], writes=[accb[qt]])
                if last:
                    pb, pbb = self.pbt.t[0], self.pbt.b[0]
                    for qt in range(4):
                        rc, rcb, _ = recr.next()
                        ob, obb, _ = obr.next()
                        S.op("dve", lambda e, qt=qt: e.reciprocal(out=rc[:], in_=acc[:, qt, 128:129]),
                             reads=[accb[qt]], writes=[rcb])
                        S.op("dve", lambda e, qt=qt: e.tensor_scalar(
                            out=ob[:], in0=acc[:, qt, 0:128], scalar1=rc[:, 0:1], scalar2=None, op0=ALU.mult),
                            reads=[accb[qt], rcb], writes=[obb])
                        S.op("pe", lambda e, qt=qt: e.transpose(out=pb[:, qt * 128:(qt + 1) * 128], in_=ob[:],
                                                                identity=self.identb[:]),
                             reads=[obb], writes=[pbb])
                    ao, aob, asem = aos.next()
                    S.op("act", lambda e: e.activation(out=ao[:], in_=pb[:, 0:512], func=AF.Copy),
                         reads=[pbb], writes=[aob])
                    S.dma("sp", self.aoT_d[h * 128:(h + 1) * 128, sb_ * 512:(sb_ + 1) * 512], ao[:], asem,
                          reads=[aob])

            front(0)
            for i in range(len(steps)):
                if i + 1 < len(steps):
                    front(i + 1)
                back(i)
            S.barrier()

    def phase_blocks(self):
        c, S = self.c, self.S
        D, KT, CT, XT, XW = c.D, c.KT, c.CT, c.XT, c.XW
        MT = c.MEM // 128
        with ExitStack() as st:
            self.slabs = Ring(self, st, "slabB", 3, [128, 32, 256], BF16)
            xcur = self.sb(st, "xcur", [128, 4, D], F32)
            xb = [Buf(f"xcur{i}") for i in range(4)]
            xsems = [self.getsem(f"xcur{i}") for i in range(4)]
            actA = self.sb(st, "actA", [128, KT, 512], BF16)
            actAb = Buf("actA")
            actB = self.sb(st, "actB", [128, 16, 512], BF16)
            actBb = Buf("actB")
            HBN = max(D, CT * 512, 4 * XW)
            hb = self.sb(st, "hbB", [128, HBN], BF16)
            hbb = Buf("hbB")
            hsem = self.getsem("hbB")
            asem = self.getsem("actBld")
            gts = Ring(self, st, "gts", 3, [128, 2, 512], BF16)
            tmp = Ring(self, st, "btmp", 4, [128, 512], F32, dma=False)
            ptx = Ring(self, st, "ptx", 4, [128, 512], BF16, dma=False)
            recr = Ring(self, st, "recx", 4, [128, 1], F32, dma=False)
            kxT = self.sb(st, "kxT", [128, XT, c.MEM], BF16)
            kxb = Buf("kxT")
            vx = self.sb(st, "vx", [128, MT, c.XH, 257], BF16)
            vxb = Buf("vx")
            gsem = self.getsem("gfin")
            actAf = actA[:].rearrange("p a b -> p (a b)").bitcast(F32)

            items = []
            xst_done = []

            def mem_norm(tile, buf):
                def get_src(tt):
                    S.dma("sp", xcur[:, tt, :], self.meml[tt * 128:(tt + 1) * 128, :], xsems[tt], writes=[xb[tt]])
                    return xcur[:, tt, :], xb[tt]
                self.norm_T(get_src, MT, 2 * KT, hb, hbb, actA, actAb)
                S.op("pool", lambda e: e.memset(vx[:, :, :, 256:257], 1.0), writes=[vxb])
            items.append((None, mem_norm))

            def epi_kx(ci, ps, psb):
                S.op("act", lambda e: e.activation(out=kxT[:, ci, :], in_=ps[:, 0:c.MEM], func=AF.Copy),
                     reads=[psb], writes=[kxb])

            def epi_vx(tt, s0, n_, ps, psb):
                hx = s0 // 256
                S.op("act", lambda e: e.activation(out=vx[:, tt, hx, 0:256], in_=ps[:, 0:n_], func=AF.Copy),
                     reads=[psb], writes=[vxb])
            self.items_fm(items, self.wk_x, 0, XW, KT, actA, actAb, c.MEM, epi_kx)
            self.items_tm(items, self.wv_x, 0, KT, 0, XW, actA, actAb, MT, epi_vx)

            for ob in range(c.OWN // 512):
                to0 = ob * 512
                tl0 = c.OWN0 + to0
                def c_load(tile, buf, to0=to0, tl0=tl0):
                    S.dma("sp", actB[:, 0:c.AW // 128, :],
                          self.aoT_d.rearrange("(kt p) t -> p kt t", p=128)[:, :, to0:to0 + 512], asem,
                          writes=[actBb])
                    S.dma("sp", hb[:, 0:CT * 512].rearrange("p (k t) -> p k t", k=CT),
                          self.y2T_d.rearrange("(kt p) t -> p kt t", p=128)[:, :, to0:to0 + 512], hsem,
                          writes=[hbb])
                    for tt in range(4):
                        S.dma("sp", xcur[:, tt, :], self.xl[tl0 + tt * 128:tl0 + (tt + 1) * 128, :], xsems[tt],
                              writes=[xb[tt]])
                items.append((None, c_load))
                y2v = hb[:, 0:CT * 512].rearrange("p (k t) -> p k t", k=CT)
                NA = c.AW // 128
                for s0 in range(0, D, 256):
                    st_ = {}

                    def fnA(tile, buf, st_=st_, s0=s0, to0=to0):
                        for ci in range(2):
                            j = s0 // 128 + ci
                            g, gb, gs = gts.next()
                            S.dma("sp", g[:, 0, :], self.gT_d[j * 128:(j + 1) * 128, to0:to0 + 512], gs, writes=[gb])
                            S.dma("sp", g[:, 1, :], self.gT_d[D + j * 128:D + (j + 1) * 128, to0:to0 + 512], gs,
                                  writes=[gb])
                            ps, psb, _ = self.pf.next()
                            for kt in range(NA):
                                S.op("pe", lambda e, kt=kt, ci=ci: e.matmul(
                                    ps[:, 0:512], lhsT=tile[:, kt, ci * 128:(ci + 1) * 128], rhs=actB[:, kt, :],
                                    start=(kt == 0), stop=(kt == NA - 1)), reads=[buf, actBb], writes=[psb])
                            t1, t1b, _ = tmp.next()
                            S.op("dve", lambda e: e.tensor_tensor(out=t1[:], in0=ps[:, 0:512], in1=g[:, 0, :],
                                                                  op=ALU.mult), reads=[psb, gb], writes=[t1b])
                            st_[ci] = (t1, t1b, g, gb)

                    def fnS(tile, buf, st_=st_, s0=s0):
                        for ci in range(2):
                            j = s0 // 128 + ci
                            t1, t1b, g, gb = st_[ci]
                            ps, psb, _ = self.pf.next()
                            for kt in range(CT):
                                S.op("pe", lambda e, kt=kt, ci=ci: e.matmul(
                                    ps[:, 0:512], lhsT=tile[:, kt, ci * 128:(ci + 1) * 128], rhs=y2v[:, kt, :],
                                    start=(kt == 0), stop=(kt == CT - 1)), reads=[buf, hbb], writes=[psb])
                            t2, t2b, _ = tmp.next()
                            S.op("dve", lambda e: e.tensor_tensor(out=t2[:], in0=ps[:, 0:512], in1=g[:, 1, :],
                                                                  op=ALU.mult), reads=[psb, gb], writes=[t2b])
                            S.op("pool", lambda e, j=j: e.tensor_tensor(out=actA[:, j, :], in0=t1[:], in1=t2[:],
                                                                        op=ALU.add),
                                 reads=[t1b, t2b], writes=[actAb])

                    items.append(((self.w_attn_up, 0, NA, s0, 256), fnA))
                    items.append(((self.w_ssm_up, 0, CT, s0, 256), fnS))

                def epi_res(tt, s0, n_, ps, psb):
                    S.op("dve", lambda e: e.tensor_tensor(out=xcur[:, tt, s0:s0 + n_], in0=ps[:, 0:n_],
                                                          in1=xcur[:, tt, s0:s0 + n_], op=ALU.add),
                         reads=[psb, xb[tt]], writes=[xb[tt]])
                self.items_tm(items, self.w_out, 0, KT, 0, D, actA, actAb, 4, epi_res)

                def d_norm(tile, buf):
                    self.norm_T(lambda tt: (xcur[:, tt, :], xb[tt]), 4, KT, hb, hbb, actA, actAb)
                items.append((None, d_norm))

                def epi_qx(ci, ps, psb):
                    S.op("act", lambda e: e.activation(out=actB[:, ci, :], in_=ps[:, 0:512], func=AF.Copy),
                         reads=[psb], writes=[actBb])
                self.items_fm(items, self.wq_x, 0, XW, KT, actA, actAb, 512, epi_qx)
                oxv = hb[:, 0:4 * XW].rearrange("p (t w) -> p t w", t=4)

                def d_attn(tile, buf):
                    for hx in range(c.XH):
                        pts = []
                        for mt in range(MT):
                            ps, psb, _ = self.pf.next()
                            for dt_ in range(2):
                                S.op("pe", lambda e, dt_=dt_, mt=mt: e.matmul(
                                    ps[:, 0:512], lhsT=kxT[:, hx * 2 + dt_, mt * 128:(mt + 1) * 128],
                                    rhs=actB[:, hx * 2 + dt_, :], start=(dt_ == 0), stop=(dt_ == 1)),
                                    reads=[kxb, actBb], writes=[psb])
                            pt, ptb, _ = ptx.next()
                            S.op("act", lambda e: e.activation(out=pt[:], in_=ps[:, 0:512], func=AF.Exp,
                                                               scale=1.0 / 16.0), reads=[psb], writes=[ptb])
                            pts.append((pt, ptb))
                        for qt in range(4):
                            po, pob, _ = self.pf.next()
                            for mt in range(MT):
                                pt, ptb = pts[mt]
                                S.op("pe", lambda e, mt=mt, pt=pt, qt=qt: e.matmul(
                                    po[:, 0:257], lhsT=pt[:, qt * 128:(qt + 1) * 128], rhs=vx[:, mt, hx, :],
                                    start=(mt == 0), stop=(mt == MT - 1)), reads=[ptb, vxb], writes=[pob])
                            rc, rcb, _ = recr.next()
                            S.op("dve", lambda e: e.reciprocal(out=rc[:], in_=po[:, 256:257]),
                                 reads=[pob], writes=[rcb])
                            S.op("dve", lambda e, qt=qt: e.tensor_scalar(
                                out=oxv[:, qt, hx * 256:(hx + 1) * 256], in0=po[:, 0:256], scalar1=rc[:, 0:1],
                                scalar2=None, op0=ALU.mult), reads=[pob, rcb], writes=[hbb])
                    for qt in range(4):
                        pb, pbb, _ = self.pbt.next()
                        for j in range(XT):
                            S.op("pe", lambda e, j=j, qt=qt: e.transpose(
                                out=pb[:, j * 128:(j + 1) * 128], in_=oxv[:, qt, j * 128:(j + 1) * 128],
                                identity=self.identb[:]), reads=[hbb], writes=[pbb])
                        S.op("act", lambda e, qt=qt: e.activation(
                            out=actB[:, 8:8 + XT, qt * 128:(qt + 1) * 128],
                            in_=pb[:, 0:XT * 128].rearrange("p (k t) -> p k t", k=XT), func=AF.Copy),
                            reads=[pbb], writes=[actBb])
                items.append((None, d_attn))
                self.items_tm(items, self.wo_x, 0, XT, 0, D, actB[:, 8:8 + XT, :], actBb, 4, epi_res)

                def e_norm(tile, buf):
                    self.norm_T(lambda tt: (xcur[:, tt, :], xb[tt]), 4, 3 * KT, hb, hbb, actA, actAb)
                items.append((None, e_norm))
                FE = c.FE
                for q in range(c.NE):
                    def epi_h(ci, ps, psb):
                        t1, t1b, _ = tmp.next()
                        S.op("act", lambda e: e.activation(out=t1[:], in_=ps[:, 0:512], func=AF.Relu),
                             reads=[psb], writes=[t1b])
                        S.op("pool", lambda e: e.tensor_tensor(out=actB[:, ci, :], in0=t1[:], in1=t1[:],
                                                               op=ALU.mult), reads=[t1b], writes=[actBb])
                    self.items_fm(items, self.w_ff1, q * FE * 128, FE * 128, KT, actA, actAb, 512, epi_h)
                    self.items_tm(items, self.w_ff2, q * FE, FE, 0, D, actB, actBb, 4, epi_res)

                def fin(tile, buf, to0=to0):
                    gv = actAf[:, 0:D]
                    S.dma("sp", gv, self.g_final_d.partition_broadcast(128), gsem, reads=[], writes=[actAb])
                    for tt in range(4):
                        ss, ssb, _ = self.small.next()
                        S.op("act", lambda e, tt=tt: e.activation(out=hb[:, 0:D], in_=xcur[:, tt, :], func=AF.Square,
                                                                   accum_out=ss[:, 0:1]),
                             reads=[xb[tt]], writes=[hbb, ssb])
                        S.op("dve", lambda e: e.tensor_scalar(out=ss[:, 1:2], in0=ss[:, 0:1], scalar1=1.0 / D,
                                                              scalar2=EPS, op0=ALU.mult, op1=ALU.add),
                             reads=[ssb], writes=[ssb])
                        S.op("pool", lambda e: e.tensor_tensor(out=ss[:, 2:3], in0=ss[:, 1:2],
                                                               in1=self.cst[:, 0:1], op=ALU.pow),
                             reads=[ssb], writes=[ssb])
                        S.op("dve", lambda e, tt=tt: e.scalar_tensor_tensor(
                            out=xcur[:, tt, :], in0=xcur[:, tt, :], scalar=ss[:, 2:3], in1=gv,
                            op0=ALU.mult, op1=ALU.mult), reads=[xb[tt], ssb, actAb], writes=[xb[tt]])
                        S.dma("sp", self.out_d[to0 + tt * 128:to0 + (tt + 1) * 128, :], xcur[:, tt, :], xsems[tt],
                              reads=[xb[tt]])
                items.append((None, fin))
            self.run_items(items)
            S.barrier()


def _const_mats():
    ident = np.eye(128, dtype=np.float32)
    k = np.arange(128)
    tri = (k[:, None] <= k[None, :]).astype(np.float32)
    sel127 = np.zeros((128, 128), np.float32)
    sel127[127, :] = 1.0
    sel63 = np.zeros((128, 128), np.float32)
    sel63[63, :] = 1.0
    return np.concatenate([ident, tri, sel127, sel63], axis=1)


def _col(v, kt):
    return np.ascontiguousarray(np.asarray(v, np.float32).reshape(kt, 128).T)


def make_core_inputs(cfg, xl, kvalid0, meml, p):
    c = cfg
    kb = np.zeros(c.L, np.float32)
    kb[:kvalid0] = -30000.0
    f32 = lambda a: np.ascontiguousarray(np.asarray(a, np.float32))
    sm = lambda a: np.ascontiguousarray(np.asarray(a, np.float32).reshape(c.NST, 128).T)
    ldt = np.repeat(np.asarray(p["log_dt"], np.float32)[:, None], 64, axis=1)
    s5p = np.concatenate([sm(p["A_re"]), sm(p["A_im"]), sm(ldt)], axis=1)

    def bl(a):
        a = np.asarray(a, np.float32).reshape(c.NST, 2, 64, 16)
        return a.transpose(1, 2, 0, 3).reshape(128, c.NST, 16)

    def cl(a):
        a = np.asarray(a, np.float32).transpose(0, 2, 1).reshape(c.NST, 2, 64, 16)
        return a.transpose(1, 2, 0, 3).reshape(128, c.NST, 16)

    return {
        "xl": f32(xl), "meml": f32(meml), "kbias": _col(kb, c.NKT),
        "w_in": f32(p["w_in"]), "w_glu": f32(p["w_glu"]), "w_attn_up": f32(p["w_attn_up"]),
        "w_ssm_up": f32(p["w_ssm_up"]), "w_out": f32(p["w_out"]), "wq_x": f32(p["wq_x"]),
        "wk_x": f32(p["wk_x"]), "wv_x": f32(p["wv_x"]), "wo_x": f32(p["wo_x"]),
        "w_ff1": f32(p["w_ff1"]), "w_ff2": f32(p["w_ff2"]),
        "gcols": np.concatenate([_col(p["g_mix"], c.KT), _col(p["g_xattn"], c.KT),
                                 _col(p["g_mem"], c.KT), _col(p["g_mlp"], c.KT)], axis=1),
        "g_final": f32(p["g_final"]),
        "b_f": f32(np.asarray(p["b_f"]).reshape(c.NH, 1)),
        "b_gate_t": _col(p["b_gate"], 2 * c.KT), "b_glu_t": _col(p["b_glu"], c.CT),
        "dskip_t": _col(np.asarray(p["D_skip"]).reshape(-1), c.CT),
        "s5p": np.ascontiguousarray(s5p),
        "s5b": np.ascontiguousarray(np.stack([bl(p["B_re"]), bl(p["B_im"])], axis=1)),
        "s5c": np.ascontiguousarray(np.stack([cl(p["C_re"]), cl(p["C_im"])], axis=1)),
        "cmat": _const_mats(),
    }


_NC_CACHE = {}


def kernel(x, mem, g_mix, w_in, b_f, b_gate, A_re, A_im, log_dt, B_re, B_im, C_re, C_im,
           D_skip, w_glu, b_glu, w_attn_up, w_ssm_up, w_out, g_xattn, g_mem, wq_x, wk_x,
           wv_x, wo_x, g_mlp, w_ff1, w_ff2, g_final):
    cfg = FULL
    x = np.asarray(x, np.float32)
    mem = np.asarray(mem, np.float32)
    p = dict(g_mix=g_mix[0], w_in=w_in[0], b_f=b_f[0], b_gate=b_gate[0], A_re=A_re[0], A_im=A_im[0],
             log_dt=log_dt[0], B_re=B_re[0], B_im=B_im[0], C_re=C_re[0], C_im=C_im[0], D_skip=D_skip[0],
             w_glu=w_glu[0], b_glu=b_glu[0], w_attn_up=w_attn_up[0], w_ssm_up=w_ssm_up[0], w_out=w_out[0],
             g_xattn=g_xattn[0], g_mem=g_mem[0], wq_x=wq_x[0], wk_x=wk_x[0], wv_x=wv_x[0], wo_x=wo_x[0],
             g_mlp=g_mlp[0], w_ff1=w_ff1[0], w_ff2=w_ff2[0], g_final=g_final)
    p = {k: np.asarray(v, np.float32) for k, v in p.items()}
    B, SEQ, D = x.shape
    nq = SEQ // cfg.OWN
    in_maps = []
    shared = None
    for core in range(8):
        b, j = core // nq, core % nq
        xl = np.zeros((cfg.L, D), np.float32)
        n = (j + 1) * cfg.OWN
        xl[cfg.L - n:] = x[b, :n]
        m = make_core_inputs(cfg, xl, cfg.L - n, mem[b], p) if shared is None else None
        if shared is None:
            shared = m
        else:
            m = dict(shared)
            kb = np.zeros(cfg.L, np.float32)
            kb[:cfg.L - n] = -30000.0
            m["xl"] = xl
            m["meml"] = np.ascontiguousarray(mem[b])
            m["kbias"] = _col(kb, cfg.NKT)
        in_maps.append(m)
    if "full" not in _NC_CACHE:
        _NC_CACHE["full"] = Builder(cfg).build()
    nc = _NC_CACHE["full"]
    res = run_bass_kernel_spmd(nc, in_maps, core_ids=list(range(8)))
    out = np.zeros((B, SEQ, D), np.float32)
    for core in range(8):
        b, j = core // nq, core % nq
        out[b, j * cfg.OWN:(j + 1) * cfg.OWN] = res.results[core]["out"]
    return out
```
